# Optimizing a Trainium2 kernel written in Bass

```python
import math
import jax, jax.numpy as jnp
from jax import lax
import numpy as np

D_MODEL = 1024
BATCH = 8
SEQ = 2048
DEPTH = 2
DEC_BATCH = 128
DEC_SEQ = 8
PAST_LEN = 16384
PAGE_SIZE = 128

N_AB = (DEPTH + 1) // 2
N_CL = DEPTH // 2
D_A = D_MODEL // 2
S5_GROUP = 16
G_A = D_A // S5_GROUP
P_STATE = 64
D_B = D_MODEL // 2
H_B = 4
DK_B = 128
DV_B = D_B // H_B
CONV_W = 4
CHUNK = 64
AB_SPLITS = (D_A, D_A + H_B * DK_B, D_A + 2 * H_B * DK_B, D_A + 2 * H_B * DK_B + D_B,
             D_A + 2 * H_B * DK_B + D_B + H_B, D_A + 2 * H_B * DK_B + D_B + 2 * H_B)
D_IN_AB = AB_SPLITS[-1] + D_B
N_HEAD_C = 64
H_C = D_MODEL // N_HEAD_C
LORA_DECAY = 64
LORA_AAA = 64
LORA_GATE = 160
RWKV_GN_EPS = 64e-5
D_FF = ((-(-8 * D_MODEL // 3) + 255) // 256) * 256
NORM_EPS = 1e-6

kernel_name = "hybrid_s5_gdn_rwkv7_decode_step"


def rmsnorm(x, w):
    xf = x.astype(jnp.float32)
    y = xf * lax.rsqrt(jnp.mean(xf * xf, axis=-1, keepdims=True) + NORM_EPS)
    return (y * w.astype(jnp.float32)).astype(x.dtype)


def l2norm(x):
    xf = x.astype(jnp.float32)
    return xf * lax.rsqrt(jnp.sum(xf * xf, axis=-1, keepdims=True) + NORM_EPS)


def swiglu(h, w_gate, w_up, w_down):
    return (jax.nn.silu(h @ w_gate) * (h @ w_up)) @ w_down


def s5_mixer(u, h0_re, h0_im, lam_re, lam_im, log_step, b_re, b_im, c_re, c_im, d_skip, w_glu):
    bsz, seq = u.shape[0], u.shape[1]
    f32 = jnp.float32
    uf = u.astype(f32).reshape(bsz, seq, G_A, S5_GROUP)
    lr, li = lam_re.astype(f32), lam_im.astype(f32)
    dt = jnp.exp(log_step.astype(f32))[:, None]
    mag = jnp.exp(lr * dt)
    ab_re, ab_im = mag * jnp.cos(li * dt), mag * jnp.sin(li * dt)
    den = lr * lr + li * li
    nr, ni = ab_re - 1.0, ab_im
    cr, ci = (nr * lr + ni * li) / den, (ni * lr - nr * li) / den
    b_re, b_im = b_re.astype(f32), b_im.astype(f32)
    bb_re = cr[..., None] * b_re - ci[..., None] * b_im
    bb_im = cr[..., None] * b_im + ci[..., None] * b_re
    bu_re = jnp.einsum("gpc,blgc->blgp", bb_re, uf)
    bu_im = jnp.einsum("gpc,blgc->blgp", bb_im, uf)
    h0r, h0i = h0_re.astype(f32), h0_im.astype(f32)
    bu_re = bu_re.at[:, 0].add(ab_re * h0r - ab_im * h0i)
    bu_im = bu_im.at[:, 0].add(ab_re * h0i + ab_im * h0r)
    a_re = jnp.broadcast_to(ab_re, bu_re.shape)
    a_im = jnp.broadcast_to(ab_im, bu_im.shape)

    def combine(e1, e2):
        a1r, a1i, b1r, b1i = e1
        a2r, a2i, b2r, b2i = e2
        return (a1r * a2r - a1i * a2i, a1r * a2i + a1i * a2r,
                a2r * b1r - a2i * b1i + b2r, a2r * b1i + a2i * b1r + b2i)

    _, _, xr, xi = lax.associative_scan(combine, (a_re, a_im, bu_re, bu_im), axis=1)
    y = (jnp.einsum("gcp,blgp->blgc", c_re.astype(f32), xr)
         - jnp.einsum("gcp,blgp->blgc", c_im.astype(f32), xi)
         + d_skip.astype(f32).reshape(G_A, S5_GROUP) * uf).reshape(bsz, seq, D_A)
    yg = jax.nn.gelu(y)
    out = yg * jax.nn.sigmoid(yg @ w_glu.astype(f32))
    return out.astype(u.dtype), xr[:, -1], xi[:, -1]


def causal_conv_silu(x, buf, w):
    seq = x.shape[1]
    xp = jnp.concatenate([buf.astype(x.dtype), x], axis=1)
    y = sum(xp[:, j:j + seq] * w[j] for j in range(CONV_W))
    return jax.nn.silu(y), xp[:, -(CONV_W - 1):]


def gated_delta_chunked(q, k, v, g, beta, s0):
    bsz, seq, nh, dk = q.shape
    dv = v.shape[-1]
    n_chunks = -(-seq // CHUNK)
    pad = n_chunks * CHUNK - seq

    def blocks(t):
        t = t.astype(jnp.float32)
        t = jnp.pad(t, [(0, 0), (0, pad)] + [(0, 0)] * (t.ndim - 2))
        t = t.reshape((bsz, n_chunks, CHUNK) + t.shape[2:])
        return jnp.swapaxes(jnp.moveaxis(t, 3, 2), 0, 1)

    qb = blocks(q) * (dk ** -0.5)
    kb, vb, gb, bb = blocks(k), blocks(v), blocks(g), blocks(beta)
    gc = jnp.cumsum(gb, axis=-1)
    idx = jnp.arange(CHUNK)
    causal = idx[:, None] >= idx[None, :]
    strict = idx[:, None] > idx[None, :]
    decay = jnp.exp(jnp.where(causal, gc[..., :, None] - gc[..., None, :], -jnp.inf))
    k_beta = kb * bb[..., None]
    v_beta = vb * bb[..., None]
    lmat = jnp.where(strict, jnp.einsum("nbhid,nbhjd->nbhij", k_beta, kb) * decay, 0.0)
    eye = jnp.eye(CHUNK, dtype=jnp.float32)
    tmat = lax.linalg.triangular_solve(eye + lmat, jnp.broadcast_to(eye, lmat.shape),
                                       left_side=True, lower=True)
    u_c = tmat @ v_beta
    w_c = tmat @ (k_beta * jnp.exp(gc)[..., None])

    def step(state, inp):
        qc, kc, uc, wc, gcc, dc = inp
        attn = jnp.einsum("bhid,bhjd->bhij", qc, kc) * dc
        v_new = uc - jnp.einsum("bhcd,bhdv->bhcv", wc, state)
        o = (jnp.einsum("bhcd,bhdv->bhcv", qc * jnp.exp(gcc)[..., None], state)
             + jnp.einsum("bhij,bhjv->bhiv", attn, v_new))
        g_last = gcc[..., -1]
        state = (state * jnp.exp(g_last)[..., None, None]
                 + jnp.einsum("bhcd,bhcv->bhdv", kc * jnp.exp(g_last[..., None] - gcc)[..., None], v_new))
        return state, o

    s_fin, o = lax.scan(step, s0.astype(jnp.float32), (qb, kb, u_c, w_c, gc, decay))
    o = jnp.transpose(o, (1, 0, 3, 2, 4)).reshape(bsz, n_chunks * CHUNK, nh, dv)[:, :seq]
    return o, s_fin


def gdn_mixer(q, k, v, b_raw, a_raw, z, s0, conv0, conv_w, a_log, dt_bias, norm_w):
    bsz, seq = q.shape[0], q.shape[1]
    qkv, conv_new = causal_conv_silu(jnp.concatenate([q, k, v], axis=-1), conv0, conv_w)
    q, k, v = jnp.split(qkv, (H_B * DK_B, 2 * H_B * DK_B), axis=-1)
    q = l2norm(q.reshape(bsz, seq, H_B, DK_B))
    k = l2norm(k.reshape(bsz, seq, H_B, DK_B))
    v = v.reshape(bsz, seq, H_B, DV_B)
    beta = jax.nn.sigmoid(b_raw.astype(jnp.float32))
    g = -jnp.exp(a_log.astype(jnp.float32)) * jax.nn.softplus(
        a_raw.astype(jnp.float32) + dt_bias.astype(jnp.float32))
    o, s_fin = gated_delta_chunked(q, k, v, g, beta, s0)
    o = rmsnorm(o, norm_w) * jax.nn.silu(z.astype(jnp.float32).reshape(bsz, seq, H_B, DV_B))
    return o.reshape(bsz, seq, D_B).astype(q.dtype if q.dtype != jnp.float32 else z.dtype), s_fin, conv_new


def rwkv7_scan(r, decay, k, v, a, b, s0):
    def step(state, inp):
        r_t, w_t, k_t, v_t, a_t, b_t = inp
        sa = jnp.einsum("bhvk,bhk->bhv", state, a_t)
        state = (state * w_t[:, :, None, :] + sa[..., None] * b_t[:, :, None, :]
                 + v_t[..., None] * k_t[:, :, None, :])
        return state, jnp.einsum("bhvk,bhk->bhv", state, r_t)

    xs = tuple(jnp.swapaxes(t.astype(jnp.float32), 0, 1) for t in (r, decay, k, v, a, b))
    s_fin, ys = lax.scan(step, s0.astype(jnp.float32), xs)
    return jnp.swapaxes(ys, 0, 1), s_fin


def rwkv7_mixer(x, shift0, s0, maa, w_r, w_k, w_v, w_o, w0, w1, w2, a0, a1, a2, g1, g2,
                k_k, k_a, r_k, ln_w, ln_b):
    bsz, seq, _ = x.shape
    x_prev = jnp.concatenate([shift0[:, None].astype(x.dtype), x[:, :-1]], axis=1)
    xx = x_prev - x
    xr, xw, xk, xv, xa, xg = (x + xx * maa[j] for j in range(6))
    r = xr @ w_r
    k = xk @ w_k
    v = xv @ w_v
    w = -jax.nn.softplus(-(w0 + jnp.tanh(xw @ w1) @ w2).astype(jnp.float32)) - 0.5
    a = jax.nn.sigmoid((a0 + (xa @ a1) @ a2).astype(jnp.float32))
    g = jax.nn.sigmoid(xg @ g1) @ g2
    heads = lambda t: t.reshape(bsz, seq, H_C, N_HEAD_C)
    kk = l2norm(heads(k * k_k))
    k = k.astype(jnp.float32) * (1.0 + (a - 1.0) * k_a.astype(jnp.float32))
    decay = jnp.exp(-jnp.exp(w))
    rh, kh, vh, ah = heads(r).astype(jnp.float32), heads(k), heads(v).astype(jnp.float32), heads(a)
    y, s_fin = rwkv7_scan(rh, heads(decay), kh, vh, -kk, kk * ah, s0)
    mu = jnp.mean(y, axis=-1, keepdims=True)
    var = jnp.mean(jnp.square(y - mu), axis=-1, keepdims=True)
    y = (y - mu) * lax.rsqrt(var + RWKV_GN_EPS)
    y = y * ln_w.astype(jnp.float32).reshape(H_C, N_HEAD_C) + ln_b.astype(jnp.float32).reshape(H_C, N_HEAD_C)
    y = y + jnp.sum(rh * kh * r_k.astype(jnp.float32), axis=-1, keepdims=True) * vh
    out = (y.reshape(bsz, seq, D_MODEL) * g.astype(jnp.float32)).astype(x.dtype) @ w_o
    return out, s_fin, x[:, -1]


def trunk(x, s5_re0, s5_im0, gdn_s0, gdn_conv0, rw_s0, rw_shift0, prm):
    s5_re_n, s5_im_n, gdn_s_n, gdn_conv_n, rw_s_n, rw_shift_n = [], [], [], [], [], []
    for layer in range(DEPTH):
        i = layer // 2
        h = rmsnorm(x, prm["norm_mix"][layer])
        if layer % 2 == 0:
            proj = h @ prm["ab_w_in"][i]
            u_a, q, k, v, b_raw, a_raw, z = jnp.split(proj, AB_SPLITS, axis=-1)
            y_a, hr, hi = s5_mixer(u_a, s5_re0[i], s5_im0[i], prm["s5_lambda_re"][i], prm["s5_lambda_im"][i],
                                   prm["s5_log_step"][i], prm["s5_B_re"][i], prm["s5_B_im"][i],
                                   prm["s5_C_re"][i], prm["s5_C_im"][i], prm["s5_D"][i], prm["s5_w_glu"][i])
            y_b, s_b, conv_b = gdn_mixer(q, k, v, b_raw, a_raw, z, gdn_s0[i], gdn_conv0[i], prm["gdn_conv_w"][i],
                                         prm["gdn_A_log"][i], prm["gdn_dt_bias"][i], prm["gdn_norm_w"][i])
            mix = jnp.concatenate([y_a, y_b.astype(y_a.dtype)], axis=-1) @ prm["ab_w_out"][i]
            s5_re_n.append(hr)
            s5_im_n.append(hi)
            gdn_s_n.append(s_b)
            gdn_conv_n.append(conv_b)
        else:
            mix, s_c, shift_c = rwkv7_mixer(
                h, rw_shift0[i], rw_s0[i], prm["rw_maa"][i], prm["rw_w_r"][i], prm["rw_w_k"][i], prm["rw_w_v"][i],
                prm["rw_w_o"][i], prm["rw_w0"][i], prm["rw_w1"][i], prm["rw_w2"][i], prm["rw_a0"][i], prm["rw_a1"][i],
                prm["rw_a2"][i], prm["rw_g1"][i], prm["rw_g2"][i], prm["rw_k_k"][i], prm["rw_k_a"][i],
                prm["rw_r_k"][i], prm["rw_ln_w"][i], prm["rw_ln_b"][i])
            rw_s_n.append(s_c)
            rw_shift_n.append(shift_c)
        x = x + mix.astype(x.dtype)
        h = rmsnorm(x, prm["norm_ffn"][layer])
        x = x + swiglu(h, prm["ffn_w_gate"][layer], prm["ffn_w_up"][layer], prm["ffn_w_down"][layer]).astype(x.dtype)
    y = rmsnorm(x, prm["norm_final"])
    return (y, jnp.stack(s5_re_n), jnp.stack(s5_im_n), jnp.stack(gdn_s_n), jnp.stack(gdn_conv_n),
            jnp.stack(rw_s_n), jnp.stack(rw_shift_n))


def setup_inputs(seed: int = 0) -> dict:
    key = jax.random.key(seed)
    ks = jax.random.split(key, 64)
    counter = [0]

    def nxt():
        counter[0] += 1
        return ks[counter[0] - 1]

    def nrm(shape, scale=1.0):
        return scale * jax.random.normal(nxt(), shape, jnp.float32)

    def unif(shape, lo, hi):
        return jax.random.uniform(nxt(), shape, jnp.float32, lo, hi)

    D = D_MODEL
    inp = {}
    inp["x_prompt"] = nrm((BATCH, SEQ, D))
    inp["x_sample"] = nrm((DEC_BATCH, DEC_SEQ, D))
    inp["state_s5_re"] = nrm((N_AB, DEC_BATCH, G_A, P_STATE), 0.3)
    inp["state_s5_im"] = nrm((N_AB, DEC_BATCH, G_A, P_STATE), 0.3)
    inp["state_gdn"] = nrm((N_AB, DEC_BATCH, H_B, DK_B, DV_B), 0.1)
    inp["state_gdn_conv"] = nrm((N_AB, DEC_BATCH, CONV_W - 1, 2 * H_B * DK_B + D_B))
    inp["state_rwkv"] = nrm((N_CL, DEC_BATCH, H_C, N_HEAD_C, N_HEAD_C), 0.3)
    inp["state_rwkv_shift"] = nrm((N_CL, DEC_BATCH, D))
    inp["norm_mix"] = 1.0 + nrm((DEPTH, D), 0.02)
    inp["norm_ffn"] = 1.0 + nrm((DEPTH, D), 0.02)
    inp["norm_final"] = 1.0 + nrm((D,), 0.02)
    inp["ffn_w_gate"] = nrm((DEPTH, D, D_FF), D ** -0.5)
    inp["ffn_w_up"] = nrm((DEPTH, D, D_FF), D ** -0.5)
    inp["ffn_w_down"] = nrm((DEPTH, D_FF, D), D_FF ** -0.5)
    inp["ab_w_in"] = nrm((N_AB, D, D_IN_AB), D ** -0.5)
    inp["ab_w_out"] = nrm((N_AB, D_A + D_B, D), (D_A + D_B) ** -0.5)
    inp["s5_lambda_re"] = -0.5 * jnp.exp(nrm((N_AB, G_A, P_STATE), 0.05))
    inp["s5_lambda_im"] = jnp.pi * jnp.arange(P_STATE, dtype=jnp.float32) + nrm((N_AB, G_A, P_STATE), 0.01)
    inp["s5_log_step"] = unif((N_AB, G_A), math.log(1e-3), math.log(1e-1))
    inp["s5_B_re"] = nrm((N_AB, G_A, P_STATE, S5_GROUP), (2 * S5_GROUP) ** -0.5)
    inp["s5_B_im"] = nrm((N_AB, G_A, P_STATE, S5_GROUP), (2 * S5_GROUP) ** -0.5)
    inp["s5_C_re"] = nrm((N_AB, G_A, S5_GROUP, P_STATE), (2 * P_STATE) ** -0.5)
    inp["s5_C_im"] = nrm((N_AB, G_A, S5_GROUP, P_STATE), (2 * P_STATE) ** -0.5)
    inp["s5_D"] = nrm((N_AB, D_A))
    inp["s5_w_glu"] = nrm((N_AB, D_A, D_A), D_A ** -0.5)
    inp["gdn_conv_w"] = nrm((N_AB, CONV_W, 2 * H_B * DK_B + D_B), 0.5)
    inp["gdn_A_log"] = jnp.log(unif((N_AB, H_B), 1.0, 16.0))
    dt = jnp.exp(unif((N_AB, H_B), math.log(1e-3), math.log(1e-1)))
    inp["gdn_dt_bias"] = dt + jnp.log(-jnp.expm1(-dt))
    inp["gdn_norm_w"] = 1.0 + nrm((N_AB, DV_B), 0.05)
    inp["rw_maa"] = unif((N_CL, 6, D), 0.0, 1.0)
    inp["rw_w_r"] = nrm((N_CL, D, D), D ** -0.5)
    inp["rw_w_k"] = nrm((N_CL, D, D), D ** -0.5)
    inp["rw_w_v"] = nrm((N_CL, D, D), D ** -0.5)
    inp["rw_w_o"] = nrm((N_CL, D, D), D ** -0.5)
    ramp = jnp.arange(D, dtype=jnp.float32) / (D - 1)
    inp["rw_w0"] = -6.0 + 5.0 * ramp ** 0.85 + nrm((N_CL, D), 0.1)
    inp["rw_w1"] = nrm((N_CL, D, LORA_DECAY), D ** -0.5)
    inp["rw_w2"] = nrm((N_CL, LORA_DECAY, D), 0.1 * LORA_DECAY ** -0.5)
    inp["rw_a0"] = nrm((N_CL, D), 0.1)
    inp["rw_a1"] = nrm((N_CL, D, LORA_AAA), D ** -0.5)
    inp["rw_a2"] = nrm((N_CL, LORA_AAA, D), 0.1 * LORA_AAA ** -0.5)
    inp["rw_g1"] = nrm((N_CL, D, LORA_GATE), D ** -0.5)
    inp["rw_g2"] = nrm((N_CL, LORA_GATE, D), LORA_GATE ** -0.5)
    inp["rw_k_k"] = 0.85 + nrm((N_CL, D), 0.02)
    inp["rw_k_a"] = 1.0 + nrm((N_CL, D), 0.02)
    inp["rw_r_k"] = nrm((N_CL, H_C, N_HEAD_C), 0.1)
    inp["rw_ln_w"] = 1.0 + nrm((N_CL, D), 0.05)
    inp["rw_ln_b"] = nrm((N_CL, D), 0.02)
    return inp


def reference(x_prompt, x_sample, state_s5_re, state_s5_im, state_gdn, state_gdn_conv, state_rwkv,
              state_rwkv_shift, norm_mix, norm_ffn, norm_final, ffn_w_gate, ffn_w_up, ffn_w_down,
              ab_w_in, ab_w_out, s5_lambda_re, s5_lambda_im, s5_log_step, s5_B_re, s5_B_im, s5_C_re, s5_C_im,
              s5_D, s5_w_glu, gdn_conv_w, gdn_A_log, gdn_dt_bias, gdn_norm_w, rw_maa, rw_w_r, rw_w_k, rw_w_v,
              rw_w_o, rw_w0, rw_w1, rw_w2, rw_a0, rw_a1, rw_a2, rw_g1, rw_g2, rw_k_k, rw_k_a, rw_r_k,
              rw_ln_w, rw_ln_b):
    prm = dict(norm_mix=norm_mix, norm_ffn=norm_ffn, norm_final=norm_final, ffn_w_gate=ffn_w_gate,
               ffn_w_up=ffn_w_up, ffn_w_down=ffn_w_down, ab_w_in=ab_w_in, ab_w_out=ab_w_out,
               s5_lambda_re=s5_lambda_re, s5_lambda_im=s5_lambda_im, s5_log_step=s5_log_step,
               s5_B_re=s5_B_re, s5_B_im=s5_B_im, s5_C_re=s5_C_re, s5_C_im=s5_C_im, s5_D=s5_D, s5_w_glu=s5_w_glu,
               gdn_conv_w=gdn_conv_w, gdn_A_log=gdn_A_log, gdn_dt_bias=gdn_dt_bias, gdn_norm_w=gdn_norm_w,
               rw_maa=rw_maa, rw_w_r=rw_w_r, rw_w_k=rw_w_k, rw_w_v=rw_w_v, rw_w_o=rw_w_o, rw_w0=rw_w0,
               rw_w1=rw_w1, rw_w2=rw_w2, rw_a0=rw_a0, rw_a1=rw_a1, rw_a2=rw_a2, rw_g1=rw_g1, rw_g2=rw_g2,
               rw_k_k=rw_k_k, rw_k_a=rw_k_a, rw_r_k=rw_r_k, rw_ln_w=rw_ln_w, rw_ln_b=rw_ln_b)
    dt = x_prompt.dtype
    (y_prompt, p_s5_re, p_s5_im, p_gdn, p_gdn_conv, p_rwkv, p_rwkv_shift) = trunk(
        x_prompt,
        jnp.zeros((N_AB, BATCH) + state_s5_re.shape[2:], jnp.float32),
        jnp.zeros((N_AB, BATCH) + state_s5_im.shape[2:], jnp.float32),
        jnp.zeros((N_AB, BATCH) + state_gdn.shape[2:], jnp.float32),
        jnp.zeros((N_AB, BATCH) + state_gdn_conv.shape[2:], dt),
        jnp.zeros((N_CL, BATCH) + state_rwkv.shape[2:], jnp.float32),
        jnp.zeros((N_CL, BATCH) + state_rwkv_shift.shape[2:], dt),
        prm)
    (y_sample, s_s5_re, s_s5_im, s_gdn, s_gdn_conv, s_rwkv, s_rwkv_shift) = trunk(
        x_sample, state_s5_re, state_s5_im, state_gdn, state_gdn_conv, state_rwkv, state_rwkv_shift, prm)
    return (y_prompt, y_sample, p_s5_re, p_s5_im, p_gdn, p_gdn_conv, p_rwkv, p_rwkv_shift,
            s_s5_re, s_s5_im, s_gdn, s_gdn_conv, s_rwkv, s_rwkv_shift)
```

```python
import math
from contextlib import ExitStack

import numpy as np
import concourse.bass as bass
import concourse.mybir as mybir
from concourse.bass_utils import run_bass_kernel_spmd

F32 = mybir.dt.float32
BF16 = mybir.dt.bfloat16
I32 = mybir.dt.int32
AF = mybir.ActivationFunctionType
ALU = mybir.AluOpType
AX = mybir.AxisListType

GEN = 12000
NDSEM = 8


class Res:
    __slots__ = ("w", "r")

    def __init__(self):
        self.w = None
        self.r = {}


class _Eng:
    def __init__(self, kb, key, eng, is_pe=False):
        self.kb, self.key, self.e, self.is_pe = kb, key, eng, is_pe
        self.n = 0
        self.sems = []
        self.known = {}

    def semval(self, seq):
        g = (seq - 1) // GEN
        while len(self.sems) <= g:
            self.sems.append(self.kb.sem(f"s_{self.key}_{len(self.sems)}"))
        return self.sems[g], (seq - 1) % GEN + 1


class _DSlot:
    def __init__(self, kb, key):
        self.key = key
        self.sem = kb.sem(f"d_{key}")
        self.n = 0

    def semval(self, seq):
        return self.sem, 16 * seq


class _Queue:
    def __init__(self, kb, name, eng, known):
        self.name, self.e, self.known = name, eng, known
        self.slots = [_DSlot(kb, f"{name}{i}") for i in range(NDSEM)]
        for s in self.slots:
            kb.prod[s.key] = s
        self.rr = 0


class KB:
    def __init__(self):
        self.nc = bass.Bass("TRN2", target_bir_lowering=False)
        self.es = ExitStack()
        self.prod = {}
        nc = self.nc
        self.PE = _Eng(self, "pe", nc.tensor, True)
        self.ACT = _Eng(self, "act", nc.scalar)
        self.DVE = _Eng(self, "dve", nc.vector)
        self.POOL = _Eng(self, "pool", nc.gpsimd)
        for e in (self.PE, self.ACT, self.DVE, self.POOL):
            self.prod[e.key] = e
        self.qs = _Queue(self, "qs", nc.sync, {})
        self.qg = _Queue(self, "qg", nc.gpsimd, self.POOL.known)
        self.nins = 0

    def sem(self, name):
        return self.es.enter_context(self.nc.semaphore(name))

    def sb(self, name, shape, dt, nres=1):
        t = self.es.enter_context(self.nc.sbuf_tensor("sb_" + name, list(shape), dt))
        if nres == 1:
            return t, Res()
        return t, [Res() for _ in range(nres)]

    def ps(self, name, shape, dt):
        t = self.es.enter_context(self.nc.psum_tensor("ps_" + name, list(shape), dt))
        return t, Res()

    def _waits(self, E_key, known, eng, R, W, skip_self):
        waits = {}

        def need(tok):
            if tok is None:
                return
            k, s = tok
            if waits.get(k, 0) < s:
                waits[k] = s

        for r in R:
            need(r.w)
        for w in W:
            need(w.w)
            for k, s in w.r.items():
                need((k, s))
        for k, s in waits.items():
            if skip_self and k == E_key:
                continue
            if known.get(k, 0) >= s:
                continue
            sem, val = self.prod[k].semval(s)
            eng.wait_ge(sem, val)
            known[k] = s

    def op(self, E, fn, R=(), W=()):
        self._waits(E.key, E.known, E.e, R, W, E.is_pe)
        ins = fn(E.e)
        n = E.n + 1
        sem, _ = E.semval(n)
        ins.then_inc(sem, 1)
        E.n = n
        self.nins += 1
        for r in R:
            r.r[E.key] = n
        for w in W:
            w.w = (E.key, n)
            w.r = {}
        return ins

    def pe(self, fn, R=(), W=()):
        return self.op(self.PE, fn, R, W)

    def act(self, fn, R=(), W=()):
        return self.op(self.ACT, fn, R, W)

    def dve(self, fn, R=(), W=()):
        return self.op(self.DVE, fn, R, W)

    def pool(self, fn, R=(), W=()):
        return self.op(self.POOL, fn, R, W)

    def dma(self, q, out, in_, R=(), W=(), **kw):
        slot = q.slots[q.rr]
        q.rr = (q.rr + 1) % NDSEM
        self._waits(None, q.known, q.e, R, W, False)
        if slot.n > 0 and q.known.get(slot.key, 0) < slot.n:
            q.e.wait_ge(slot.sem, 16 * slot.n)
            q.known[slot.key] = slot.n
        ins = q.e.dma_start(out=out, in_=in_, **kw)
        ins.then_inc(slot.sem, 16)
        slot.n += 1
        self.nins += 1
        for r in R:
            r.r[slot.key] = slot.n
        for w in W:
            w.w = (slot.key, slot.n)
            w.r = {}

    def finish(self):
        for q in (self.qs, self.qg):
            for s in q.slots:
                if s.n > 0:
                    q.e.wait_ge(s.sem, 16 * s.n)
        self.es.close()

    def make_identity(self, t, r, n=128, dt=F32):
        self.pool(lambda e: e.memset(t[:], 0.0), W=[r])
        self.pool(lambda e: e.affine_select(out=t[:], in_=t[:], pattern=[[-1, n]],
                                            compare_op=ALU.not_equal, fill=1.0, base=0,
                                            channel_multiplier=1), R=[r], W=[r])


    def barrier(self):
        toks = [(e.key, e.n) for e in (self.PE, self.ACT, self.DVE, self.POOL) if e.n > 0]
        for q in (self.qs, self.qg):
            toks += [(s.key, s.n) for s in q.slots if s.n > 0]
        cons = [(e.known, e.e) for e in (self.PE, self.ACT, self.DVE, self.POOL)] + [(self.qs.known, self.qs.e)]
        for known, eng in cons:
            for k, s in toks:
                if known.get(k, 0) >= s:
                    continue
                sem, val = self.prod[k].semval(s)
                eng.wait_ge(sem, val)
                known[k] = s


class Arena:
    def __init__(self, kb, nbytes):
        self.kb = kb
        self.t, _ = kb.sb("arena", [128, nbytes // 4], F32)
        self.cap = nbytes // 4
        self.off = 0
        self.peak = 0

    def reset(self):
        self.kb.barrier()
        self.off = 0

    def alloc(self, shape, dt, nres=1):
        n = int(np.prod(shape[1:]))
        words = n if dt in (F32, I32) else (n + 1) // 2
        assert self.off + words <= self.cap, f"arena overflow {self.off + words} > {self.cap}"
        v = self.t[:shape[0], self.off:self.off + words]
        self.off += words
        self.peak = max(self.peak, self.off)
        if dt != F32:
            v = v.bitcast(dt)
            if dt == BF16 and n % 2:
                v = v[:, :n]
        if len(shape) == 3:
            v = v.rearrange("p (a b) -> p a b", a=shape[1])
        elif len(shape) == 4:
            v = v.rearrange("p (a b c) -> p a b c", a=shape[1], b=shape[2])
        if nres == 1:
            return v, Res()
        return v, [Res() for _ in range(nres)]


D = 1024
TT = 2176
DFF = 2816
NB_CST = 2064
CB = dict(MsT_p=0, MiT_p=128, MsT_s=256, MiT_s=384, NEG_p=512, NEG_s=640, ones=768,
          mask4=896, rst_p=1408, rst_s=1920, ind16=2048)
CF = dict(ident=0, jidx=128, blk64=256, ones=384)
TWO_PI = 2.0 * math.pi
C1 = 6.28125
C2 = TWO_PI - C1
EXPM05 = math.exp(-0.5)
DEBUG = {}


class _Stop(Exception):
    pass


def build(dbg=(), stop_after=None):
    kb = KB()
    nc = kb.nc
    PE, ACT, DVE, POOL = kb.pe, kb.act, kb.dve, kb.pool
    mm = lambda e, *a, **k: e.matmul(*a, **k)

    def din(name, shape):
        return nc.dram_tensor(name, list(shape), F32, kind="ExternalInput").ap()

    def dout(name, shape):
        return nc.dram_tensor(name, list(shape), F32, kind="ExternalOutput").ap()

    I = {}
    for name, shape in [
        ("xT", [D, TT]), ("cstf", [128, 512]), ("cstb", [128, NB_CST]), ("sel8", [8, 1024]),
        ("nw", [128, 40]),
        ("W_in", [D, 2568]), ("W_out", [D, D]), ("W_glu", [512, 512]),
        ("Wg0", [D, DFF]), ("Wu0", [D, DFF]), ("Wd0", [DFF, D]),
        ("Wg1", [D, DFF]), ("Wu1", [D, DFF]), ("Wd1", [DFF, D]),
        ("W_r", [D, D]), ("W_k", [D, D]), ("W_v", [D, D]), ("W_o", [D, D]),
        ("w1", [D, 64]), ("w2", [64, D]), ("a1", [D, 64]), ("a2", [64, D]),
        ("g1", [D, 160]), ("g2", [160, D]),
        ("s5p", [128, 1076]), ("s5h0", [128, 512]),
        ("gp", [128, 51]), ("gconv0", [128, 576]), ("gs0", [4, 128, 2048]),
        ("rwp", [128, 104]), ("rshift0", [128, 128]), ("rs0", [8, 128, 1024]),
    ]:
        I[name] = din(name, shape)
    O = {}
    for name, shape in [
        ("yT", [D, TT]), ("o_s5", [128, 2 * 16 * 17]), ("o_gdn_p", [4, 128, 128]),
        ("o_gdn_s", [4, 128, 2048]), ("o_conv", [128, 12 * 17 * 3]), ("o_rw_p", [8, 128, 64]),
        ("o_rw_s", [8, 128, 1024]), ("o_shift", [128, 8 * 17]),
    ]:
        O[name] = dout(name, shape)

    def dbgout(name, ap, res, shape):
        if name in dbg:
            d = dout("dbg_" + name, shape)
            kb.dma(kb.qs, d, ap, R=[res] if isinstance(res, Res) else list(res))

    def checkpoint(name):
        if stop_after == name:
            raise _Stop()

    cstf, cstf_r = kb.sb("cstf", [128, 512], F32)
    cstb, cstb_r = kb.sb("cstb", [128, NB_CST], BF16)
    sel8, sel8_r = kb.sb("sel8", [8, 1024], F32)
    nw, nw_r = kb.sb("nw", [128, 40], F32)
    kc, kc_r = kb.sb("kc", [128, 8], F32)
    xT, xT_r = kb.sb("xTs", [128, 8, 512], F32)
    hT, hT_r = kb.sb("hT", [128, 8, 512], BF16)
    rstd, rstd_r = kb.sb("rstd", [128, 512], F32)
    NSLAB = 2
    wsl = [kb.sb(f"wslab{i}", [128, 4096], BF16) for i in range(NSLAB)]
    wsl_i = [0]
    ident = cstf[:, 0:128]
    jidx = cstf[:, 128:256]
    blk64 = cstf[:, 256:384]
    onesf = cstf[:, 384:512]
    cb_ = lambda k, n=128: cstb[:, CB[k]:CB[k] + n]
    onesb = cb_("ones")

    kb.dma(kb.qs, cstf[:], I["cstf"], W=[cstf_r])
    kb.dma(kb.qg, cstb[:], I["cstb"], W=[cstb_r])
    kb.dma(kb.qs, sel8[:], I["sel8"], W=[sel8_r])
    kb.dma(kb.qs, nw[:], I["nw"], W=[nw_r])
    for i, v in enumerate([math.pi / 2, 1e-6, 1.0, 64e-5, 0.0]):
        DVE(lambda e: e.memset(kc[:, i:i + 1], v), W=[kc_r])

    pA = [kb.ps(f"pA{i}", [128, 512], F32) for i in range(2)]
    pB, pB_r = kb.ps("pB", [128, 512], F32)
    pT, _ = kb.ps("pT", [128, 1024], BF16)
    pT_regs = [(pT[:, 0:128], Res())]
    pDb = [kb.ps(f"pD{i}", [128, 512], F32)[0] for i in range(4)]
    pD_regs = [(pDb[i], Res()) for i in range(4)] + [pA[0], pA[1], (pB, pB_r)]
    ctr = dict(a=0, d=0, t=0)

    def psA():
        ctr["a"] += 1
        return pA[ctr["a"] % 2]

    def psD():
        ctr["d"] += 1
        r_ = pD_regs[ctr["d"] % 7]
        return r_[0][:, 0:128], r_[1]

    def psT_full():
        return pT, pT_regs[0][1]

    def psD512():
        ctr["d"] += 1
        return pD_regs[ctr["d"] % 7]

    def psDw():
        ctr["d"] += 1
        r_ = pD_regs[ctr["d"] % 7]
        return r_[0][:, 0:256], r_[1]

    def psT():
        ctr["t"] += 1
        return pT_regs[0]

    arena = Arena(kb, 79 * 1024)

    SUB = 1024
    slab_state = dict(subs=[], p=0)

    def set_pool(buffers):
        slab_state["subs"] = [(t_, si, Res()) for t_, _ in buffers for si in range(4)]
        slab_state["p"] = 0
    set_pool(wsl)

    def get_slab(nelem):
        subs = slab_state["subs"]
        n = 1 if nelem <= SUB else (2 if nelem <= 2 * SUB else 4)
        p = slab_state["p"]
        p = (p + n - 1) // n * n
        if p + n > len(subs):
            p = 0
        t_, si, _ = subs[p]
        slab_state["p"] = p + n
        return t_[:, si * SUB:si * SUB + nelem], [subs[p + i][2] for i in range(n)]

    def linear_fm(W, K, col0, ncols, act, ntok, cb):
        kcs = [(r0, min(128, K - r0)) for r0 in range(0, K, 128)]
        nk = len(kcs)
        SW = ncols if ncols < 128 else min(ncols, (4096 // nk) // 128 * 128)
        for s0 in range(0, ncols, SW):
            sw = min(SW, ncols - s0)
            slab, sres = get_slab(nk * sw)
            view = slab.rearrange("p (k n) -> p k n", k=nk)
            if K % 128 == 0:
                kb.dma(kb.qg, view, W[:, col0 + s0:col0 + s0 + sw].rearrange("(k p) n -> p k n", p=128), W=sres)
            else:
                for ki, (r0, rows) in enumerate(kcs):
                    kb.dma(kb.qg, view[:rows, ki, :], W[r0:r0 + rows, col0 + s0:col0 + s0 + sw], W=sres)
            for m0 in range(0, sw, 128):
                m = min(128, sw - m0)
                ps, pres = psA()
                for ki, (r0, rows) in enumerate(kcs):
                    a, ares = act[ki]
                    PE(lambda e: e.matmul(ps[:m, :ntok], lhsT=view[:rows, ki, m0:m0 + m], rhs=a[:rows, :ntok],
                                          start=(ki == 0), stop=(ki == nk - 1)), R=sres + [ares], W=[pres])
                cb(ps, pres, col0 + s0 + m0, m)

    def rsqrt_into(out_ap, out_r, in_ap, in_r, scale, eps_col, n, parts=128):
        ACT(lambda e: e.activation(out=out_ap, in_=in_ap, func=AF.Ln, bias=kc[:parts, eps_col:eps_col + 1], scale=scale),
            R=[in_r, kc_r], W=[out_r])
        ACT(lambda e: e.activation(out=out_ap, in_=out_ap, func=AF.Exp, scale=-0.5), R=[out_r], W=[out_r])

    def rmsnorm(widx, ntok, scratch=None):
        sq, sq_r = scratch if scratch is not None else arena.alloc([128, 8, 512], BF16)
        ACT(lambda e: e.activation(out=sq[:, :, :ntok], in_=xT[:, :, :ntok], func=AF.Square), R=[xT_r], W=[sq_r])
        ps, pres = psA()
        for c in range(8):
            PE(lambda e: e.matmul(ps[:, :ntok], lhsT=onesb, rhs=sq[:, c, :ntok], start=(c == 0), stop=(c == 7)),
               R=[cstb_r, sq_r], W=[pres])
        rsqrt_into(rstd[:, :ntok], rstd_r, ps[:, :ntok], pres, 1.0 / D, 1, ntok)
        for c in range(8):
            DVE(lambda e: e.scalar_tensor_tensor(out=hT[:, c, :ntok], in0=xT[:, c, :ntok],
                                                 scalar=nw[:, widx * 8 + c:widx * 8 + c + 1], in1=rstd[:, :ntok],
                                                 op0=ALU.mult, op1=ALU.mult), R=[xT_r, nw_r, rstd_r], W=[hT_r])

    def resid_add_cb(ntok):
        def cb(ps, pres, col, m):
            c = col // 128
            DVE(lambda e: e.tensor_tensor(out=xT[:, c, :ntok], in0=xT[:, c, :ntok], in1=ps[:, :ntok], op=ALU.add),
                R=[xT_r, pres], W=[xT_r])
        return cb

    def ffn(layer, ntok):
        arena.reset()
        actT, actT_r = arena.alloc([128, 22, 512], BF16)
        sg, sg_r = arena.alloc([128, 4, 512], BF16, nres=4)
        rmsnorm(1 + 2 * layer, ntok)
        extra = []
        for i in range(5):
            t_, r_ = arena.alloc([128, 4096], BF16)
            extra.append((t_, r_))
        set_pool(list(wsl) + extra)
        Wg, Wu, Wd = I[f"Wg{layer}"], I[f"Wu{layer}"], I[f"Wd{layer}"]
        hact = [(hT[:, c, :], hT_r) for c in range(8)]
        cnt = [0]

        def cb_gate(ps, pres, col, m):
            i = (col // 128) % 4
            ACT(lambda e: e.activation(out=sg[:, i, :ntok], in_=ps[:, :ntok], func=AF.Silu), R=[pres], W=[sg_r[i]])

        def cb_up(ps, pres, col, m):
            c = col // 128
            i = c % 4
            DVE(lambda e: e.tensor_tensor(out=actT[:, c, :ntok], in0=sg[:, i, :ntok], in1=ps[:, :ntok], op=ALU.mult),
                R=[sg_r[i], pres], W=[actT_r])
        for f0 in range(0, DFF, 512):
            fw = min(512, DFF - f0)
            linear_fm(Wg, D, f0, fw, hact, ntok, cb_gate)
            linear_fm(Wu, D, f0, fw, hact, ntok, cb_up)
        aact = [(actT[:, c, :], actT_r) for c in range(22)]
        linear_fm(Wd, DFF, 0, D, aact, ntok, resid_add_cb(ntok))
        set_pool(wsl)

    lhsTB, lhsTB_r = kb.sb("lhsTB", [128, 2, 16, 128], BF16)
    lhsTC, lhsTC_r = kb.sb("lhsTC", [128, 2, 16, 128], BF16)
    Ecs, Ecs_r = kb.sb("Ecs", [128, 2, 16, 128], F32)
    s5c, s5c_r = kb.sb("s5c", [128, 8, 16], F32)
    s5o, s5o_r = kb.sb("s5o", [128, 2, 16, 17], F32)
    s5k, s5k_r = kb.sb("s5k", [128, 3, 16, 2], F32)

    def sincos(x, n, out_s, out_c, res_x, res_o):
        s1, s1_r = arena.alloc([128, n], F32)
        si, si_r = arena.alloc([128, n], I32)
        r, r_r = arena.alloc([128, n], F32)
        DVE(lambda e: e.tensor_scalar(out=s1, in0=x, scalar1=1.0 / TWO_PI, scalar2=None, op0=ALU.mult), R=[res_x], W=[s1_r])
        DVE(lambda e: e.tensor_copy(out=si, in_=s1), R=[s1_r], W=[si_r])
        DVE(lambda e: e.tensor_copy(out=s1, in_=si), R=[si_r], W=[s1_r])
        DVE(lambda e: e.scalar_tensor_tensor(out=r, in0=s1, scalar=-C1, in1=x, op0=ALU.mult, op1=ALU.add), R=[s1_r, res_x], W=[r_r])
        DVE(lambda e: e.scalar_tensor_tensor(out=r, in0=s1, scalar=-C2, in1=r, op0=ALU.mult, op1=ALU.add), R=[s1_r, r_r], W=[r_r])
        DVE(lambda e: e.tensor_scalar(out=s1, in0=r, scalar1=math.pi, scalar2=-TWO_PI, op0=ALU.is_gt, op1=ALU.mult), R=[r_r], W=[s1_r])
        DVE(lambda e: e.tensor_tensor(out=r, in0=r, in1=s1, op=ALU.add), R=[r_r, s1_r], W=[r_r])
        DVE(lambda e: e.tensor_scalar(out=s1, in0=r, scalar1=-math.pi, scalar2=TWO_PI, op0=ALU.is_lt, op1=ALU.mult), R=[r_r], W=[s1_r])
        DVE(lambda e: e.tensor_tensor(out=r, in0=r, in1=s1, op=ALU.add), R=[r_r, s1_r], W=[r_r])
        ACT(lambda e: e.activation(out=out_s, in_=r, func=AF.Sin), R=[r_r], W=[res_o])
        DVE(lambda e: e.scalar_tensor_tensor(out=s1, in0=r, scalar=-1.0, in1=r, op0=ALU.mult, op1=ALU.max), R=[r_r], W=[s1_r])
        ACT(lambda e: e.activation(out=out_c, in_=s1, func=AF.Sin, bias=kc[:, 0:1], scale=-1.0), R=[s1_r, kc_r], W=[res_o])

    def s5_setup():
        p, p_r = arena.alloc([128, 1076], F32)
        kb.dma(kb.qs, p, I["s5p"], W=[p_r])
        t, t_r = arena.alloc([128, 12, 16], F32)
        lr, li, ls = p[:, 0:16], p[:, 16:32], p[:, 32:48]
        T = lambda i: t[:, i, :]
        tt = lambda o, a, b, op: DVE(lambda e: e.tensor_tensor(out=o, in0=a, in1=b, op=op), R=[p_r, t_r, s5c_r], W=[t_r])
        ACT(lambda e: e.activation(out=T(0), in_=ls, func=AF.Exp), R=[p_r], W=[t_r])
        tt(T(1), lr, T(0), ALU.mult)
        ACT(lambda e: e.activation(out=s5c[:, 0, :], in_=T(1), func=AF.Exp), R=[t_r], W=[s5c_r])
        tt(T(2), li, T(0), ALU.mult)
        checkpoint('c1a')
        sincos(T(2), 16, T(3), T(4), t_r, t_r)
        checkpoint('c1b')
        tt(T(5), s5c[:, 0, :], T(4), ALU.mult)
        tt(T(6), s5c[:, 0, :], T(3), ALU.mult)
        DVE(lambda e: e.tensor_scalar(out=T(5), in0=T(5), scalar1=-1.0, scalar2=None, op0=ALU.add), R=[t_r], W=[t_r])
        tt(T(7), lr, lr, ALU.mult)
        tt(T(8), li, li, ALU.mult)
        tt(T(7), T(7), T(8), ALU.add)
        DVE(lambda e: e.reciprocal(out=T(7), in_=T(7)), R=[t_r], W=[t_r])
        tt(T(8), T(5), lr, ALU.mult)
        tt(T(9), T(6), li, ALU.mult)
        tt(T(8), T(8), T(9), ALU.add)
        tt(T(8), T(8), T(7), ALU.mult)
        tt(T(9), T(6), lr, ALU.mult)
        tt(T(10), T(5), li, ALU.mult)
        tt(T(9), T(9), T(10), ALU.subtract)
        tt(T(9), T(9), T(7), ALU.mult)
        DVE(lambda e: e.tensor_copy(out=s5c[:, 1, 0:4], in_=p[:, 1072:1076]), R=[p_r], W=[s5c_r])
        DVE(lambda e: e.memset(s5c[:, 2:4, :], 0.0), W=[s5c_r])
        checkpoint('c1c')
        Bre = p[:, 48:304].rearrange("p (s c) -> p s c", s=16)
        Bim = p[:, 304:560].rearrange("p (s c) -> p s c", s=16)
        Cre = p[:, 560:816].rearrange("p (s c) -> p s c", s=16)
        Cim = p[:, 816:1072].rearrange("p (s c) -> p s c", s=16)
        bb, bb_r = arena.alloc([128, 4, 16, 16], F32)
        crb = T(8).unsqueeze(2).to_broadcast([128, 16, 16])
        cib = T(9).unsqueeze(2).to_broadcast([128, 16, 16])
        t3 = lambda o, a, b, op: DVE(lambda e: e.tensor_tensor(out=o, in0=a, in1=b, op=op), R=[p_r, t_r, bb_r], W=[bb_r])
        t3(bb[:, 0], Bre, crb, ALU.mult)
        t3(bb[:, 1], Bim, cib, ALU.mult)
        t3(bb[:, 0], bb[:, 0], bb[:, 1], ALU.subtract)
        t3(bb[:, 2], Bim, crb, ALU.mult)
        t3(bb[:, 3], Bre, cib, ALU.mult)
        t3(bb[:, 2], bb[:, 2], bb[:, 3], ALU.add)
        checkpoint('c1d')
        ex, ex_r = arena.alloc([128, 2, 128], F32, nres=2)
        n = 0
        for ri, src in ((0, bb[:, 0]), (1, bb[:, 2])):
            for sb_ in range(16):
                i = n % 2
                n += 1
                m4 = cstb[:, CB["mask4"] + (sb_ % 4) * 128:CB["mask4"] + (sb_ % 4 + 1) * 128].rearrange("p (g c) -> p g c", g=8)
                DVE(lambda e: e.tensor_tensor(out=ex[:, i, :].rearrange("p (g c) -> p g c", g=8),
                                              in0=src[:, sb_, :].unsqueeze(1).to_broadcast([128, 8, 16]), in1=m4, op=ALU.mult),
                    R=[bb_r, cstb_r], W=[ex_r[i]])
                ps, pres = psD()
                PE(lambda e: e.transpose(ps, ex[:, i, :], ident), R=[ex_r[i], cstf_r], W=[pres])
                ACT(lambda e: e.activation(out=lhsTB[:, ri, sb_, :], in_=ps, func=AF.Copy), R=[pres], W=[lhsTB_r])
        checkpoint('c1d2')
        for sb_ in range(16):
            m4 = cstb[:, CB["mask4"] + (sb_ % 4) * 128:CB["mask4"] + (sb_ % 4 + 1) * 128].rearrange("p (g c) -> p g c", g=8)
            DVE(lambda e: e.tensor_tensor(out=lhsTC[:, 0, sb_, :].rearrange("p (g c) -> p g c", g=8),
                                          in0=Cre[:, sb_, :].unsqueeze(1).to_broadcast([128, 8, 16]), in1=m4, op=ALU.mult),
                R=[p_r, cstb_r], W=[lhsTC_r])
            DVE(lambda e: e.scalar_tensor_tensor(out=lhsTC[:, 1, sb_, :].rearrange("p (g c) -> p g c", g=8),
                                                 in0=Cim[:, sb_, :].unsqueeze(1).to_broadcast([128, 8, 16]), scalar=-1.0, in1=m4,
                                                 op0=ALU.mult, op1=ALU.mult), R=[p_r, cstb_r], W=[lhsTC_r])
        checkpoint('c1e')
        phi, phi_r = arena.alloc([128, 16, 128], F32)
        DVE(lambda e: e.tensor_tensor(out=phi, in0=T(2).unsqueeze(2).to_broadcast([128, 16, 128]),
                                      in1=jidx.unsqueeze(1).to_broadcast([128, 16, 128]), op=ALU.mult),
            R=[t_r, cstf_r], W=[phi_r])
        sincos(phi.rearrange("p a b -> p (a b)"), 2048, Ecs[:, 1].rearrange("p a b -> p (a b)"),
               Ecs[:, 0].rearrange("p a b -> p (a b)"), phi_r, Ecs_r)
        DVE(lambda e: e.memset(s5k[:, 0], 0.0), W=[s5k_r])
        DVE(lambda e: e.tensor_copy(out=s5k[:, 1, :, 0], in_=Ecs[:, 0, :, 127]), R=[Ecs_r], W=[s5k_r])
        DVE(lambda e: e.tensor_copy(out=s5k[:, 1, :, 1], in_=Ecs[:, 1, :, 127]), R=[Ecs_r], W=[s5k_r])
        DVE(lambda e: e.tensor_scalar(out=s5k[:, 2, :, 0], in0=Ecs[:, 1, :, 127], scalar1=-1.0, scalar2=None, op0=ALU.mult), R=[Ecs_r], W=[s5k_r])
        DVE(lambda e: e.tensor_copy(out=s5k[:, 2, :, 1], in_=Ecs[:, 0, :, 127]), R=[Ecs_r], W=[s5k_r])

    arena.off = 0
    _early = False
    try:
        checkpoint("c0")
        s5_setup()
        checkpoint("c1")
    except _Stop:
        _early = True
    if _early:
        dbgout("Ecs", Ecs[:].rearrange("p a b c -> p (a b c)"), Ecs_r, [128, 4096])
        dbgout("s5c", s5c[:].rearrange("p a b -> p (a b)"), s5c_r, [128, 128])
        dbgout("kc", kc[:], kc_r, [128, 8])
        kb.finish()
        return nc

    identb, identb_r = kb.sb("identb", [128, 128], BF16)
    blk64b, blk64b_r = kb.sb("blk64b", [128, 128], BF16)
    DVE(lambda e: e.tensor_copy(out=blk64b[:], in_=blk64), R=[cstf_r], W=[blk64b_r])
    DVE(lambda e: e.tensor_copy(out=identb[:], in_=ident), R=[cstf_r], W=[identb_r])
    nwAll, _ = kb.sb("nwAll", [128, 4, 512], F32)
    nwB32 = [(nwAll[:, i, :].rearrange("p (a b) -> p a b", a=4), [Res() for _ in range(4)]) for i in range(4)]
    nwAll16 = nwAll[:].rearrange("p a b -> p (a b)").bitcast(BF16).rearrange("p (a b) -> p a b", a=4)
    nwB16 = [(nwAll16[:, i, :640].rearrange("p (a b) -> p a b", a=5), [Res() for _ in range(5)]) for i in range(4)]
    nwFall, _ = kb.sb("nwFall", [128, 4, 128], F32)
    nwF = [(nwFall[:, i, :], Res()) for i in range(4)]
    TTfin, TTfin_r = kb.sb("TTfin", [128, 8, 128], F32, nres=8)
    chF = [kb.sb(f"chF{i}", [128, 2, 128], F32, nres=2) for i in range(2)]
    chB = [kb.sb(f"chB{i}", [128, 1, 128], BF16, nres=1) for i in range(2)]
    matsB, _ = kb.sb("matsB", [128, 24, 128], BF16)
    matsB_r = [Res() for _ in range(24)]
    tmB, _ = kb.sb("tmB", [128, 12, 128], BF16)
    tmB_r = [Res() for _ in range(12)]
    slotK = [kb.sb(f"slotK{i}", [128, 2, 512], BF16, nres=2) for i in range(2)]

    def tm_buf(ti, hi, k):
        i = ((ti % 2) * 2 + hi) * 3 + k
        return tmB[:, i, :], tmB_r[i]

    def run_gens(gens):
        gens = list(gens)
        while gens:
            for g in list(gens):
                try:
                    next(g)
                except StopIteration:
                    gens.remove(g)

    def dplr_run(nt, heads, dk, dv, is_s, emit_XT, emit_tm, ops, S_of, eT_of, ycb_of, NEU_BF16=False, extra=(), make_only=False):
        nseg = 16 if is_s else 1
        nlev = 2 if is_s else 6
        NG, TM = {}, {}

        def neumann(ti, hi, ws):
            nb, nb_r = (nwB16 if NEU_BF16 else nwB32)[ws]
            tf, tf_r = nwF[ws]
            k = (ti % 4) * 2 + hi
            fin = (TTfin[:, k, :], TTfin_r[k])
            oth = (tf, tf_r)
            B = lambda i: (nb[:, i, :], nb_r[i])
            XTc = B(1)
            emit_XT(ti, hi, XTc[0], XTc[1], [(matsB[:, k * 3 + i, :], matsB_r[k * 3 + i]) for i in range(3)])
            yield
            Xc = B(0)
            if NEU_BF16:
                pt, pt_r = psT()
                PE(lambda e: e.transpose(pt, XTc[0], identb[:]), R=[XTc[1], identb_r], W=[pt_r])
                ACT(lambda e: e.activation(out=Xc[0], in_=pt, func=AF.Copy), R=[pt_r], W=[Xc[1]])
            else:
                ps, pres = psD()
                PE(lambda e: e.transpose(ps, XTc[0], ident), R=[XTc[1], cstf_r], W=[pres])
                ACT(lambda e: e.activation(out=Xc[0], in_=ps, func=AF.Copy), R=[pres], W=[Xc[1]])
            TT = fin if nlev % 2 == 0 else oth
            TTo = oth if nlev % 2 == 0 else fin
            DVE(lambda e: e.tensor_tensor(out=TT[0], in0=XTc[0], in1=ident, op=ALU.add), R=[XTc[1], cstf_r], W=[TT[1]])
            if NEU_BF16:
                TTb = B(4)
                DVE(lambda e: e.tensor_tensor(out=TTb[0], in0=XTc[0], in1=identb[:], op=ALU.add), R=[XTc[1], identb_r], W=[TTb[1]])
            else:
                TTb = TT
            yield
            for lev in range(1, nlev + 1):
                pi_ = 2 * (lev % 2)
                Xn, XTn = B(pi_), B(pi_ + 1)
                if lev < nlev:
                    psw, psw_r = psDw()
                    PE(lambda e: e.matmul(psw[:, 0:128], lhsT=XTc[0], rhs=Xc[0], start=True, stop=True), R=[XTc[1], Xc[1]], W=[psw_r])
                    PE(lambda e: e.matmul(psw[:, 128:256], lhsT=Xc[0], rhs=XTc[0], start=True, stop=True), R=[XTc[1], Xc[1]], W=[psw_r])
                    ACT(lambda e: e.activation(out=nb[:, pi_:pi_ + 2, :], in_=psw.rearrange("p (a b) -> p a b", a=2), func=AF.Copy),
                        R=[psw_r], W=[Xn[1], XTn[1]])
                else:
                    psx, psx_r = psD()
                    PE(lambda e: e.matmul(psx, lhsT=XTc[0], rhs=Xc[0], start=True, stop=True), R=[XTc[1], Xc[1]], W=[psx_r])
                    ACT(lambda e: e.activation(out=Xn[0], in_=psx, func=AF.Copy), R=[psx_r], W=[Xn[1]])
                yield
                psa, psa_r = psD()
                PE(lambda e: e.matmul(psa, lhsT=Xn[0], rhs=TTb[0], start=True, stop=True), R=[Xn[1], TTb[1]], W=[psa_r])
                DVE(lambda e: e.tensor_tensor(out=TTo[0], in0=psa, in1=TT[0], op=ALU.add), R=[psa_r, TT[1]], W=[TTo[1]])
                TT, TTo = TTo, TT
                if NEU_BF16:
                    if lev < nlev:
                        ACT(lambda e: e.activation(out=TTb[0], in_=TT[0], func=AF.Copy), R=[TT[1]], W=[TTb[1]])
                else:
                    TTb = TT
                Xc, XTc = Xn, XTn
                yield
            assert TT[0] is fin[0]
            NG[(ti, hi)] = TT

        def neumann_pair(ti, pw):
            wsA, wsB = 2 * pw, 2 * pw + 1
            bf = NEU_BF16
            if bf:
                nbp = nwAll16[:, wsA:wsB + 1, :640].rearrange("p h (a b) -> p h a b", a=5)
                rA, rB = nwB16[wsA][1], nwB16[wsB][1]
            else:
                nbp = nwAll[:, wsA:wsB + 1, :].rearrange("p h (a b) -> p h a b", a=4)
                rA, rB = nwB32[wsA][1], nwB32[wsB][1]
            k0 = (ti % 4) * 2
            fin = (TTfin[:, k0:k0 + 2, :], [TTfin_r[k0], TTfin_r[k0 + 1]])
            oth = (nwFall[:, wsA:wsB + 1, :], [nwF[wsA][1], nwF[wsB][1]])
            BR = lambda i: [rA[i], rB[i]]
            for hi in range(2):
                emit_XT(ti, hi, nbp[:, hi, 1, :], BR(1)[hi], [(matsB[:, (k0 + hi) * 3 + i, :], matsB_r[(k0 + hi) * 3 + i]) for i in range(3)])
            yield
            if bf:
                pt, pt_r = psT_full()
                idn, idn_r = identb[:], identb_r
            else:
                pt, pt_r = psDw()
                idn, idn_r = ident, cstf_r
            for hi in range(2):
                PE(lambda e: e.transpose(pt[:, hi * 128:(hi + 1) * 128], nbp[:, hi, 1, :], idn), R=[BR(1)[hi], idn_r], W=[pt_r])
            ACT(lambda e: e.activation(out=nbp[:, :, 0, :], in_=pt[:, 0:256].rearrange("p (h b) -> p h b", h=2), func=AF.Copy),
                R=[pt_r], W=BR(0))
            TT, TTo = fin, oth
            DVE(lambda e: e.tensor_tensor(out=TT[0], in0=nbp[:, :, 1, :], in1=ident.unsqueeze(1).to_broadcast([128, 2, 128]), op=ALU.add),
                R=BR(1) + [cstf_r], W=TT[1])
            if bf:
                DVE(lambda e: e.tensor_tensor(out=nbp[:, :, 4, :], in0=nbp[:, :, 1, :], in1=identb[:].unsqueeze(1).to_broadcast([128, 2, 128]),
                                              op=ALU.add), R=BR(1) + [identb_r], W=BR(4))
            yield
            cur = 0
            for lev in range(1, nlev + 1):
                nx = 2 - cur
                if lev < nlev:
                    psw, psw_r = psD512()
                    for hi in range(2):
                        PE(lambda e: e.matmul(psw[:, hi * 256:hi * 256 + 128], lhsT=nbp[:, hi, cur + 1, :], rhs=nbp[:, hi, cur, :],
                                              start=True, stop=True), R=[BR(cur)[hi], BR(cur + 1)[hi]], W=[psw_r])
                        PE(lambda e: e.matmul(psw[:, hi * 256 + 128:hi * 256 + 256], lhsT=nbp[:, hi, cur, :], rhs=nbp[:, hi, cur + 1, :],
                                              start=True, stop=True), R=[BR(cur)[hi], BR(cur + 1)[hi]], W=[psw_r])
                    ACT(lambda e: e.activation(out=nbp[:, :, nx:nx + 2, :], in_=psw.rearrange("p (h a b) -> p h a b", h=2, a=2), func=AF.Copy),
                        R=[psw_r], W=BR(nx) + BR(nx + 1))
                else:
                    psx, psx_r = psDw()
                    for hi in range(2):
                        PE(lambda e: e.matmul(psx[:, hi * 128:(hi + 1) * 128], lhsT=nbp[:, hi, cur + 1, :], rhs=nbp[:, hi, cur, :],
                                              start=True, stop=True), R=[BR(cur)[hi], BR(cur + 1)[hi]], W=[psx_r])
                    ACT(lambda e: e.activation(out=nbp[:, :, nx, :], in_=psx.rearrange("p (h b) -> p h b", h=2), func=AF.Copy),
                        R=[psx_r], W=BR(nx))
                yield
                psa, psa_r = psDw()
                for hi in range(2):
                    rhs_tt = nbp[:, hi, 4, :] if bf else TT[0][:, hi, :]
                    rhs_r = BR(4)[hi] if bf else TT[1][hi]
                    PE(lambda e: e.matmul(psa[:, hi * 128:(hi + 1) * 128], lhsT=nbp[:, hi, nx, :], rhs=rhs_tt, start=True, stop=True),
                       R=[BR(nx)[hi], rhs_r], W=[psa_r])
                DVE(lambda e: e.tensor_tensor(out=TTo[0], in0=psa.rearrange("p (h b) -> p h b", h=2), in1=TT[0], op=ALU.add),
                    R=[psa_r] + TT[1], W=TTo[1])
                TT, TTo = TTo, TT
                if bf and lev < nlev:
                    DVE(lambda e: e.tensor_copy(out=nbp[:, :, 4, :], in_=TT[0]), R=TT[1], W=BR(4))
                cur = nx
                yield
            assert TT is fin
            for hi in range(2):
                NG[(ti, hi)] = (TTfin[:, k0 + hi, :], TTfin_r[k0 + hi])

        def chain_seq(hi, tiles):
            pb = heads[hi]
            cf, cf_r = chF[hi]
            cb2, cb2_r = chB[hi]
            sk, sk_r = slotK[hi]
            RHS = (cf[:, 0, :], cf_r[0])
            QT = (cf[:, 1, :], cf_r[1])
            U = (cb2[:, 0, :dv], cb2_r)
            S = S_of(hi)
            for ti in tiles:
                if ti not in TM:
                    TM[ti] = emit_tm(ti)
                Vt, bSt, kSt = TM[ti][hi]
                k_ = (ti % 4) * 2 + hi
                AKT, RBT, RKT = [(matsB[:, k_ * 3 + i, :], matsB_r[k_ * 3 + i]) for i in range(3)]
                TT = NG[(ti, hi)]
                aS, rS, aSr = ops(ti, hi)
                eTot, eTot_r = eT_of(ti, hi)
                yield
                ps, pres = psD()
                PE(lambda e: e.matmul(ps[:, :dv], lhsT=AKT[0], rhs=Vt[0], start=True, stop=False), R=[AKT[1], Vt[1]], W=[pres])
                if not is_s:
                    PE(lambda e: e.matmul(ps[:, :dv], lhsT=aS, rhs=S["b"], start=False, stop=True), R=[aSr, S["r"]], W=[pres])
                else:
                    psq, psq_r = psD()
                    for j in range(16):
                        PE(lambda e: e.matmul(psq[:dv, 8 * j:8 * j + 8], lhsT=S["b"][:, j * dv:(j + 1) * dv], rhs=aS[:, 8 * j:8 * j + 8],
                                              start=(j == 0), stop=(j == 15)), R=[aSr, S["r"]], W=[psq_r])
                    ACT(lambda e: e.activation(out=QT[0][:dv, :], in_=psq[:dv, :], func=AF.Copy), R=[psq_r], W=[QT[1]])
                    PE(lambda e: e.matmul(ps[:, :dv], lhsT=QT[0][:dv, :], rhs=ident[:dv, :dv], start=False, stop=True),
                       R=[QT[1], cstf_r], W=[pres])
                DVE(lambda e: e.tensor_copy(out=RHS[0][:, :dv], in_=ps[:, :dv]), R=[pres], W=[RHS[1]])
                yield
                ps, pres = psD()
                PE(lambda e: e.matmul(ps[:, :dv], lhsT=TT[0], rhs=RHS[0][:, :dv], start=True, stop=True), R=[TT[1], RHS[1]], W=[pres])
                ACT(lambda e: e.activation(out=U[0], in_=ps[:, :dv], func=AF.Copy), R=[pres], W=[U[1]])
                yield
                ps, pres = psD()
                yo = ps[pb:pb + dv, :]
                PE(lambda e: e.matmul(yo, lhsT=U[0], rhs=RBT[0], start=True, stop=False), R=[U[1], RBT[1]], W=[pres])
                PE(lambda e: e.matmul(yo, lhsT=Vt[0], rhs=RKT[0], start=False, stop=False), R=[Vt[1], RKT[1]], W=[pres])
                for j in range(nseg):
                    c0, Lg = (8 * j, 8) if is_s else (0, 128)
                    PE(lambda e: e.matmul(ps[pb:pb + dv, c0:c0 + Lg], lhsT=S["b"][:, j * dv:(j + 1) * dv], rhs=rS[:, c0:c0 + Lg],
                                          start=False, stop=(j == nseg - 1)), R=[aSr, S["r"]], W=[pres])
                ycb_of(ti, hi)(yo, pres)
                yield
                if not is_s:
                    ps, pres = psD()
                    so = ps[pb:pb + dk, :dv]
                    PE(lambda e: e.matmul(so, lhsT=bSt[0], rhs=U[0], start=True, stop=False), R=[bSt[1], U[1]], W=[pres])
                    PE(lambda e: e.matmul(so, lhsT=kSt[0], rhs=Vt[0], start=False, stop=True), R=[kSt[1], Vt[1]], W=[pres])
                    DVE(lambda e: e.scalar_tensor_tensor(out=S["f"], in0=S["f"], scalar=eTot[:, 0:1], in1=so, op0=ALU.mult, op1=ALU.add),
                        R=[S["r"], eTot_r, pres], W=[S["r"]])
                    ACT(lambda e: e.activation(out=S["b"], in_=S["f"], func=AF.Copy), R=[S["r"]], W=[S["r"]])
                else:
                    spg = 512 // dv
                    ind = cstb[:, CB["ind16"]:CB["ind16"] + 16]
                    for g in range(16 // spg):
                        for wi_, src in ((0, U), (1, Vt)):
                            DVE(lambda e: e.tensor_tensor(out=sk[:, wi_, :].rearrange("p (j v) -> p j v", j=spg),
                                                          in0=src[0].unsqueeze(1).to_broadcast([128, spg, dv]),
                                                          in1=ind[:, g * spg:(g + 1) * spg].unsqueeze(2).to_broadcast([128, spg, dv]),
                                                          op=ALU.mult), R=[src[1], cstb_r], W=[sk_r[wi_]])
                        PE(lambda e: e.matmul(pB[pb:pb + dk, :], lhsT=bSt[0], rhs=sk[:, 0, :], start=True, stop=False),
                           R=[bSt[1], sk_r[0]], W=[pB_r])
                        PE(lambda e: e.matmul(pB[pb:pb + dk, :], lhsT=kSt[0], rhs=sk[:, 1, :], start=False, stop=True),
                           R=[kSt[1], sk_r[1]], W=[pB_r])
                        for jj in range(spg):
                            j = g * spg + jj
                            DVE(lambda e: e.scalar_tensor_tensor(out=S["f"][:, j * dv:(j + 1) * dv], in0=S["f"][:, j * dv:(j + 1) * dv],
                                                                 scalar=eTot[:, j:j + 1], in1=pB[pb:pb + dk, jj * dv:(jj + 1) * dv],
                                                                 op0=ALU.mult, op1=ALU.add), R=[S["r"], eTot_r, pB_r], W=[S["r"]])
                yield

        nh = len(heads)
        pairs = [list(range(t0, min(t0 + 2, nt))) for t0 in range(0, nt, 2)]

        def neu_gens(tiles):
            if nh == 2 and nlev % 2 == 0:
                return [neumann_pair(ti, ti % 2) for ti in tiles]
            return [neumann(ti, hi, (ti % 2) * 2 + hi) for ti in tiles for hi in range(nh)]

        def chain_gens(tiles):
            return [chain_seq(hi, tiles) for hi in range(nh)]
        if make_only:
            return dict(neu=neu_gens, chain=chain_gens, pairs=pairs)
        run_gens(neu_gens(pairs[0]) + list(extra))
        for pi, tiles in enumerate(pairs):
            gens = chain_gens(tiles)
            if pi + 1 < len(pairs):
                gens += neu_gens(pairs[pi + 1])
            run_gens(gens + list(extra))

    gp, gp_r = kb.sb("gp", [128, 51], F32)
    kb.dma(kb.qs, gp[:], I["gp"], W=[gp_r])
    negA8, negA8_r = kb.sb("negA8", [8, 1], F32)
    ACT(lambda e: e.activation(out=negA8[:], in_=gp[:8, 49:50], func=AF.Exp), R=[gp_r], W=[negA8_r])
    DVE(lambda e: e.tensor_scalar(out=negA8[:], in0=negA8[:], scalar1=-1.0, scalar2=None, op0=ALU.mult), R=[negA8_r], W=[negA8_r])
    hist, hist_r = kb.sb("hist", [128, 12, 3], F32)
    DVE(lambda e: e.memset(hist[:], 0.0), W=[hist_r])
    oconv, oconv_r = kb.sb("oconv", [128, 12, 17, 3], F32)
    Sg_f, _ = kb.sb("Sg_f", [128, 4, 128], F32)
    Sg_b, _ = kb.sb("Sg_b", [128, 4, 128], BF16)
    Sg_r = [Res() for _ in range(4)]
    DVE(lambda e: e.memset(Sg_f[:], 0.0), W=Sg_r)
    DVE(lambda e: e.memset(Sg_b[:], 0.0), W=Sg_r)

    def s5_tb(tb, ntok, is_s, uaT, uaT_r, catT, catT_r, XR, XR_r):
        nseg, L = (16, 8) if is_s else (4, 128)
        wsets = [arena.alloc([128, 6, 512], F32, nres=6) for _ in range(2)]
        tmpcs = [arena.alloc([128, 4, 16], F32) for _ in range(2)]
        yt, yt_r = arena.alloc([128, 2, 512], F32, nres=2)
        ygT, ygT_r = arena.alloc([128, 4, 512], BF16)
        if is_s:
            h0, h0_r = arena.alloc([128, 2, 16, 16], F32)
            kb.dma(kb.qs, h0.rearrange("p a b c -> p (a b c)"), I["s5h0"], W=[h0_r])
        sv = lambda ap: ap.rearrange("p (s l) -> p s l", s=nseg)

        def sb_gen(sb_, k):
            q, sbq = sb_ // 4, sb_ % 4
            w, w_r = wsets[k]
            tmpc, tmpc_r = tmpcs[k]
            W_ = lambda i: w[:, i, :ntok]
            psr, psr_r = psD512()
            PE(lambda e: e.matmul(psr[:, :ntok], lhsT=lhsTB[:, 0, sb_, :], rhs=uaT[:, q, :ntok], start=True, stop=True),
               R=[lhsTB_r, uaT_r], W=[psr_r])
            psi, psi_r = psD512()
            PE(lambda e: e.matmul(psi[:, :ntok], lhsT=lhsTB[:, 1, sb_, :], rhs=uaT[:, q, :ntok], start=True, stop=True),
               R=[lhsTB_r, uaT_r], W=[psi_r])
            Ec = Ecs[:, 0, sb_, 0:L].unsqueeze(1).to_broadcast([128, nseg, L])
            Es = Ecs[:, 1, sb_, 0:L].unsqueeze(1).to_broadcast([128, nseg, L])

            def tt(eng, o, oi, a, ar, b, op):
                eng(lambda e: e.tensor_tensor(out=sv(o), in0=sv(a), in1=b, op=op), R=[ar, Ecs_r], W=[w_r[oi]])

            def t2(eng, oi, ai, bi, op):
                eng(lambda e: e.tensor_tensor(out=W_(oi), in0=W_(ai), in1=W_(bi), op=op), R=[w_r[ai], w_r[bi]], W=[w_r[oi]])
            ACT(lambda e: e.activation(out=W_(4), in_=psr[:, :ntok], func=AF.Copy), R=[psr_r], W=[w_r[4]])
            ACT(lambda e: e.activation(out=W_(5), in_=psi[:, :ntok], func=AF.Copy), R=[psi_r], W=[w_r[5]])
            yield
            tt(DVE, W_(0), 0, W_(4), w_r[4], Ec, ALU.mult)
            tt(POOL, W_(1), 1, W_(5), w_r[5], Es, ALU.mult)
            tt(POOL, W_(2), 2, W_(5), w_r[5], Ec, ALU.mult)
            tt(DVE, W_(3), 3, W_(4), w_r[4], Es, ALU.mult)
            yield
            t2(POOL, 0, 0, 1, ALU.add)
            t2(POOL, 2, 2, 3, ALU.subtract)
            yield
            absa = s5c[:, 0, sb_:sb_ + 1]
            if not is_s:
                for s_ in range(nseg):
                    c0 = s_ * L
                    ir, ii = s5k[:, 0, sb_, 0:1], s5k[:, 0, sb_, 1:2]
                    DVE(lambda e: e.tensor_tensor_scan(out=w[:, 4, c0:c0 + L], data0=absa.to_broadcast([128, L]), data1=w[:, 0, c0:c0 + L],
                                                       initial=ir, op0=ALU.mult, op1=ALU.add), R=[s5c_r, w_r[0], s5k_r], W=[w_r[4]])
                    DVE(lambda e: e.tensor_tensor_scan(out=w[:, 5, c0:c0 + L], data0=absa.to_broadcast([128, L]), data1=w[:, 2, c0:c0 + L],
                                                       initial=ii, op0=ALU.mult, op1=ALU.add), R=[s5c_r, w_r[2], s5k_r], W=[w_r[5]])
                    wre, wie = w[:, 4, c0 + L - 1:c0 + L], w[:, 5, c0 + L - 1:c0 + L]
                    DVE(lambda e: e.tensor_scalar(out=tmpc[:, 0, 0:2], in0=s5k[:, 2, sb_, :], scalar1=wie, scalar2=None, op0=ALU.mult),
                        R=[w_r[5], s5k_r], W=[tmpc_r])
                    DVE(lambda e: e.scalar_tensor_tensor(out=s5k[:, 0, sb_, :], in0=s5k[:, 1, sb_, :], scalar=wre, in1=tmpc[:, 0, 0:2],
                                                         op0=ALU.mult, op1=ALU.add), R=[w_r[4], s5k_r, tmpc_r], W=[s5k_r])
                    yield
            else:
                d0, d0_i = w[:, 1, :ntok], 1
                DVE(lambda e: e.tensor_scalar(out=d0, in0=cstb[:, CB["rst_s"]:CB["rst_s"] + 128], scalar1=absa, scalar2=None, op0=ALU.mult),
                    R=[cstb_r, s5c_r], W=[w_r[1]])
                for ci_, wi_ in ((0, 0), (1, 2)):
                    v0 = w[:, wi_, :ntok].rearrange("p (s l) -> p s l", s=16)[:, :, 0]
                    DVE(lambda e: e.scalar_tensor_tensor(out=v0, in0=h0[:, ci_, sb_, :], scalar=absa, in1=v0, op0=ALU.mult, op1=ALU.add),
                        R=[h0_r, s5c_r, w_r[wi_]], W=[w_r[wi_]])
                yield
                DVE(lambda e: e.tensor_tensor_scan(out=w[:, 4, :ntok], data0=d0, data1=w[:, 0, :ntok], initial=0.0, op0=ALU.mult, op1=ALU.add),
                    R=[w_r[1], w_r[0]], W=[w_r[4]])
                DVE(lambda e: e.tensor_tensor_scan(out=w[:, 5, :ntok], data0=d0, data1=w[:, 2, :ntok], initial=0.0, op0=ALU.mult, op1=ALU.add),
                    R=[w_r[1], w_r[2]], W=[w_r[5]])
                yield
                cc, ss = Ecs[:, 0, sb_, L - 1:L], Ecs[:, 1, sb_, L - 1:L]
                wre = w[:, 4, :ntok].rearrange("p (s l) -> p s l", s=16)[:, :, L - 1]
                wie = w[:, 5, :ntok].rearrange("p (s l) -> p s l", s=16)[:, :, L - 1]
                DVE(lambda e: e.tensor_scalar(out=tmpc[:, 0, :], in0=wie, scalar1=ss, scalar2=None, op0=ALU.mult), R=[w_r[5], Ecs_r], W=[tmpc_r])
                DVE(lambda e: e.tensor_scalar(out=tmpc[:, 1, :], in0=wie, scalar1=cc, scalar2=None, op0=ALU.mult), R=[w_r[5], Ecs_r], W=[tmpc_r])
                DVE(lambda e: e.scalar_tensor_tensor(out=s5o[:, 0, sb_, 1:17], in0=wre, scalar=cc, in1=tmpc[:, 0, :],
                                                     op0=ALU.mult, op1=ALU.subtract), R=[w_r[4], Ecs_r, tmpc_r], W=[s5o_r])
                DVE(lambda e: e.scalar_tensor_tensor(out=s5o[:, 1, sb_, 1:17], in0=wre, scalar=ss, in1=tmpc[:, 1, :],
                                                     op0=ALU.mult, op1=ALU.add), R=[w_r[4], Ecs_r, tmpc_r], W=[s5o_r])
                yield
            tt(POOL, W_(0), 0, W_(4), w_r[4], Ec, ALU.mult)
            tt(DVE, W_(1), 1, W_(5), w_r[5], Es, ALU.mult)
            tt(POOL, W_(2), 2, W_(4), w_r[4], Es, ALU.mult)
            tt(DVE, W_(3), 3, W_(5), w_r[5], Ec, ALU.mult)
            yield
            POOL(lambda e: e.tensor_tensor(out=XR[:, 0, sbq, :ntok], in0=W_(0), in1=W_(1), op=ALU.subtract), R=[w_r[0], w_r[1]], W=[XR_r])
            POOL(lambda e: e.tensor_tensor(out=XR[:, 1, sbq, :ntok], in0=W_(2), in1=W_(3), op=ALU.add), R=[w_r[2], w_r[3]], W=[XR_r])
            yield

        def y_mm(q):
            ps, pres = psD512()
            for sbq in range(4):
                for ri in range(2):
                    PE(lambda e: e.matmul(ps[:, :ntok], lhsT=lhsTC[:, ri, 4 * q + sbq, :], rhs=XR[:, ri, sbq, :ntok],
                                          start=(sbq == 0 and ri == 0), stop=(sbq == 3 and ri == 1)), R=[lhsTC_r, XR_r], W=[pres])
            return ps, pres

        def y_tail(q, ps, pres):
            Y0, Y1 = yt[:, 0, :ntok], yt[:, 1, :ntok]
            DVE(lambda e: e.scalar_tensor_tensor(out=Y0, in0=uaT[:, q, :ntok], scalar=s5c[:, 1, q:q + 1], in1=ps[:, :ntok],
                                                 op0=ALU.mult, op1=ALU.add), R=[uaT_r, s5c_r, pres], W=[yt_r[0]])
            yield
            ACT(lambda e: e.activation(out=Y1, in_=Y0, func=AF.Square), R=[yt_r[0]], W=[yt_r[1]])
            yield
            DVE(lambda e: e.tensor_scalar(out=Y1, in0=Y1, scalar1=0.044715, scalar2=1.0, op0=ALU.mult, op1=ALU.add), R=[yt_r[1]], W=[yt_r[1]])
            DVE(lambda e: e.tensor_tensor(out=Y1, in0=Y1, in1=Y0, op=ALU.mult), R=[yt_r[1], yt_r[0]], W=[yt_r[1]])
            yield
            ACT(lambda e: e.activation(out=Y1, in_=Y1, func=AF.Sigmoid, scale=1.5957691216), R=[yt_r[1]], W=[yt_r[1]])
            yield
            DVE(lambda e: e.tensor_tensor(out=ygT[:, q, :ntok], in0=Y0, in1=Y1, op=ALU.mult), R=[yt_r[0], yt_r[1]], W=[ygT_r])
            yield

        tail = []
        for q in range(4):
            run_gens([sb_gen(4 * q, 0), sb_gen(4 * q + 1, 1)] + tail)
            tail = []
            run_gens([sb_gen(4 * q + 2, 0), sb_gen(4 * q + 3, 1)])
            ps, pres = y_mm(q)
            tail = [y_tail(q, ps, pres)]
        run_gens(tail)
        if tb == 3:
            DVE(lambda e: e.tensor_copy(out=s5o[:, :, :, 0], in_=s5k[:, 0].rearrange("p s c -> p c s")), R=[s5k_r], W=[s5o_r])

        def cb_glu(ps, pres, col, m):
            c = col // 128
            ACT(lambda e: e.activation(out=yt[:, 0, :ntok], in_=ps[:, :ntok], func=AF.Sigmoid), R=[pres], W=[yt_r[0]])
            DVE(lambda e: e.tensor_tensor(out=catT[:, c, :ntok], in0=ygT[:, c, :ntok], in1=yt[:, 0, :ntok], op=ALU.mult),
                R=[ygT_r, yt_r[0]], W=[catT_r])
        linear_fm(I["W_glu"], 512, 0, 512, [(ygT[:, c, :], ygT_r) for c in range(4)], ntok, cb_glu)

    def layer0(tb, tok0, ntok, is_s):
        arena.reset()
        nseg, L = (16, 8) if is_s else (4, 128)
        nt = ntok // 128
        sq_dummy = None
        qkvs, qkvs_r = arena.alloc([128, 12, 512], BF16)
        zs, zs_r = arena.alloc([128, 4, 512], BF16)
        catT, catT_r = arena.alloc([128, 8, 512], BF16)
        bd, bd_r = arena.alloc([8, 3, 512], F32)
        mark = arena.off
        XR, XR_r = arena.alloc([128, 2, 4, 512], BF16)
        rmsnorm(0, ntok, scratch=(XR.rearrange("p a b c -> p (a b) c"), XR_r))
        uaT, uaT_r = arena.alloc([128, 4, 512], BF16)
        hact = [(hT[:, c, :], hT_r) for c in range(8)]

        def cb_ua(ps, pres, col, m):
            ACT(lambda e: e.activation(out=uaT[:, col // 128, :ntok], in_=ps[:, :ntok], func=AF.Copy), R=[pres], W=[uaT_r])
        linear_fm(I["W_in"], D, 0, 512, hact, ntok, cb_ua)
        s5_tb(tb, ntok, is_s, uaT, uaT_r, catT, catT_r, XR, XR_r)
        kb.barrier()
        arena.off = mark
        nsc, Lc_ = (16, 8) if is_s else (1, 512)
        pre, pre_r = arena.alloc([128, 2, 528], F32, nres=2)
        cacc, cacc_r = arena.alloc([128, 512], F32)
        if is_s:
            c0t, c0t_r = arena.alloc([128, 12, 16, 3], F32)
            kb.dma(kb.qs, c0t.rearrange("p a b c -> p (a b c)"), I["gconv0"], W=[c0t_r])
        mark_front = arena.off
        extra_sl = [arena.alloc([128, 4096], BF16) for _ in range(4)]
        set_pool(list(wsl) + extra_sl)
        cnt = [0]

        def cb_qkv(ps, pres, col, m):
            c = (col - 512) // 128
            i = cnt[0] % 2
            cnt[0] += 1
            pv = pre[:, i, :nsc * (3 + Lc_)].rearrange("p (s l) -> p s l", s=nsc)
            if is_s:
                POOL(lambda e: e.tensor_copy(out=pv[:, :, 0:3], in_=c0t[:, c, :, :]), R=[c0t_r], W=[pre_r[i]])
            else:
                POOL(lambda e: e.tensor_copy(out=pv[:, 0, 0:3], in_=hist[:, c, :]), R=[hist_r], W=[pre_r[i]])
            ACT(lambda e: e.activation(out=pv[:, :, 3:3 + Lc_], in_=ps[:, :ntok].rearrange("p (s l) -> p s l", s=nsc), func=AF.Copy),
                R=[pres], W=[pre_r[i]])
            if is_s:
                POOL(lambda e: e.tensor_copy(out=oconv[:, c, 1:17, :], in_=pv[:, :, Lc_:Lc_ + 3]), R=[pre_r[i]], W=[oconv_r])
            else:
                POOL(lambda e: e.tensor_copy(out=hist[:, c, :], in_=pv[:, 0, Lc_:Lc_ + 3]), R=[pre_r[i]], W=[hist_r])
                if tb == 3:
                    POOL(lambda e: e.tensor_copy(out=oconv[:, c, 0, :], in_=pv[:, 0, Lc_:Lc_ + 3]), R=[pre_r[i]], W=[oconv_r])
            av = cacc[:, :ntok].rearrange("p (s l) -> p s l", s=nsc)
            DVE(lambda e: e.tensor_scalar(out=av, in0=pv[:, :, 0:Lc_], scalar1=gp[:, c * 4:c * 4 + 1], scalar2=None, op0=ALU.mult),
                R=[pre_r[i], gp_r], W=[cacc_r])
            for r in range(1, 4):
                DVE(lambda e: e.scalar_tensor_tensor(out=av, in0=pv[:, :, r:r + Lc_], scalar=gp[:, c * 4 + r:c * 4 + r + 1], in1=av,
                                                     op0=ALU.mult, op1=ALU.add), R=[pre_r[i], gp_r, cacc_r], W=[cacc_r])
            ACT(lambda e: e.activation(out=qkvs[:, c, :ntok], in_=cacc[:, :ntok], func=AF.Silu), R=[cacc_r], W=[qkvs_r])
        linear_fm(I["W_in"], D, 512, 1536, hact, ntok, cb_qkv)

        def cb_bd(ps, pres, col, m):
            ACT(lambda e: e.activation(out=bd[:, 0, :ntok], in_=ps[:8, :ntok], func=AF.Sigmoid), R=[pres], W=[bd_r])
            ACT(lambda e: e.activation(out=bd[:, 2, :ntok], in_=ps[:8, :ntok], func=AF.Exp, bias=gp[:8, 50:51], scale=1.0), R=[pres, gp_r], W=[bd_r])
            ACT(lambda e: e.activation(out=bd[:, 2, :ntok], in_=bd[:, 2, :ntok], func=AF.Ln, bias=kc[:8, 2:3], scale=1.0), R=[bd_r, kc_r], W=[bd_r])
            DVE(lambda e: e.tensor_scalar(out=bd[:, 1, :ntok], in0=bd[:, 2, :ntok], scalar1=negA8[:, 0:1], scalar2=None, op0=ALU.mult),
                R=[bd_r, negA8_r], W=[bd_r])
        linear_fm(I["W_in"], D, 2048, 8, hact, ntok, cb_bd)

        def cb_z(ps, pres, col, m):
            ACT(lambda e: e.activation(out=zs[:, (col - 2056) // 128, :ntok], in_=ps[:, :ntok], func=AF.Silu), R=[pres], W=[zs_r])
        linear_fm(I["W_in"], D, 2056, 512, hact, ntok, cb_z)
        kb.barrier()
        set_pool(wsl)
        arena.off = mark_front

        kTn, kTn_r = arena.alloc([128, 4, 512], BF16)
        qTn, qTn_r = arena.alloc([128, 4, 512], BF16)
        tb16, tb16_r = arena.alloc([128, 512], BF16)
        NH = 1 if is_s else 2
        hp_ = [dict() for _ in range(NH)]
        for s_ in hp_:
            s_["f"], s_["f_r"] = arena.alloc([128, 3, 512], F32, nres=3)
            s_["b"], s_["b_r"] = arena.alloc([128, 5, 512], BF16, nres=5)
            s_["e"], s_["e_r"] = arena.alloc([128, 2, 16], F32)
            s_["o"], s_["o_r"] = arena.alloc([128, 512], F32)
            s_["d"], s_["d_r"] = arena.alloc([128, 3, 128], F32, nres=3)
            s_["c"], s_["c_r"] = arena.alloc([128, 4], F32)
        if is_s:
            S0f, S0f_r = arena.alloc([128, 2048], F32)
            S0b, _ = arena.alloc([128, 2048], BF16)
        for h in range(4):
            ACT(lambda e: e.activation(out=tb16[:, :ntok], in_=qkvs[:, 4 + h, :ntok], func=AF.Square), R=[qkvs_r], W=[tb16_r])
            ps, pres = psA()
            PE(lambda e: e.matmul(ps[:, :ntok], lhsT=onesb, rhs=tb16[:, :ntok], start=True, stop=True), R=[cstb_r, tb16_r], W=[pres])
            rsqrt_into(rstd[:, :ntok], rstd_r, ps[:, :ntok], pres, 1.0, 1, ntok)
            DVE(lambda e: e.tensor_tensor(out=kTn[:, h, :ntok], in0=qkvs[:, 4 + h, :ntok], in1=rstd[:, :ntok], op=ALU.mult),
                R=[qkvs_r, rstd_r], W=[kTn_r])
            ACT(lambda e: e.activation(out=tb16[:, :ntok], in_=qkvs[:, h, :ntok], func=AF.Square), R=[qkvs_r], W=[tb16_r])
            ps, pres = psA()
            PE(lambda e: e.matmul(ps[:, :ntok], lhsT=onesb, rhs=tb16[:, :ntok], start=True, stop=True), R=[cstb_r, tb16_r], W=[pres])
            rsqrt_into(rstd[:, :ntok], rstd_r, ps[:, :ntok], pres, 1.0, 1, ntok)
            DVE(lambda e: e.scalar_tensor_tensor(out=qTn[:, h, :ntok], in0=qkvs[:, h, :ntok], scalar=128.0 ** -0.5, in1=rstd[:, :ntok],
                                                 op0=ALU.mult, op1=ALU.mult), R=[qkvs_r, rstd_r], W=[qTn_r])
        rst = cstb[:, CB["rst_s"]:CB["rst_s"] + 128] if is_s else cstb[:, CB["rst_p"]:CB["rst_p"] + 512]
        MsT = cb_("MsT_s") if is_s else cb_("MsT_p")
        NEG = cb_("NEG_s") if is_s else cb_("NEG_p")

        def prep_head(h, s_):
            f, f_r, b, b_r = s_["f"], s_["f_r"], s_["b"], s_["b_r"]
            psb, psb_r = psA()
            PE(lambda e: e.matmul(psb[:, :ntok], lhsT=sel8[:8, h * 128:(h + 1) * 128], rhs=bd[:, 0, :ntok], start=True, stop=True),
               R=[sel8_r, bd_r], W=[psb_r])
            ACT(lambda e: e.activation(out=f[:, 0, :ntok], in_=psb[:, :ntok], func=AF.Copy), R=[psb_r], W=[f_r[0]])
            psg, psg_r = psA()
            PE(lambda e: e.matmul(psg[:, :ntok], lhsT=sel8[:8, (4 + h) * 128:(5 + h) * 128], rhs=bd[:, 1, :ntok], start=True, stop=True),
               R=[sel8_r, bd_r], W=[psg_r])
            DVE(lambda e: e.tensor_tensor_scan(out=f[:, 1, :ntok], data0=rst[:, :ntok], data1=psg[:, :ntok], initial=0.0,
                                               op0=ALU.mult, op1=ALU.add), R=[cstb_r, psg_r], W=[f_r[1]])
            ACT(lambda e: e.activation(out=f[:, 2, :ntok], in_=f[:, 1, :ntok], func=AF.Exp), R=[f_r[1]], W=[f_r[2]])
            DVE(lambda e: e.tensor_tensor(out=b[:, 0, :ntok], in0=kTn[:, h, :ntok], in1=f[:, 2, :ntok], op=ALU.mult), R=[kTn_r, f_r[2]], W=[b_r[0]])
            DVE(lambda e: e.tensor_tensor(out=b[:, 1, :ntok], in0=qTn[:, h, :ntok], in1=f[:, 2, :ntok], op=ALU.mult), R=[qTn_r, f_r[2]], W=[b_r[0]])
            Lcv = f[:, 1, :ntok].rearrange("p (s l) -> p s l", s=nseg)
            DVE(lambda e: e.tensor_copy(out=s_["e"][:, 0, :nseg], in_=Lcv[:, :, L - 1]), R=[f_r[1]], W=[s_["e_r"]])
            ACT(lambda e: e.activation(out=s_["e"][:, 1, :nseg], in_=s_["e"][:, 0, :nseg], func=AF.Exp), R=[s_["e_r"]], W=[s_["e_r"]])
            DVE(lambda e: e.tensor_tensor(out=f[:, 2, :ntok].rearrange("p (s l) -> p s l", s=nseg),
                                          in0=s_["e"][:, 0, :nseg].unsqueeze(2).to_broadcast([128, nseg, L]), in1=Lcv, op=ALU.subtract),
                R=[s_["e_r"], f_r[1]], W=[f_r[2]])
            ACT(lambda e: e.activation(out=f[:, 2, :ntok], in_=f[:, 2, :ntok], func=AF.Exp), R=[f_r[2]], W=[f_r[2]])
            DVE(lambda e: e.tensor_tensor(out=b[:, 2, :ntok], in0=kTn[:, h, :ntok], in1=f[:, 2, :ntok], op=ALU.mult), R=[kTn_r, f_r[2]], W=[b_r[2]])
            DVE(lambda e: e.scalar_tensor_tensor(out=b[:, 3, :ntok], in0=b[:, 2, :ntok], scalar=-1.0, in1=f[:, 0, :ntok],
                                                 op0=ALU.mult, op1=ALU.mult), R=[b_r[2], f_r[0]], W=[b_r[3]])
            DVE(lambda e: e.tensor_tensor(out=b[:, 4, :ntok], in0=qkvs[:, 8 + h, :ntok], in1=f[:, 0, :ntok], op=ALU.mult), R=[qkvs_r, f_r[0]], W=[b_r[4]])

        def gdn_run(hs, Ss):
            def emit_XT(ti, hi, xt, xt_r, mats):
                h, s_ = hs[hi], hp_[hi]
                f, f_r, d, d_r = s_["f"], s_["f_r"], s_["d"], s_["d_r"]
                cs = slice(ti * 128, (ti + 1) * 128)
                DVE(lambda e: e.scalar_tensor_tensor(out=d[:, 0, :], in0=f[:, 1, cs], scalar=-1.0, in1=ident, op0=ALU.mult, op1=ALU.mult,
                                                     accum_out=s_["c"][:, 0:1]), R=[f_r[1], cstf_r], W=[d_r[0], s_["c_r"]])
                DVE(lambda e: e.scalar_tensor_tensor(out=d[:, 0, :], in0=f[:, 0, cs], scalar=-1.0, in1=ident, op0=ALU.mult, op1=ALU.mult,
                                                     accum_out=s_["c"][:, 1:2]), R=[f_r[0], cstf_r, d_r[0]], W=[d_r[0], s_["c_r"]])
                DVE(lambda e: e.tensor_tensor(out=d[:, 0, :], in0=f[:, 1, cs], in1=NEG, op=ALU.add), R=[f_r[1], cstb_r], W=[d_r[0]])
                ACT(lambda e: e.activation(out=d[:, 1, :], in_=d[:, 0, :], func=AF.Exp, bias=s_["c"][:, 0:1], scale=1.0),
                    R=[d_r[0], s_["c_r"]], W=[d_r[1]])
                DVE(lambda e: e.tensor_tensor(out=d[:, 2, :], in0=d[:, 1, :], in1=MsT, op=ALU.mult), R=[d_r[1], cstb_r], W=[d_r[2]])
                p1, p1_r = psD()
                PE(lambda e: e.matmul(p1, lhsT=kTn[:, h, cs], rhs=kTn[:, h, cs], start=True, stop=True), R=[kTn_r], W=[p1_r])
                DVE(lambda e: e.scalar_tensor_tensor(out=xt, in0=p1, scalar=s_["c"][:, 1:2], in1=d[:, 2, :], op0=ALU.mult, op1=ALU.mult),
                    R=[p1_r, s_["c_r"], d_r[2]], W=[xt_r])
                DVE(lambda e: e.tensor_tensor(out=mats[0][0], in0=p1, in1=d[:, 2, :], op=ALU.mult), R=[p1_r, d_r[2]], W=[mats[0][1]])
                p2, p2_r = psD()
                PE(lambda e: e.matmul(p2, lhsT=kTn[:, h, cs], rhs=qTn[:, h, cs], start=True, stop=True), R=[kTn_r, qTn_r], W=[p2_r])
                DVE(lambda e: e.scalar_tensor_tensor(out=mats[1][0], in0=p2, scalar=s_["c"][:, 1:2], in1=d[:, 1, :], op0=ALU.mult, op1=ALU.mult),
                    R=[p2_r, s_["c_r"], d_r[1]], W=[mats[1][1]])
                DVE(lambda e: e.tensor_tensor(out=mats[2][0], in0=p2, in1=d[:, 1, :], op=ALU.mult), R=[p2_r, d_r[1]], W=[mats[2][1]])

            def emit_tm(ti):
                cs = slice(ti * 128, (ti + 1) * 128)
                out = []
                for hi in range(len(hs)):
                    b, b_r = hp_[hi]["b"], hp_[hi]["b_r"]
                    i0 = ((ti % 2) * 2 + hi) * 3
                    pt, pt_r = psT_full()
                    for k, src in enumerate((4, 3, 2)):
                        PE(lambda e: e.transpose(pt[:, k * 128:(k + 1) * 128], b[:, src, cs], identb[:]), R=[b_r[src], identb_r], W=[pt_r])
                    ACT(lambda e: e.activation(out=tmB[:, i0:i0 + 3, :], in_=pt[:, 0:384].rearrange("p (k b) -> p k b", k=3), func=AF.Copy),
                        R=[pt_r], W=[tmB_r[i0], tmB_r[i0 + 1], tmB_r[i0 + 2]])
                    out.append([(tmB[:, i0 + k, :], tmB_r[i0 + k]) for k in range(3)])
                return out

            def ops(ti, hi):
                b, b_r = hp_[hi]["b"], hp_[hi]["b_r"]
                cs = slice(ti * 128, (ti + 1) * 128)
                return b[:, 0, cs], b[:, 1, cs], b_r[0]

            def eT_of(ti, hi):
                s_ = hp_[hi]
                return (s_["e"][:, 1, ti:ti + 1] if not is_s else s_["e"][:, 1, :16]), s_["e_r"]

            def ycb_of(ti, hi):
                s_ = hp_[hi]
                cs = slice(ti * 128, (ti + 1) * 128)

                def ycb(yo, pres):
                    ACT(lambda e: e.activation(out=s_["o"][:, cs], in_=yo, func=AF.Copy), R=[pres], W=[s_["o_r"]])
                return ycb
            dplr_run(nt, [0] * len(hs), 128, 128, is_s, emit_XT, emit_tm, ops, lambda hi: Ss[hi], eT_of, ycb_of)

        def head_finish(h, s_):
            f, f_r = s_["f"], s_["f_r"]
            ACT(lambda e: e.activation(out=tb16[:, :ntok], in_=s_["o"][:, :ntok], func=AF.Square), R=[s_["o_r"]], W=[tb16_r])
            ps, pres = psA()
            PE(lambda e: e.matmul(ps[:, :ntok], lhsT=onesb, rhs=tb16[:, :ntok], start=True, stop=True), R=[cstb_r, tb16_r], W=[pres])
            rsqrt_into(rstd[:, :ntok], rstd_r, ps[:, :ntok], pres, 1.0 / 128, 1, ntok)
            DVE(lambda e: e.scalar_tensor_tensor(out=f[:, 2, :ntok], in0=s_["o"][:, :ntok], scalar=gp[:, 48:49], in1=rstd[:, :ntok],
                                                 op0=ALU.mult, op1=ALU.mult), R=[s_["o_r"], gp_r, rstd_r], W=[f_r[2]])
            DVE(lambda e: e.tensor_tensor(out=catT[:, 4 + h, :ntok], in0=f[:, 2, :ntok], in1=zs[:, h, :ntok], op=ALU.mult),
                R=[f_r[2], zs_r], W=[catT_r])

        for h0_ in range(0, 4, NH):
            hs = list(range(h0_, h0_ + NH))
            Ss = []
            for i, h in enumerate(hs):
                prep_head(h, hp_[i])
                if is_s:
                    kb.dma(kb.qs, S0f, I["gs0"][h], W=[S0f_r])
                    kb.dma(kb.qg, S0b, I["gs0"][h], W=[S0f_r])
                    Ss.append(dict(f=S0f, b=S0b, r=S0f_r))
                else:
                    Ss.append(dict(f=Sg_f[:, h, :], b=Sg_b[:, h, :], r=Sg_r[h]))
            gdn_run(hs, Ss)
            for i, h in enumerate(hs):
                head_finish(h, hp_[i])
                if is_s:
                    kb.dma(kb.qs, O["o_gdn_s"][h], S0f, R=[S0f_r])
                elif tb == 3:
                    kb.dma(kb.qs, O["o_gdn_p"][h], Sg_f[:, h, :], R=[Sg_r[h]])
        linear_fm(I["W_out"], D, 0, D, [(catT[:, c, :], catT_r) for c in range(8)], ntok, resid_add_cb(ntok))

    rwp, rwp_r = kb.sb("rwp", [128, 104], F32)
    kb.dma(kb.qs, rwp[:], I["rwp"], W=[rwp_r])
    hprev, hprev_r = kb.sb("hprev", [128, 8, 1], F32)
    DVE(lambda e: e.memset(hprev[:], 0.0), W=[hprev_r])
    oshift, oshift_r = kb.sb("oshift", [128, 8, 17], F32)
    Sr_f, _ = kb.sb("Sr_f", [128, 8, 64], F32)
    Sr_b, _ = kb.sb("Sr_b", [128, 8, 64], BF16)
    Sr_r = [Res() for _ in range(8)]
    DVE(lambda e: e.memset(Sr_f[:], 0.0), W=Sr_r)
    DVE(lambda e: e.memset(Sr_b[:], 0.0), W=Sr_r)

    def layer1(tb, tok0, ntok, is_s):
        arena.reset()
        nseg, L = (16, 8) if is_s else (4, 128)
        nsc, Lc_ = (16, 8) if is_s else (1, 512)
        nt = ntok // 128
        mark1 = arena.off
        rmsnorm(2, ntok)
        kb.barrier()
        arena.off = mark1
        mix, mix_r = arena.alloc([128, 3, 8, 512], BF16, nres=3)
        lora, lora_r = arena.alloc([128, 4, 512], BF16)
        ygT, ygT_r = arena.alloc([128, 8, 512], BF16)
        def mixes(js):
            for c in range(8):
                i = c % 2
                hv = hF[:, i, :nsc * (1 + Lc_)].rearrange("p (s l) -> p s l", s=nsc)
                DVE(lambda e: e.scalar_tensor_tensor(out=hv[:, :, 1:1 + Lc_], in0=xT[:, c, :ntok].rearrange("p (s l) -> p s l", s=nsc),
                                                     scalar=nw[:, 16 + c:17 + c], in1=rstd[:, :ntok].rearrange("p (s l) -> p s l", s=nsc),
                                                     op0=ALU.mult, op1=ALU.mult), R=[xT_r, nw_r, rstd_r], W=[hF_r[i]])
                if is_s:
                    POOL(lambda e: e.tensor_copy(out=hv[:, :, 0], in_=sh0[:, c, :]), R=[sh0_r], W=[hF_r[i]])
                else:
                    POOL(lambda e: e.tensor_copy(out=hv[:, 0, 0:1], in_=hprev[:, c, :]), R=[hprev_r], W=[hF_r[i]])
                if js[0] == 0:
                    if is_s:
                        POOL(lambda e: e.tensor_copy(out=oshift[:, c, 1:17], in_=hv[:, :, Lc_]), R=[hF_r[i]], W=[oshift_r])
                    else:
                        POOL(lambda e: e.tensor_copy(out=hprev[:, c, :], in_=hv[:, 0, Lc_:Lc_ + 1]), R=[hF_r[i]], W=[hprev_r])
                        if tb == 3:
                            POOL(lambda e: e.tensor_copy(out=oshift[:, c, 0:1], in_=hv[:, 0, Lc_:Lc_ + 1]), R=[hF_r[i]], W=[oshift_r])
                xv = xx[:, i, :ntok].rearrange("p (s l) -> p s l", s=nsc)
                DVE(lambda e: e.tensor_tensor(out=xv, in0=hv[:, :, 0:Lc_], in1=hv[:, :, 1:1 + Lc_], op=ALU.subtract), R=[hF_r[i]], W=[xx_r[i]])
                for k_, j in enumerate(js):
                    DVE(lambda e: e.scalar_tensor_tensor(out=mix[:, k_, c, :ntok].rearrange("p (s l) -> p s l", s=nsc), in0=xv,
                                                         scalar=rwp[:, 56 + j * 8 + c:57 + j * 8 + c], in1=hv[:, :, 1:1 + Lc_],
                                                         op0=ALU.mult, op1=ALU.add), R=[xx_r[i], rwp_r, hF_r[i]], W=[mix_r[k_]])
        if is_s:
            sh0, sh0_r = arena.alloc([128, 8, 16], F32)
            kb.dma(kb.qs, sh0.rearrange("p a b -> p (a b)"), I["rshift0"], W=[sh0_r])
        mark2 = arena.off
        hF, hF_r = arena.alloc([128, 2, 520], F32, nres=2)
        xx, xx_r = arena.alloc([128, 2, 512], F32, nres=2)
        mixes([1, 4, 5])
        act_of = lambda k_: [(mix[:, k_, c, :], mix_r[k_]) for c in range(8)]

        def cb_hw(ps, pres, col, m):
            ACT(lambda e: e.activation(out=lora[:64, 0, :ntok], in_=ps[:64, :ntok], func=AF.Tanh), R=[pres], W=[lora_r])
        linear_fm(I["w1"], D, 0, 64, act_of(0), ntok, cb_hw)

        def cb_ha(ps, pres, col, m):
            ACT(lambda e: e.activation(out=lora[:64, 1, :ntok], in_=ps[:64, :ntok], func=AF.Copy), R=[pres], W=[lora_r])
        linear_fm(I["a1"], D, 0, 64, act_of(1), ntok, cb_ha)

        def cb_hg(ps, pres, col, m):
            ACT(lambda e: e.activation(out=lora[:m, 2 + col // 128, :ntok], in_=ps[:m, :ntok], func=AF.Sigmoid), R=[pres], W=[lora_r])
        linear_fm(I["g1"], D, 0, 160, act_of(2), ntok, cb_hg)
        mixes([0, 2, 3])
        kb.barrier()
        arena.off = mark2

        P_, P_r = arena.alloc([128, 10, 512], F32, nres=10)
        NL = 1 if is_s else 2
        Ls = []
        for _i in range(NL):
            d_ = {}
            d_["Bq"], d_["Bq_r"] = arena.alloc([128, 7, 512], BF16, nres=7)
            d_["yF"], d_["yF_r"] = arena.alloc([128, 512], F32)
            d_["rg"], d_["rg_r"] = arena.alloc([128, 2, 512], BF16, nres=2)
            d_["eL"], d_["eL_r"] = arena.alloc([128, 2, 16], F32)
            Ls.append(d_)
        if is_s:
            S0f, S0f_r = arena.alloc([128, 1024], F32)
            S0b, _ = arena.alloc([128, 1024], BF16)
        rst = cstb[:, CB["rst_s"]:CB["rst_s"] + 128] if is_s else cstb[:, CB["rst_p"]:CB["rst_p"] + 512]
        MsT = cb_("MsT_s") if is_s else cb_("MsT_p")
        MiT = cb_("MiT_s") if is_s else cb_("MiT_p")
        PV = lambda i: P_[:, i, :ntok]

        def prep_gen(hp, Lh):
            Bq, Bq_r, eL, eL_r, rg, rg_r = Lh["Bq"], Lh["Bq_r"], Lh["eL"], Lh["eL_r"], Lh["rg"], Lh["rg_r"]
            col = lambda i: rwp[:, hp * 7 + i:hp * 7 + i + 1]

            def cp(i):
                def cb(ps, pres, c_, m):
                    ACT(lambda e: e.activation(out=PV(i), in_=ps[:, :ntok], func=AF.Copy), R=[pres], W=[P_r[i]])
                return cb
            linear_fm(I["W_r"], D, hp * 128, 128, act_of(0), ntok, cp(0))
            yield
            linear_fm(I["W_k"], D, hp * 128, 128, act_of(1), ntok, cp(1))
            yield
            linear_fm(I["W_v"], D, hp * 128, 128, act_of(2), ntok, cp(2))
            yield

            def cb_w(ps, pres, c_, m):
                ACT(lambda e: e.activation(out=PV(3), in_=ps[:, :ntok], func=AF.Sigmoid, bias=col(0), scale=1.0), R=[pres, rwp_r], W=[P_r[3]])
                DVE(lambda e: e.tensor_scalar(out=PV(3), in0=PV(3), scalar1=-EXPM05, scalar2=None, op0=ALU.mult), R=[P_r[3]], W=[P_r[3]])
            linear_fm(I["w2"], 64, hp * 128, 128, [(lora[:, 0, :], lora_r)], ntok, cb_w)

            def cb_a(ps, pres, c_, m):
                ACT(lambda e: e.activation(out=PV(4), in_=ps[:, :ntok], func=AF.Sigmoid, bias=col(1), scale=1.0), R=[pres, rwp_r], W=[P_r[4]])
            linear_fm(I["a2"], 64, hp * 128, 128, [(lora[:, 1, :], lora_r)], ntok, cb_a)
            yield

            def cb_g(ps, pres, c_, m):
                ACT(lambda e: e.activation(out=rg[:, 1, :ntok], in_=ps[:, :ntok], func=AF.Copy), R=[pres], W=[rg_r[1]])
            linear_fm(I["g2"], 160, hp * 128, 128, [(lora[:, 2, :], lora_r), (lora[:, 3, :], lora_r)], ntok, cb_g)
            yield
            DVE(lambda e: e.tensor_scalar(out=PV(6), in0=PV(1), scalar1=col(2), scalar2=None, op0=ALU.mult), R=[P_r[1], rwp_r], W=[P_r[6]])
            ACT(lambda e: e.activation(out=PV(9), in_=PV(6), func=AF.Square), R=[P_r[6]], W=[P_r[9]])
            yield
            ps, pres = psA()
            PE(lambda e: e.matmul(ps[:, :ntok], lhsT=blk64, rhs=PV(9), start=True, stop=True), R=[cstf_r, P_r[9]], W=[pres])
            rsqrt_into(PV(9), P_r[9], ps[:, :ntok], pres, 1.0, 1, ntok)
            DVE(lambda e: e.tensor_scalar(out=PV(7), in0=PV(4), scalar1=-1.0, scalar2=col(3), op0=ALU.add, op1=ALU.mult), R=[P_r[4], rwp_r], W=[P_r[7]])
            DVE(lambda e: e.scalar_tensor_tensor(out=PV(7), in0=PV(7), scalar=1.0, in1=PV(1), op0=ALU.add, op1=ALU.mult), R=[P_r[7], P_r[1]], W=[P_r[7]])
            DVE(lambda e: e.scalar_tensor_tensor(out=rg[:, 0, :ntok], in0=PV(0), scalar=col(6), in1=PV(7), op0=ALU.mult, op1=ALU.mult),
                R=[P_r[0], rwp_r, P_r[7]], W=[rg_r[0]])
            yield
            DVE(lambda e: e.tensor_tensor_scan(out=PV(8), data0=rst[:, :ntok], data1=PV(3), initial=0.0, op0=ALU.mult, op1=ALU.add),
                R=[cstb_r, P_r[3]], W=[P_r[8]])
            yield
            DVE(lambda e: e.tensor_tensor(out=PV(6), in0=PV(6), in1=PV(9), op=ALU.mult), R=[P_r[6], P_r[9]], W=[P_r[6]])
            DVE(lambda e: e.tensor_tensor(out=PV(9), in0=PV(8), in1=PV(3), op=ALU.subtract), R=[P_r[8], P_r[3]], W=[P_r[9]])
            yield
            ACT(lambda e: e.activation(out=PV(9), in_=PV(9), func=AF.Exp), R=[P_r[9]], W=[P_r[9]])
            DVE(lambda e: e.tensor_tensor(out=PV(5), in0=PV(6), in1=PV(4), op=ALU.mult), R=[P_r[6], P_r[4]], W=[P_r[5]])
            yield
            DVE(lambda e: e.scalar_tensor_tensor(out=Bq[:, 0, :ntok], in0=PV(6), scalar=-1.0, in1=PV(9), op0=ALU.mult, op1=ALU.mult),
                R=[P_r[6], P_r[9]], W=[Bq_r[0]])
            yield
            ACT(lambda e: e.activation(out=PV(9), in_=PV(8), func=AF.Exp), R=[P_r[8]], W=[P_r[9]])
            yield
            DVE(lambda e: e.tensor_tensor(out=Bq[:, 3, :ntok], in0=PV(0), in1=PV(9), op=ALU.mult), R=[P_r[0], P_r[9]], W=[Bq_r[3]])
            yield
            ACT(lambda e: e.activation(out=PV(9), in_=PV(8), func=AF.Exp, scale=-1.0), R=[P_r[8]], W=[P_r[9]])
            yield
            DVE(lambda e: e.tensor_tensor(out=Bq[:, 1, :ntok], in0=PV(5), in1=PV(9), op=ALU.mult), R=[P_r[5], P_r[9]], W=[Bq_r[1]])
            DVE(lambda e: e.tensor_tensor(out=Bq[:, 2, :ntok], in0=PV(7), in1=PV(9), op=ALU.mult), R=[P_r[7], P_r[9]], W=[Bq_r[2]])
            Lcv = PV(8).rearrange("p (s l) -> p s l", s=nseg)
            DVE(lambda e: e.tensor_copy(out=eL[:, 0, :nseg], in_=Lcv[:, :, L - 1]), R=[P_r[8]], W=[eL_r])
            yield
            ACT(lambda e: e.activation(out=eL[:, 1, :nseg], in_=eL[:, 0, :nseg], func=AF.Exp), R=[eL_r], W=[eL_r])
            DVE(lambda e: e.tensor_tensor(out=PV(9).rearrange("p (s l) -> p s l", s=nseg),
                                          in0=eL[:, 0, :nseg].unsqueeze(2).to_broadcast([128, nseg, L]), in1=Lcv, op=ALU.subtract),
                R=[eL_r, P_r[8]], W=[P_r[9]])
            yield
            ACT(lambda e: e.activation(out=PV(9), in_=PV(9), func=AF.Exp), R=[P_r[9]], W=[P_r[9]])
            ACT(lambda e: e.activation(out=Bq[:, 6, :ntok], in_=PV(2), func=AF.Copy), R=[P_r[2]], W=[Bq_r[6]])
            yield
            DVE(lambda e: e.tensor_tensor(out=Bq[:, 4, :ntok], in0=PV(5), in1=PV(9), op=ALU.mult), R=[P_r[5], P_r[9]], W=[Bq_r[4]])
            DVE(lambda e: e.tensor_tensor(out=Bq[:, 5, :ntok], in0=PV(7), in1=PV(9), op=ALU.mult), R=[P_r[7], P_r[9]], W=[Bq_r[5]])
            if is_s:
                kb.dma(kb.qs, S0f, I["rs0"][hp], W=[S0f_r])
                kb.dma(kb.qg, S0b, I["rs0"][hp], W=[S0f_r])
            yield

        def dplr_hp(hp, Lh, extra):
            Bq, Bq_r, eL, eL_r = Lh["Bq"], Lh["Bq_r"], Lh["eL"], Lh["eL_r"]

            def emit_XT(ti, hi, xt, xt_r, mats):
                cs = slice(ti * 128, (ti + 1) * 128)
                hsl = slice(64 * hi, 64 * hi + 64)

                p, p_r = psD512()
                for blk, (lh, rh) in enumerate(((1, 0), (2, 0), (1, 3), (2, 3))):
                    PE(lambda e: e.matmul(p[:, blk * 128:(blk + 1) * 128], lhsT=Bq[hsl, lh, cs], rhs=Bq[hsl, rh, cs], start=True, stop=True),
                       R=[Bq_r[lh], Bq_r[rh]], W=[p_r])
                k_ = (ti % 4) * 2 + hi
                mb = CB["MsT_s"] if is_s else CB["MsT_p"]
                DVE(lambda e: e.tensor_tensor(out=xt, in0=p[:, 0:128], in1=MsT, op=ALU.mult), R=[p_r, cstb_r], W=[xt_r])
                DVE(lambda e: e.tensor_tensor(out=matsB[:, k_ * 3:k_ * 3 + 2, :], in0=p[:, 128:384].rearrange("p (a b) -> p a b", a=2),
                                              in1=cstb[:, mb:mb + 256].rearrange("p (a b) -> p a b", a=2), op=ALU.mult),
                    R=[p_r, cstb_r], W=[mats[0][1], mats[1][1]])
                DVE(lambda e: e.tensor_tensor(out=mats[2][0], in0=p[:, 384:512], in1=MiT, op=ALU.mult), R=[p_r, cstb_r], W=[mats[2][1]])

            def emit_tm(ti):
                cs = slice(ti * 128, (ti + 1) * 128)
                i0 = ((ti % 2) * 2) * 3
                pt, pt_r = psT_full()
                for k, src in enumerate((6, 4, 5)):
                    PE(lambda e: e.transpose(pt[:, k * 128:(k + 1) * 128], Bq[:, src, cs], identb[:]), R=[Bq_r[src], identb_r], W=[pt_r])
                ACT(lambda e: e.activation(out=tmB[:, i0:i0 + 3, :], in_=pt[:, 0:384].rearrange("p (k b) -> p k b", k=3), func=AF.Copy),
                    R=[pt_r], W=[tmB_r[i0], tmB_r[i0 + 1], tmB_r[i0 + 2]])
                tm = [(tmB[:, i0 + k, :], tmB_r[i0 + k]) for k in range(3)]
                return [[(t_[:, 64 * hi:64 * hi + 64], r_) for t_, r_ in tm] for hi in range(2)]

            def ops(ti, hi):
                cs = slice(ti * 128, (ti + 1) * 128)
                hsl = slice(64 * hi, 64 * hi + 64)
                return Bq[hsl, 0, cs], Bq[hsl, 3, cs], Bq_r[0]

            def S_of(hi):
                hsl = slice(64 * hi, 64 * hi + 64)
                if is_s:
                    return dict(f=S0f[hsl, :], b=S0b[hsl, :], r=S0f_r)
                return dict(f=Sr_f[hsl, hp, :], b=Sr_b[hsl, hp, :], r=Sr_r[hp])

            def eT_of(ti, hi):
                hsl = slice(64 * hi, 64 * hi + 64)
                return (eL[hsl, 1, :16] if is_s else eL[hsl, 1, ti:ti + 1]), eL_r

            def ycb_of(ti, hi):
                cs = slice(ti * 128, (ti + 1) * 128)
                pb = 64 * hi

                def ycb(yo, pres):
                    ACT(lambda e: e.activation(out=Lh["yF"][pb:pb + 64, cs], in_=yo, func=AF.Copy), R=[pres], W=[Lh["yF_r"]])
                return ycb
            if extra is None:
                return dplr_run(nt, [0, 64], 64, 64, is_s, emit_XT, emit_tm, ops, S_of, eT_of, ycb_of, NEU_BF16=True, make_only=True)
            dplr_run(nt, [0, 64], 64, 64, is_s, emit_XT, emit_tm, ops, S_of, eT_of, ycb_of, NEU_BF16=True, extra=extra)
            if is_s:
                kb.dma(kb.qs, O["o_rw_s"][hp], S0f, R=[S0f_r])
            elif tb == 3:
                kb.dma(kb.qs, O["o_rw_p"][hp], Sr_f[:, hp, :], R=[Sr_r[hp]])

        def tail_gen(hp, Lh):
            col = lambda i: rwp[:, hp * 7 + i:hp * 7 + i + 1]
            yF, yF_r, rg, rg_r, Bq, Bq_r = Lh["yF"], Lh["yF_r"], Lh["rg"], Lh["rg_r"], Lh["Bq"], Lh["Bq_r"]
            Y = yF[:, :ntok]
            ps, pres = psA()
            PE(lambda e: e.matmul(ps[:, :ntok], lhsT=blk64, rhs=Y, start=True, stop=True), R=[cstf_r, yF_r], W=[pres])
            DVE(lambda e: e.scalar_tensor_tensor(out=PV(9), in0=ps[:, :ntok], scalar=-1.0 / 64, in1=Y, op0=ALU.mult, op1=ALU.add),
                R=[pres, yF_r], W=[P_r[9]])
            yield
            ACT(lambda e: e.activation(out=PV(5), in_=PV(9), func=AF.Square), R=[P_r[9]], W=[P_r[5]])
            yield
            ps, pres = psA()
            PE(lambda e: e.matmul(ps[:, :ntok], lhsT=blk64, rhs=PV(5), start=True, stop=True), R=[cstf_r, P_r[5]], W=[pres])
            rsqrt_into(PV(5), P_r[5], ps[:, :ntok], pres, 1.0 / 64, 3, ntok)
            yield
            DVE(lambda e: e.tensor_tensor(out=PV(9), in0=PV(9), in1=PV(5), op=ALU.mult), R=[P_r[9], P_r[5]], W=[P_r[9]])
            DVE(lambda e: e.tensor_scalar(out=PV(9), in0=PV(9), scalar1=col(4), scalar2=col(5), op0=ALU.mult, op1=ALU.add),
                R=[P_r[9], rwp_r], W=[P_r[9]])
            ps, pres = psA()
            PE(lambda e: e.matmul(ps[:, :ntok], lhsT=blk64b[:], rhs=rg[:, 0, :ntok], start=True, stop=True), R=[blk64b_r, rg_r[0]], W=[pres])
            DVE(lambda e: e.tensor_tensor(out=PV(5), in0=ps[:, :ntok], in1=Bq[:, 6, :ntok], op=ALU.mult), R=[pres, Bq_r[6]], W=[P_r[5]])
            DVE(lambda e: e.tensor_tensor(out=PV(9), in0=PV(9), in1=PV(5), op=ALU.add), R=[P_r[9], P_r[5]], W=[P_r[9]])
            DVE(lambda e: e.tensor_tensor(out=ygT[:, hp, :ntok], in0=PV(9), in1=rg[:, 1, :ntok], op=ALU.mult), R=[P_r[9], rg_r[1]], W=[ygT_r])
            yield

        def seq_gens(*gs):
            for g_ in gs:
                yield from g_

        if is_s:
            for hp in range(8):
                run_gens([prep_gen(hp, Ls[0])])
                dplr_hp(hp, Ls[0], [])
                run_gens([tail_gen(hp, Ls[0])])
        else:
            run_gens([prep_gen(0, Ls[0])])
            ctx = dplr_hp(0, Ls[0], None)
            P0, P1 = ctx["pairs"]
            run_gens(ctx["neu"](P0))
            for hp in range(8):
                parts = []
                if hp > 0:
                    parts.append(tail_gen(hp - 1, Ls[(hp - 1) % 2]))
                if hp < 7:
                    parts.append(prep_gen(hp + 1, Ls[(hp + 1) % 2]))
                ex = [seq_gens(*parts)] if parts else []
                run_gens(ctx["chain"](P0) + ctx["neu"](P1) + ex)
                run_gens(ex)
                gens = ctx["chain"](P1)
                nctx = None
                if hp < 7:
                    nctx = dplr_hp(hp + 1, Ls[(hp + 1) % 2], None)
                    gens += nctx["neu"](P0)
                run_gens(gens)
                if tb == 3:
                    kb.dma(kb.qs, O["o_rw_p"][hp], Sr_f[:, hp, :], R=[Sr_r[hp]])
                ctx = nctx
            run_gens([tail_gen(7, Ls[1])])
        linear_fm(I["W_o"], D, 0, D, [(ygT[:, c, :], ygT_r) for c in range(8)], ntok, resid_add_cb(ntok))

    TBS = [(0, 512, False), (512, 512, False), (1024, 512, False), (1536, 512, False), (2048, 128, True)]
    if stop_after is not None and stop_after.startswith("s:"):
        TBS = [TBS[4]]
    try:
      checkpoint("setup")
      for tb, (tok0, ntok, is_s) in enumerate(TBS):
        if len(TBS) == 1:
            tb = 4
        kb.dma(kb.qs, xT[:, :, :ntok], I["xT"][:, tok0:tok0 + ntok].rearrange("(c p) t -> p c t", p=128), W=[xT_r])
        layer0(tb, tok0, ntok, is_s)
        checkpoint("l0") ; checkpoint("s:l0")
        ffn(0, ntok)
        checkpoint("f0") ; checkpoint("s:f0")
        layer1(tb, tok0, ntok, is_s)
        checkpoint("l1") ; checkpoint("s:l1")
        ffn(1, ntok)
        arena.reset()
        yo_, yo_r = arena.alloc([128, 8, 512], F32)
        rmsnorm(4, ntok)
        for c in range(8):
            DVE(lambda e: e.scalar_tensor_tensor(out=yo_[:, c, :ntok], in0=xT[:, c, :ntok], scalar=nw[:, 32 + c:33 + c], in1=rstd[:, :ntok],
                                                 op0=ALU.mult, op1=ALU.mult), R=[xT_r, nw_r, rstd_r], W=[yo_r])
        kb.dma(kb.qs, O["yT"][:, tok0:tok0 + ntok].rearrange("(c p) t -> p c t", p=128), yo_[:, :, :ntok], R=[yo_r])
        checkpoint("tb0") ; checkpoint("s:tb")
    except _Stop:
        dbgout("xT", xT[:].rearrange("p a b -> p (a b)"), xT_r, [128, 4096])
        dbgout("Ecs", Ecs[:].rearrange("p a b c -> p (a b c)"), Ecs_r, [128, 4096])
        dbgout("s5c", s5c[:].rearrange("p a b -> p (a b)"), s5c_r, [128, 128])
        dbgout("s5o", s5o[:].rearrange("p a b c -> p (a b c)"), s5o_r, [128, 544])
    kb.dma(kb.qs, O["o_s5"], s5o[:].rearrange("p a b c -> p (a b c)"), R=[s5o_r])
    kb.dma(kb.qs, O["o_conv"], oconv[:].rearrange("p a b c -> p (a b c)"), R=[oconv_r])
    kb.dma(kb.qs, O["o_shift"], oshift[:].rearrange("p a b -> p (a b)"), R=[oshift_r])
    kb.finish()
    DEBUG["arena_peak"] = arena.peak * 4
    DEBUG["nins"] = kb.nins
    return nc


def _consts():
    i = np.arange(128)
    s, t = i[:, None], i[None, :]
    same = (s // 8) == (t // 8)
    ident = np.eye(128, dtype=np.float32)
    jidx = np.tile(np.arange(1, 129, dtype=np.float32), (128, 1))
    blk64 = ((s // 64) == (t // 64)).astype(np.float32)
    ones = np.ones((128, 128), np.float32)
    cstf = np.concatenate([ident, jidx, blk64, ones], 1)
    MsT_p, MiT_p = (s < t), (s <= t)
    MsT_s, MiT_s = MsT_p & same, MiT_p & same
    NEG_p = np.where(MiT_p, 0.0, -30000.0)
    NEG_s = np.where(MiT_s, 0.0, -30000.0)
    g2 = (i // 64)[:, None, None]
    j4 = np.arange(4)[None, :, None]
    g8 = (np.arange(128) // 16)[None, None, :]
    mask4 = (g8 == 2 * j4 + g2).astype(np.float32).reshape(128, 512)
    rst_p = np.ones((128, 512), np.float32)
    rst_p[:, ::128] = 0
    rst_s = np.ones((128, 128), np.float32)
    rst_s[:, ::8] = 0
    ind16 = ((i[:, None] // 8) == np.arange(16)[None, :]).astype(np.float32)
    cstb = np.concatenate([MsT_p, MiT_p, MsT_s, MiT_s, NEG_p, NEG_s, ones, mask4, rst_p, rst_s, ind16], 1).astype(np.float32)
    sel8 = np.zeros((8, 8, 128), np.float32)
    for j in range(8):
        sel8[j, j, :] = 1.0
    return cstf, cstb, sel8.reshape(8, 1024)


_NC_CACHE = {}


def _prep(inp):
    f = lambda a: np.ascontiguousarray(np.asarray(a, dtype=np.float32))
    inp = {k: np.asarray(v) for k, v in inp.items()}
    cstf, cstb, sel8 = _consts()
    pc = lambda w: f(w.reshape(-1, 128).T)
    nw = np.concatenate([pc(inp["norm_mix"][0]), pc(inp["norm_ffn"][0]), pc(inp["norm_mix"][1]),
                         pc(inp["norm_ffn"][1]), pc(inp["norm_final"])], 1)
    sp = lambda a: a.reshape(16, 2, 64).transpose(1, 2, 0).reshape(128, 16)
    ls = np.repeat(inp["s5_log_step"][0].reshape(16, 2, 1), 64, axis=2).transpose(1, 2, 0).reshape(128, 16)
    Bl = lambda a: a.reshape(16, 2, 64, 16).transpose(1, 2, 0, 3).reshape(128, 256)
    Cl = lambda a: a.reshape(16, 2, 16, 64).transpose(1, 3, 0, 2).reshape(128, 256)
    s5p = f(np.concatenate([sp(inp["s5_lambda_re"][0]), sp(inp["s5_lambda_im"][0]), ls, Bl(inp["s5_B_re"][0]),
                            Bl(inp["s5_B_im"][0]), Cl(inp["s5_C_re"][0]), Cl(inp["s5_C_im"][0]),
                            inp["s5_D"][0].reshape(4, 128).T], 1))
    gp = np.zeros((128, 51), np.float32)
    gp[:, 0:48] = inp["gdn_conv_w"][0].reshape(4, 12, 128).transpose(2, 1, 0).reshape(128, 48)
    gp[:, 48] = inp["gdn_norm_w"][0]
    gp[4:8, 49] = inp["gdn_A_log"][0]
    gp[4:8, 50] = inp["gdn_dt_bias"][0]
    rwp = np.zeros((128, 104), np.float32)
    for i, k in enumerate(["rw_w0", "rw_a0", "rw_k_k", "rw_k_a", "rw_ln_w", "rw_ln_b"]):
        rwp[:, i:56:7] = inp[k][0].reshape(8, 128).T
    rwp[:, 6:56:7] = inp["rw_r_k"][0].reshape(8, 128).T
    rwp[:, 56:104] = inp["rw_maa"][0].reshape(6, 8, 128).transpose(2, 0, 1).reshape(128, 48)
    shared = dict(cstf=cstf, cstb=cstb, sel8=sel8, nw=f(nw), s5p=s5p, gp=gp, rwp=rwp,
                  W_in=f(inp["ab_w_in"][0]), W_out=f(inp["ab_w_out"][0]), W_glu=f(inp["s5_w_glu"][0]),
                  Wg0=f(inp["ffn_w_gate"][0]), Wu0=f(inp["ffn_w_up"][0]), Wd0=f(inp["ffn_w_down"][0]),
                  Wg1=f(inp["ffn_w_gate"][1]), Wu1=f(inp["ffn_w_up"][1]), Wd1=f(inp["ffn_w_down"][1]),
                  W_r=f(inp["rw_w_r"][0]), W_k=f(inp["rw_w_k"][0]), W_v=f(inp["rw_w_v"][0]), W_o=f(inp["rw_w_o"][0]),
                  w1=f(inp["rw_w1"][0]), w2=f(inp["rw_w2"][0]), a1=f(inp["rw_a1"][0]), a2=f(inp["rw_a2"][0]),
                  g1=f(inp["rw_g1"][0]), g2=f(inp["rw_g2"][0]))
    in_maps = []
    for c in range(8):
        sl = slice(16 * c, 16 * c + 16)
        x = np.concatenate([inp["x_prompt"][c], inp["x_sample"][sl].reshape(128, D)], 0)
        m = dict(shared)
        m["xT"] = f(x.T)
        st = lambda a: a[0, sl].reshape(16, 16, 2, 64).transpose(2, 3, 1, 0).reshape(128, 256)
        m["s5h0"] = f(np.concatenate([st(inp["state_s5_re"]), st(inp["state_s5_im"])], 1))
        m["gconv0"] = f(inp["state_gdn_conv"][0, sl].reshape(16, 3, 12, 128).transpose(3, 2, 0, 1).reshape(128, 576))
        m["gs0"] = f(inp["state_gdn"][0, sl].transpose(1, 2, 0, 3).reshape(4, 128, 2048))
        m["rshift0"] = f(inp["state_rwkv_shift"][0, sl].reshape(16, 8, 128).transpose(2, 1, 0).reshape(128, 128))
        m["rs0"] = f(inp["state_rwkv"][0, sl].reshape(16, 8, 2, 64, 64).transpose(1, 2, 4, 0, 3).reshape(8, 128, 1024))
        in_maps.append(m)
    return in_maps


def kernel(**inp):
    in_maps = _prep(inp)
    if "nc" not in _NC_CACHE:
        _NC_CACHE["nc"] = build()
    res = run_bass_kernel_spmd(_NC_CACHE["nc"], in_maps, core_ids=list(range(8)))
    return _post(res.results)


def _post(R):
    y_p = np.zeros((8, 2048, D), np.float32)
    y_s = np.zeros((128, 8, D), np.float32)
    p_s5 = np.zeros((2, 1, 8, 32, 64), np.float32)
    s_s5 = np.zeros((2, 1, 128, 32, 64), np.float32)
    p_gdn = np.zeros((1, 8, 4, 128, 128), np.float32)
    s_gdn = np.zeros((1, 128, 4, 128, 128), np.float32)
    p_conv = np.zeros((1, 8, 3, 1536), np.float32)
    s_conv = np.zeros((1, 128, 3, 1536), np.float32)
    p_rw = np.zeros((1, 8, 16, 64, 64), np.float32)
    s_rw = np.zeros((1, 128, 16, 64, 64), np.float32)
    p_sh = np.zeros((1, 8, D), np.float32)
    s_sh = np.zeros((1, 128, D), np.float32)
    for c in range(8):
        r = R[c]
        sl = slice(16 * c, 16 * c + 16)
        y = r["yT"].T
        y_p[c] = y[:2048]
        y_s[sl] = y[2048:].reshape(16, 8, D)
        o5 = r["o_s5"].reshape(2, 64, 2, 16, 17)
        o5 = o5.transpose(2, 4, 3, 0, 1).reshape(2, 17, 32, 64)
        p_s5[:, 0, c] = o5[:, 0]
        s_s5[:, 0, sl] = o5[:, 1:]
        p_gdn[0, c] = r["o_gdn_p"]
        s_gdn[0, sl] = r["o_gdn_s"].reshape(4, 128, 16, 128).transpose(2, 0, 1, 3)
        oc = r["o_conv"].reshape(128, 12, 17, 3).transpose(2, 3, 1, 0).reshape(17, 3, 1536)
        p_conv[0, c] = oc[0]
        s_conv[0, sl] = oc[1:]
        p_rw[0, c] = r["o_rw_p"].reshape(8, 2, 64, 64).transpose(0, 1, 3, 2).reshape(16, 64, 64)
        s_rw[0, sl] = r["o_rw_s"].reshape(8, 2, 64, 16, 64).transpose(3, 0, 1, 4, 2).reshape(16, 16, 64, 64)
        osx = r["o_shift"].reshape(128, 8, 17).transpose(2, 1, 0).reshape(17, D)
        p_sh[0, c] = osx[0]
        s_sh[0, sl] = osx[1:]
    return (y_p, y_s, p_s5[0], p_s5[1], p_gdn, p_conv, p_rw, p_sh,
            s_s5[0], s_s5[1], s_gdn, s_conv, s_rw, s_sh)
```

```python
import math
from contextlib import ExitStack

import numpy as np
import concourse.bass as bass
import concourse.mybir as mybir
from concourse.bass_utils import run_bass_kernel_spmd

F32 = mybir.dt.float32
BF16 = mybir.dt.bfloat16
I32 = mybir.dt.int32
AF = mybir.ActivationFunctionType
ALU = mybir.AluOpType
AX = mybir.AxisListType

GEN = 12000
NDSEM = 8


class Res:
    __slots__ = ("w", "r")

    def __init__(self):
        self.w = None
        self.r = {}


class _Eng:
    def __init__(self, kb, key, eng, is_pe=False):
        self.kb, self.key, self.e, self.is_pe = kb, key, eng, is_pe
        self.n = 0
        self.sems = []
        self.known = {}

    def semval(self, seq):
        g = (seq - 1) // GEN
        while len(self.sems) <= g:
            self.sems.append(self.kb.sem(f"s_{self.key}_{len(self.sems)}"))
        return self.sems[g], (seq - 1) % GEN + 1


class _DSlot:
    def __init__(self, kb, key):
        self.key = key
        self.sem = kb.sem(f"d_{key}")
        self.n = 0

    def semval(self, seq):
        return self.sem, 16 * seq


class _Queue:
    def __init__(self, kb, name, eng, known):
        self.name, self.e, self.known = name, eng, known
        self.slots = [_DSlot(kb, f"{name}{i}") for i in range(NDSEM)]
        for s in self.slots:
            kb.prod[s.key] = s
        self.rr = 0


class KB:
    def __init__(self):
        self.nc = bass.Bass("TRN2", target_bir_lowering=False)
        self.es = ExitStack()
        self.prod = {}
        nc = self.nc
        self.PE = _Eng(self, "pe", nc.tensor, True)
        self.ACT = _Eng(self, "act", nc.scalar)
        self.DVE = _Eng(self, "dve", nc.vector)
        self.POOL = _Eng(self, "pool", nc.gpsimd)
        for e in (self.PE, self.ACT, self.DVE, self.POOL):
            self.prod[e.key] = e
        self.qs = _Queue(self, "qs", nc.sync, {})
        self.qg = _Queue(self, "qg", nc.gpsimd, self.POOL.known)
        self.nins = 0

    def sem(self, name):
        return self.es.enter_context(self.nc.semaphore(name))

    def sb(self, name, shape, dt, nres=1):
        t = self.es.enter_context(self.nc.sbuf_tensor("sb_" + name, list(shape), dt))
        if nres == 1:
            return t, Res()
        return t, [Res() for _ in range(nres)]

    def ps(self, name, shape, dt):
        t = self.es.enter_context(self.nc.psum_tensor("ps_" + name, list(shape), dt))
        return t, Res()

    def _waits(self, E_key, known, eng, R, W, skip_self):
        waits = {}

        def need(tok):
            if tok is None:
                return
            k, s = tok
            if waits.get(k, 0) < s:
                waits[k] = s

        for r in R:
            need(r.w)
        for w in W:
            need(w.w)
            for k, s in w.r.items():
                need((k, s))
        for k, s in waits.items():
            if skip_self and k == E_key:
                continue
            if known.get(k, 0) >= s:
                continue
            sem, val = self.prod[k].semval(s)
            eng.wait_ge(sem, val)
            known[k] = s

    def op(self, E, fn, R=(), W=()):
        self._waits(E.key, E.known, E.e, R, W, E.is_pe)
        ins = fn(E.e)
        n = E.n + 1
        sem, _ = E.semval(n)
        ins.then_inc(sem, 1)
        E.n = n
        self.nins += 1
        for r in R:
            r.r[E.key] = n
        for w in W:
            w.w = (E.key, n)
            w.r = {}
        return ins

    def pe(self, fn, R=(), W=()):
        return self.op(self.PE, fn, R, W)

    def act(self, fn, R=(), W=()):
        return self.op(self.ACT, fn, R, W)

    def dve(self, fn, R=(), W=()):
        return self.op(self.DVE, fn, R, W)

    def pool(self, fn, R=(), W=()):
        return self.op(self.POOL, fn, R, W)

    def dma(self, q, out, in_, R=(), W=(), **kw):
        slot = q.slots[q.rr]
        q.rr = (q.rr + 1) % NDSEM
        self._waits(None, q.known, q.e, R, W, False)
        if slot.n > 0 and q.known.get(slot.key, 0) < slot.n:
            q.e.wait_ge(slot.sem, 16 * slot.n)
            q.known[slot.key] = slot.n
        ins = q.e.dma_start(out=out, in_=in_, **kw)
        ins.then_inc(slot.sem, 16)
        slot.n += 1
        self.nins += 1
        for r in R:
            r.r[slot.key] = slot.n
        for w in W:
            w.w = (slot.key, slot.n)
            w.r = {}

    def finish(self):
        for q in (self.qs, self.qg):
            for s in q.slots:
                if s.n > 0:
                    q.e.wait_ge(s.sem, 16 * s.n)
        self.es.close()

    def make_identity(self, t, r, n=128, dt=F32):
        self.pool(lambda e: e.memset(t[:], 0.0), W=[r])
        self.pool(lambda e: e.affine_select(out=t[:], in_=t[:], pattern=[[-1, n]],
                                            compare_op=ALU.not_equal, fill=1.0, base=0,
                                            channel_multiplier=1), R=[r], W=[r])


    def barrier(self):
        toks = [(e.key, e.n) for e in (self.PE, self.ACT, self.DVE, self.POOL) if e.n > 0]
        for q in (self.qs, self.qg):
            toks += [(s.key, s.n) for s in q.slots if s.n > 0]
        cons = [(e.known, e.e) for e in (self.PE, self.ACT, self.DVE, self.POOL)] + [(self.qs.known, self.qs.e)]
        for known, eng in cons:
            for k, s in toks:
                if known.get(k, 0) >= s:
                    continue
                sem, val = self.prod[k].semval(s)
                eng.wait_ge(sem, val)
                known[k] = s


class Arena:
    def __init__(self, kb, nbytes):
        self.kb = kb
        self.t, _ = kb.sb("arena", [128, nbytes // 4], F32)
        self.cap = nbytes // 4
        self.off = 0
        self.peak = 0

    def reset(self):
        self.kb.barrier()
        self.off = 0

    def alloc(self, shape, dt, nres=1):
        n = int(np.prod(shape[1:]))
        words = n if dt in (F32, I32) else (n + 1) // 2
        assert self.off + words <= self.cap, f"arena overflow {self.off + words} > {self.cap}"
        v = self.t[:shape[0], self.off:self.off + words]
        self.off += words
        self.peak = max(self.peak, self.off)
        if dt != F32:
            v = v.bitcast(dt)
            if dt == BF16 and n % 2:
                v = v[:, :n]
        if len(shape) == 3:
            v = v.rearrange("p (a b) -> p a b", a=shape[1])
        elif len(shape) == 4:
            v = v.rearrange("p (a b c) -> p a b c", a=shape[1], b=shape[2])
        if nres == 1:
            return v, Res()
        return v, [Res() for _ in range(nres)]


D = 1024
TT = 2176
DFF = 2816
NB_CST = 2064
CB = dict(MsT_p=0, MiT_p=128, MsT_s=256, MiT_s=384, NEG_p=512, NEG_s=640, ones=768,
          mask4=896, rst_p=1408, rst_s=1920, ind16=2048)
CF = dict(ident=0, jidx=128, blk64=256, ones=384)
TWO_PI = 2.0 * math.pi
C1 = 6.28125
C2 = TWO_PI - C1
EXPM05 = math.exp(-0.5)
DEBUG = {}


class _Stop(Exception):
    pass


def build(dbg=(), stop_after=None):
    kb = KB()
    nc = kb.nc
    PE, ACT, DVE, POOL = kb.pe, kb.act, kb.dve, kb.pool
    mm = lambda e, *a, **k: e.matmul(*a, **k)

    def din(name, shape):
        return nc.dram_tensor(name, list(shape), F32, kind="ExternalInput").ap()

    def dout(name, shape):
        return nc.dram_tensor(name, list(shape), F32, kind="ExternalOutput").ap()

    I = {}
    for name, shape in [
        ("xT", [D, TT]), ("cstf", [128, 512]), ("cstb", [128, NB_CST]), ("sel8", [8, 1024]),
        ("nw", [128, 40]),
        ("W_in", [D, 2568]), ("W_out", [D, D]), ("W_glu", [512, 512]),
        ("Wg0", [D, DFF]), ("Wu0", [D, DFF]), ("Wd0", [DFF, D]),
        ("Wg1", [D, DFF]), ("Wu1", [D, DFF]), ("Wd1", [DFF, D]),
        ("W_r", [D, D]), ("W_k", [D, D]), ("W_v", [D, D]), ("W_o", [D, D]),
        ("w1", [D, 64]), ("w2", [64, D]), ("a1", [D, 64]), ("a2", [64, D]),
        ("g1", [D, 160]), ("g2", [160, D]),
        ("s5p", [128, 1076]), ("s5h0", [128, 512]),
        ("gp", [128, 51]), ("gconv0", [128, 576]), ("gs0", [4, 128, 2048]),
        ("rwp", [128, 104]), ("rshift0", [128, 128]), ("rs0", [8, 128, 1024]),
    ]:
        I[name] = din(name, shape)
    O = {}
    for name, shape in [
        ("yT", [D, TT]), ("o_s5", [128, 2 * 16 * 17]), ("o_gdn_p", [4, 128, 128]),
        ("o_gdn_s", [4, 128, 2048]), ("o_conv", [128, 12 * 17 * 3]), ("o_rw_p", [8, 128, 64]),
        ("o_rw_s", [8, 128, 1024]), ("o_shift", [128, 8 * 17]),
    ]:
        O[name] = dout(name, shape)

    def dbgout(name, ap, res, shape):
        if name in dbg:
            d = dout("dbg_" + name, shape)
            kb.dma(kb.qs, d, ap, R=[res] if isinstance(res, Res) else list(res))

    def checkpoint(name):
        if stop_after == name:
            raise _Stop()

    cstf, cstf_r = kb.sb("cstf", [128, 512], F32)
    cstb, cstb_r = kb.sb("cstb", [128, NB_CST], BF16)
    sel8, sel8_r = kb.sb("sel8", [8, 1024], F32)
    nw, nw_r = kb.sb("nw", [128, 40], F32)
    kc, kc_r = kb.sb("kc", [128, 8], F32)
    xT, xT_r = kb.sb("xTs", [128, 8, 512], F32)
    hT, hT_r = kb.sb("hT", [128, 8, 512], BF16)
    rstd, rstd_r = kb.sb("rstd", [128, 512], F32)
    NSLAB = 2
    wsl = [kb.sb(f"wslab{i}", [128, 4096], BF16) for i in range(NSLAB)]
    wsl_i = [0]
    ident = cstf[:, 0:128]
    jidx = cstf[:, 128:256]
    blk64 = cstf[:, 256:384]
    onesf = cstf[:, 384:512]
    cb_ = lambda k, n=128: cstb[:, CB[k]:CB[k] + n]
    onesb = cb_("ones")

    kb.dma(kb.qs, cstf[:], I["cstf"], W=[cstf_r])
    kb.dma(kb.qg, cstb[:], I["cstb"], W=[cstb_r])
    kb.dma(kb.qs, sel8[:], I["sel8"], W=[sel8_r])
    kb.dma(kb.qs, nw[:], I["nw"], W=[nw_r])
    for i, v in enumerate([math.pi / 2, 1e-6, 1.0, 64e-5, 0.0]):
        DVE(lambda e: e.memset(kc[:, i:i + 1], v), W=[kc_r])

    pA = [kb.ps(f"pA{i}", [128, 512], F32) for i in range(2)]
    pB, pB_r = kb.ps("pB", [128, 512], F32)
    pT, _ = kb.ps("pT", [128, 1024], BF16)
    pT_regs = [(pT[:, 0:128], Res())]
    pDb = [kb.ps(f"pD{i}", [128, 512], F32)[0] for i in range(4)]
    pD_regs = [(pDb[i], Res()) for i in range(4)] + [pA[0], pA[1], (pB, pB_r)]
    ctr = dict(a=0, d=0, t=0)

    def psA():
        ctr["a"] += 1
        return pA[ctr["a"] % 2]

    def psD():
        ctr["d"] += 1
        r_ = pD_regs[ctr["d"] % 7]
        return r_[0][:, 0:128], r_[1]

    def psT_full():
        return pT, pT_regs[0][1]

    def psD512():
        ctr["d"] += 1
        return pD_regs[ctr["d"] % 7]

    def psDw():
        ctr["d"] += 1
        r_ = pD_regs[ctr["d"] % 7]
        return r_[0][:, 0:256], r_[1]

    def psT():
        ctr["t"] += 1
        return pT_regs[0]

    arena = Arena(kb, 79 * 1024)

    SUB = 1024
    slab_state = dict(subs=[], p=0)

    def set_pool(buffers):
        slab_state["subs"] = [(t_, si, Res()) for t_, _ in buffers for si in range(4)]
        slab_state["p"] = 0
    set_pool(wsl)

    def get_slab(nelem):
        subs = slab_state["subs"]
        n = 1 if nelem <= SUB else (2 if nelem <= 2 * SUB else 4)
        p = slab_state["p"]
        p = (p + n - 1) // n * n
        if p + n > len(subs):
            p = 0
        t_, si, _ = subs[p]
        slab_state["p"] = p + n
        return t_[:, si * SUB:si * SUB + nelem], [subs[p + i][2] for i in range(n)]

    def linear_fm(W, K, col0, ncols, act, ntok, cb):
        kcs = [(r0, min(128, K - r0)) for r0 in range(0, K, 128)]
        nk = len(kcs)
        SW = ncols if ncols < 128 else min(ncols, (4096 // nk) // 128 * 128)
        for s0 in range(0, ncols, SW):
            sw = min(SW, ncols - s0)
            slab, sres = get_slab(nk * sw)
            view = slab.rearrange("p (k n) -> p k n", k=nk)
            if K % 128 == 0:
                kb.dma(kb.qg, view, W[:, col0 + s0:col0 + s0 + sw].rearrange("(k p) n -> p k n", p=128), W=sres)
            else:
                for ki, (r0, rows) in enumerate(kcs):
                    kb.dma(kb.qg, view[:rows, ki, :], W[r0:r0 + rows, col0 + s0:col0 + s0 + sw], W=sres)
            for m0 in range(0, sw, 128):
                m = min(128, sw - m0)
                ps, pres = psA()
                for ki, (r0, rows) in enumerate(kcs):
                    a, ares = act[ki]
                    PE(lambda e: e.matmul(ps[:m, :ntok], lhsT=view[:rows, ki, m0:m0 + m], rhs=a[:rows, :ntok],
                                          start=(ki == 0), stop=(ki == nk - 1)), R=sres + [ares], W=[pres])
                cb(ps, pres, col0 + s0 + m0, m)

    def rsqrt_into(out_ap, out_r, in_ap, in_r, scale, eps_col, n, parts=128):
        ACT(lambda e: e.activation(out=out_ap, in_=in_ap, func=AF.Ln, bias=kc[:parts, eps_col:eps_col + 1], scale=scale),
            R=[in_r, kc_r], W=[out_r])
        ACT(lambda e: e.activation(out=out_ap, in_=out_ap, func=AF.Exp, scale=-0.5), R=[out_r], W=[out_r])

    def rmsnorm(widx, ntok, scratch=None):
        sq, sq_r = scratch if scratch is not None else arena.alloc([128, 8, 512], BF16)
        ACT(lambda e: e.activation(out=sq[:, :, :ntok], in_=xT[:, :, :ntok], func=AF.Square), R=[xT_r], W=[sq_r])
        ps, pres = psA()
        for c in range(8):
            PE(lambda e: e.matmul(ps[:, :ntok], lhsT=onesb, rhs=sq[:, c, :ntok], start=(c == 0), stop=(c == 7)),
               R=[cstb_r, sq_r], W=[pres])
        rsqrt_into(rstd[:, :ntok], rstd_r, ps[:, :ntok], pres, 1.0 / D, 1, ntok)
        for c in range(8):
            DVE(lambda e: e.scalar_tensor_tensor(out=hT[:, c, :ntok], in0=xT[:, c, :ntok],
                                                 scalar=nw[:, widx * 8 + c:widx * 8 + c + 1], in1=rstd[:, :ntok],
                                                 op0=ALU.mult, op1=ALU.mult), R=[xT_r, nw_r, rstd_r], W=[hT_r])

    def resid_add_cb(ntok):
        def cb(ps, pres, col, m):
            c = col // 128
            DVE(lambda e: e.tensor_tensor(out=xT[:, c, :ntok], in0=xT[:, c, :ntok], in1=ps[:, :ntok], op=ALU.add),
                R=[xT_r, pres], W=[xT_r])
        return cb

    def ffn(layer, ntok):
        arena.reset()
        actT, actT_r = arena.alloc([128, 22, 512], BF16)
        sg, sg_r = arena.alloc([128, 4, 512], BF16, nres=4)
        rmsnorm(1 + 2 * layer, ntok)
        extra = []
        for i in range(5):
            t_, r_ = arena.alloc([128, 4096], BF16)
            extra.append((t_, r_))
        set_pool(list(wsl) + extra)
        Wg, Wu, Wd = I[f"Wg{layer}"], I[f"Wu{layer}"], I[f"Wd{layer}"]
        hact = [(hT[:, c, :], hT_r) for c in range(8)]
        cnt = [0]

        def cb_gate(ps, pres, col, m):
            i = (col // 128) % 4
            ACT(lambda e: e.activation(out=sg[:, i, :ntok], in_=ps[:, :ntok], func=AF.Silu), R=[pres], W=[sg_r[i]])

        def cb_up(ps, pres, col, m):
            c = col // 128
            i = c % 4
            DVE(lambda e: e.tensor_tensor(out=actT[:, c, :ntok], in0=sg[:, i, :ntok], in1=ps[:, :ntok], op=ALU.mult),
                R=[sg_r[i], pres], W=[actT_r])
        for f0 in range(0, DFF, 512):
            fw = min(512, DFF - f0)
            linear_fm(Wg, D, f0, fw, hact, ntok, cb_gate)
            linear_fm(Wu, D, f0, fw, hact, ntok, cb_up)
        aact = [(actT[:, c, :], actT_r) for c in range(22)]
        linear_fm(Wd, DFF, 0, D, aact, ntok, resid_add_cb(ntok))
        set_pool(wsl)

    lhsTB, lhsTB_r = kb.sb("lhsTB", [128, 2, 16, 128], BF16)
    lhsTC, lhsTC_r = kb.sb("lhsTC", [128, 2, 16, 128], BF16)
    Ecs, Ecs_r = kb.sb("Ecs", [128, 2, 16, 128], F32)
    s5c, s5c_r = kb.sb("s5c", [128, 8, 16], F32)
    s5o, s5o_r = kb.sb("s5o", [128, 2, 16, 17], F32)
    s5k, s5k_r = kb.sb("s5k", [128, 3, 16, 2], F32)

    def sincos(x, n, out_s, out_c, res_x, res_o):
        s1, s1_r = arena.alloc([128, n], F32)
        si, si_r = arena.alloc([128, n], I32)
        r, r_r = arena.alloc([128, n], F32)
        DVE(lambda e: e.tensor_scalar(out=s1, in0=x, scalar1=1.0 / TWO_PI, scalar2=None, op0=ALU.mult), R=[res_x], W=[s1_r])
        DVE(lambda e: e.tensor_copy(out=si, in_=s1), R=[s1_r], W=[si_r])
        DVE(lambda e: e.tensor_copy(out=s1, in_=si), R=[si_r], W=[s1_r])
        DVE(lambda e: e.scalar_tensor_tensor(out=r, in0=s1, scalar=-C1, in1=x, op0=ALU.mult, op1=ALU.add), R=[s1_r, res_x], W=[r_r])
        DVE(lambda e: e.scalar_tensor_tensor(out=r, in0=s1, scalar=-C2, in1=r, op0=ALU.mult, op1=ALU.add), R=[s1_r, r_r], W=[r_r])
        DVE(lambda e: e.tensor_scalar(out=s1, in0=r, scalar1=math.pi, scalar2=-TWO_PI, op0=ALU.is_gt, op1=ALU.mult), R=[r_r], W=[s1_r])
        DVE(lambda e: e.tensor_tensor(out=r, in0=r, in1=s1, op=ALU.add), R=[r_r, s1_r], W=[r_r])
        DVE(lambda e: e.tensor_scalar(out=s1, in0=r, scalar1=-math.pi, scalar2=TWO_PI, op0=ALU.is_lt, op1=ALU.mult), R=[r_r], W=[s1_r])
        DVE(lambda e: e.tensor_tensor(out=r, in0=r, in1=s1, op=ALU.add), R=[r_r, s1_r], W=[r_r])
        ACT(lambda e: e.activation(out=out_s, in_=r, func=AF.Sin), R=[r_r], W=[res_o])
        DVE(lambda e: e.scalar_tensor_tensor(out=s1, in0=r, scalar=-1.0, in1=r, op0=ALU.mult, op1=ALU.max), R=[r_r], W=[s1_r])
        ACT(lambda e: e.activation(out=out_c, in_=s1, func=AF.Sin, bias=kc[:, 0:1], scale=-1.0), R=[s1_r, kc_r], W=[res_o])

    def s5_setup():
        p, p_r = arena.alloc([128, 1076], F32)
        kb.dma(kb.qs, p, I["s5p"], W=[p_r])
        t, t_r = arena.alloc([128, 12, 16], F32)
        lr, li, ls = p[:, 0:16], p[:, 16:32], p[:, 32:48]
        T = lambda i: t[:, i, :]
        tt = lambda o, a, b, op: DVE(lambda e: e.tensor_tensor(out=o, in0=a, in1=b, op=op), R=[p_r, t_r, s5c_r], W=[t_r])
        ACT(lambda e: e.activation(out=T(0), in_=ls, func=AF.Exp), R=[p_r], W=[t_r])
        tt(T(1), lr, T(0), ALU.mult)
        ACT(lambda e: e.activation(out=s5c[:, 0, :], in_=T(1), func=AF.Exp), R=[t_r], W=[s5c_r])
        tt(T(2), li, T(0), ALU.mult)
        checkpoint('c1a')
        sincos(T(2), 16, T(3), T(4), t_r, t_r)
        checkpoint('c1b')
        tt(T(5), s5c[:, 0, :], T(4), ALU.mult)
        tt(T(6), s5c[:, 0, :], T(3), ALU.mult)
        DVE(lambda e: e.tensor_scalar(out=T(5), in0=T(5), scalar1=-1.0, scalar2=None, op0=ALU.add), R=[t_r], W=[t_r])
        tt(T(7), lr, lr, ALU.mult)
        tt(T(8), li, li, ALU.mult)
        tt(T(7), T(7), T(8), ALU.add)
        DVE(lambda e: e.reciprocal(out=T(7), in_=T(7)), R=[t_r], W=[t_r])
        tt(T(8), T(5), lr, ALU.mult)
        tt(T(9), T(6), li, ALU.mult)
        tt(T(8), T(8), T(9), ALU.add)
        tt(T(8), T(8), T(7), ALU.mult)
        tt(T(9), T(6), lr, ALU.mult)
        tt(T(10), T(5), li, ALU.mult)
        tt(T(9), T(9), T(10), ALU.subtract)
        tt(T(9), T(9), T(7), ALU.mult)
        DVE(lambda e: e.tensor_copy(out=s5c[:, 1, 0:4], in_=p[:, 1072:1076]), R=[p_r], W=[s5c_r])
        DVE(lambda e: e.memset(s5c[:, 2:4, :], 0.0), W=[s5c_r])
        checkpoint('c1c')
        Bre = p[:, 48:304].rearrange("p (s c) -> p s c", s=16)
        Bim = p[:, 304:560].rearrange("p (s c) -> p s c", s=16)
        Cre = p[:, 560:816].rearrange("p (s c) -> p s c", s=16)
        Cim = p[:, 816:1072].rearrange("p (s c) -> p s c", s=16)
        bb, bb_r = arena.alloc([128, 4, 16, 16], F32)
        crb = T(8).unsqueeze(2).to_broadcast([128, 16, 16])
        cib = T(9).unsqueeze(2).to_broadcast([128, 16, 16])
        t3 = lambda o, a, b, op: DVE(lambda e: e.tensor_tensor(out=o, in0=a, in1=b, op=op), R=[p_r, t_r, bb_r], W=[bb_r])
        t3(bb[:, 0], Bre, crb, ALU.mult)
        t3(bb[:, 1], Bim, cib, ALU.mult)
        t3(bb[:, 0], bb[:, 0], bb[:, 1], ALU.subtract)
        t3(bb[:, 2], Bim, crb, ALU.mult)
        t3(bb[:, 3], Bre, cib, ALU.mult)
        t3(bb[:, 2], bb[:, 2], bb[:, 3], ALU.add)
        checkpoint('c1d')
        ex, ex_r = arena.alloc([128, 2, 128], F32, nres=2)
        n = 0
        for ri, src in ((0, bb[:, 0]), (1, bb[:, 2])):
            for sb_ in range(16):
                i = n % 2
                n += 1
                m4 = cstb[:, CB["mask4"] + (sb_ % 4) * 128:CB["mask4"] + (sb_ % 4 + 1) * 128].rearrange("p (g c) -> p g c", g=8)
                DVE(lambda e: e.tensor_tensor(out=ex[:, i, :].rearrange("p (g c) -> p g c", g=8),
                                              in0=src[:, sb_, :].unsqueeze(1).to_broadcast([128, 8, 16]), in1=m4, op=ALU.mult),
                    R=[bb_r, cstb_r], W=[ex_r[i]])
                ps, pres = psD()
                PE(lambda e: e.transpose(ps, ex[:, i, :], ident), R=[ex_r[i], cstf_r], W=[pres])
                ACT(lambda e: e.activation(out=lhsTB[:, ri, sb_, :], in_=ps, func=AF.Copy), R=[pres], W=[lhsTB_r])
        checkpoint('c1d2')
        for sb_ in range(16):
            m4 = cstb[:, CB["mask4"] + (sb_ % 4) * 128:CB["mask4"] + (sb_ % 4 + 1) * 128].rearrange("p (g c) -> p g c", g=8)
            DVE(lambda e: e.tensor_tensor(out=lhsTC[:, 0, sb_, :].rearrange("p (g c) -> p g c", g=8),
                                          in0=Cre[:, sb_, :].unsqueeze(1).to_broadcast([128, 8, 16]), in1=m4, op=ALU.mult),
                R=[p_r, cstb_r], W=[lhsTC_r])
            DVE(lambda e: e.scalar_tensor_tensor(out=lhsTC[:, 1, sb_, :].rearrange("p (g c) -> p g c", g=8),
                                                 in0=Cim[:, sb_, :].unsqueeze(1).to_broadcast([128, 8, 16]), scalar=-1.0, in1=m4,
                                                 op0=ALU.mult, op1=ALU.mult), R=[p_r, cstb_r], W=[lhsTC_r])
        checkpoint('c1e')
        phi, phi_r = arena.alloc([128, 16, 128], F32)
        DVE(lambda e: e.tensor_tensor(out=phi, in0=T(2).unsqueeze(2).to_broadcast([128, 16, 128]),
                                      in1=jidx.unsqueeze(1).to_broadcast([128, 16, 128]), op=ALU.mult),
            R=[t_r, cstf_r], W=[phi_r])
        sincos(phi.rearrange("p a b -> p (a b)"), 2048, Ecs[:, 1].rearrange("p a b -> p (a b)"),
               Ecs[:, 0].rearrange("p a b -> p (a b)"), phi_r, Ecs_r)
        DVE(lambda e: e.memset(s5k[:, 0], 0.0), W=[s5k_r])
        DVE(lambda e: e.tensor_copy(out=s5k[:, 1, :, 0], in_=Ecs[:, 0, :, 127]), R=[Ecs_r], W=[s5k_r])
        DVE(lambda e: e.tensor_copy(out=s5k[:, 1, :, 1], in_=Ecs[:, 1, :, 127]), R=[Ecs_r], W=[s5k_r])
        DVE(lambda e: e.tensor_scalar(out=s5k[:, 2, :, 0], in0=Ecs[:, 1, :, 127], scalar1=-1.0, scalar2=None, op0=ALU.mult), R=[Ecs_r], W=[s5k_r])
        DVE(lambda e: e.tensor_copy(out=s5k[:, 2, :, 1], in_=Ecs[:, 0, :, 127]), R=[Ecs_r], W=[s5k_r])

    arena.off = 0
    _early = False
    try:
        checkpoint("c0")
        s5_setup()
        checkpoint("c1")
    except _Stop:
        _early = True
    if _early:
        dbgout("Ecs", Ecs[:].rearrange("p a b c -> p (a b c)"), Ecs_r, [128, 4096])
        dbgout("s5c", s5c[:].rearrange("p a b -> p (a b)"), s5c_r, [128, 128])
        dbgout("kc", kc[:], kc_r, [128, 8])
        kb.finish()
        return nc

    identb, identb_r = kb.sb("identb", [128, 128], BF16)
    blk64b, blk64b_r = kb.sb("blk64b", [128, 128], BF16)
    DVE(lambda e: e.tensor_copy(out=blk64b[:], in_=blk64), R=[cstf_r], W=[blk64b_r])
    DVE(lambda e: e.tensor_copy(out=identb[:], in_=ident), R=[cstf_r], W=[identb_r])
    nwAll, _ = kb.sb("nwAll", [128, 4, 512], F32)
    nwB32 = [(nwAll[:, i, :].rearrange("p (a b) -> p a b", a=4), [Res() for _ in range(4)]) for i in range(4)]
    nwAll16 = nwAll[:].rearrange("p a b -> p (a b)").bitcast(BF16).rearrange("p (a b) -> p a b", a=4)
    nwB16 = [(nwAll16[:, i, :640].rearrange("p (a b) -> p a b", a=5), [Res() for _ in range(5)]) for i in range(4)]
    nwFall, _ = kb.sb("nwFall", [128, 4, 128], F32)
    nwF = [(nwFall[:, i, :], Res()) for i in range(4)]
    TTfin, TTfin_r = kb.sb("TTfin", [128, 8, 128], F32, nres=8)
    chF = [kb.sb(f"chF{i}", [128, 2, 128], F32, nres=2) for i in range(2)]
    chB = [kb.sb(f"chB{i}", [128, 1, 128], BF16, nres=1) for i in range(2)]
    matsB, _ = kb.sb("matsB", [128, 24, 128], BF16)
    matsB_r = [Res() for _ in range(24)]
    tmB, _ = kb.sb("tmB", [128, 12, 128], BF16)
    tmB_r = [Res() for _ in range(12)]
    slotK = [kb.sb(f"slotK{i}", [128, 2, 512], BF16, nres=2) for i in range(2)]

    def tm_buf(ti, hi, k):
        i = ((ti % 2) * 2 + hi) * 3 + k
        return tmB[:, i, :], tmB_r[i]

    def run_gens(gens):
        gens = list(gens)
        while gens:
            for g in list(gens):
                try:
                    next(g)
                except StopIteration:
                    gens.remove(g)

    def dplr_run(nt, heads, dk, dv, is_s, emit_XT, emit_tm, ops, S_of, eT_of, ycb_of, NEU_BF16=False, extra=(), make_only=False):
        nseg = 16 if is_s else 1
        nlev = 2 if is_s else 6
        NG, TM = {}, {}

        def neumann(ti, hi, ws):
            nb, nb_r = (nwB16 if NEU_BF16 else nwB32)[ws]
            tf, tf_r = nwF[ws]
            k = (ti % 4) * 2 + hi
            fin = (TTfin[:, k, :], TTfin_r[k])
            oth = (tf, tf_r)
            B = lambda i: (nb[:, i, :], nb_r[i])
            XTc = B(1)
            emit_XT(ti, hi, XTc[0], XTc[1], [(matsB[:, k * 3 + i, :], matsB_r[k * 3 + i]) for i in range(3)])
            yield
            Xc = B(0)
            if NEU_BF16:
                pt, pt_r = psT()
                PE(lambda e: e.transpose(pt, XTc[0], identb[:]), R=[XTc[1], identb_r], W=[pt_r])
                ACT(lambda e: e.activation(out=Xc[0], in_=pt, func=AF.Copy), R=[pt_r], W=[Xc[1]])
            else:
                ps, pres = psD()
                PE(lambda e: e.transpose(ps, XTc[0], ident), R=[XTc[1], cstf_r], W=[pres])
                ACT(lambda e: e.activation(out=Xc[0], in_=ps, func=AF.Copy), R=[pres], W=[Xc[1]])
            TT = fin if nlev % 2 == 0 else oth
            TTo = oth if nlev % 2 == 0 else fin
            DVE(lambda e: e.tensor_tensor(out=TT[0], in0=XTc[0], in1=ident, op=ALU.add), R=[XTc[1], cstf_r], W=[TT[1]])
            if NEU_BF16:
                TTb = B(4)
                DVE(lambda e: e.tensor_tensor(out=TTb[0], in0=XTc[0], in1=identb[:], op=ALU.add), R=[XTc[1], identb_r], W=[TTb[1]])
            else:
                TTb = TT
            yield
            for lev in range(1, nlev + 1):
                pi_ = 2 * (lev % 2)
                Xn, XTn = B(pi_), B(pi_ + 1)
                if lev < nlev:
                    psw, psw_r = psDw()
                    PE(lambda e: e.matmul(psw[:, 0:128], lhsT=XTc[0], rhs=Xc[0], start=True, stop=True), R=[XTc[1], Xc[1]], W=[psw_r])
                    PE(lambda e: e.matmul(psw[:, 128:256], lhsT=Xc[0], rhs=XTc[0], start=True, stop=True), R=[XTc[1], Xc[1]], W=[psw_r])
                    ACT(lambda e: e.activation(out=nb[:, pi_:pi_ + 2, :], in_=psw.rearrange("p (a b) -> p a b", a=2), func=AF.Copy),
                        R=[psw_r], W=[Xn[1], XTn[1]])
                else:
                    psx, psx_r = psD()
                    PE(lambda e: e.matmul(psx, lhsT=XTc[0], rhs=Xc[0], start=True, stop=True), R=[XTc[1], Xc[1]], W=[psx_r])
                    ACT(lambda e: e.activation(out=Xn[0], in_=psx, func=AF.Copy), R=[psx_r], W=[Xn[1]])
                yield
                psa, psa_r = psD()
                PE(lambda e: e.matmul(psa, lhsT=Xn[0], rhs=TTb[0], start=True, stop=True), R=[Xn[1], TTb[1]], W=[psa_r])
                DVE(lambda e: e.tensor_tensor(out=TTo[0], in0=psa, in1=TT[0], op=ALU.add), R=[psa_r, TT[1]], W=[TTo[1]])
                TT, TTo = TTo, TT
                if NEU_BF16:
                    if lev < nlev:
                        ACT(lambda e: e.activation(out=TTb[0], in_=TT[0], func=AF.Copy), R=[TT[1]], W=[TTb[1]])
                else:
                    TTb = TT
                Xc, XTc = Xn, XTn
                yield
            assert TT[0] is fin[0]
            NG[(ti, hi)] = TT

        def neumann_pair(ti, pw):
            wsA, wsB = 2 * pw, 2 * pw + 1
            nbp = nwAll16[:, wsA:wsB + 1, :640].rearrange("p h (a b) -> p h a b", a=5)
            rA, rB = nwB16[wsA][1], nwB16[wsB][1]
            k0 = (ti % 4) * 2
            fin = (TTfin[:, k0:k0 + 2, :], [TTfin_r[k0], TTfin_r[k0 + 1]])
            oth = (nwFall[:, wsA:wsB + 1, :], [nwF[wsA][1], nwF[wsB][1]])
            BR = lambda i: [rA[i], rB[i]]
            for hi in range(2):
                emit_XT(ti, hi, nbp[:, hi, 1, :], BR(1)[hi], [(matsB[:, (k0 + hi) * 3 + i, :], matsB_r[(k0 + hi) * 3 + i]) for i in range(3)])
            yield
            pt, pt_r = psT_full()
            for hi in range(2):
                PE(lambda e: e.transpose(pt[:, hi * 128:(hi + 1) * 128], nbp[:, hi, 1, :], identb[:]), R=[BR(1)[hi], identb_r], W=[pt_r])
            ACT(lambda e: e.activation(out=nbp[:, :, 0, :], in_=pt[:, 0:256].rearrange("p (h b) -> p h b", h=2), func=AF.Copy),
                R=[pt_r], W=BR(0))
            TT, TTo = fin, oth
            DVE(lambda e: e.tensor_tensor(out=TT[0], in0=nbp[:, :, 1, :], in1=ident.unsqueeze(1).to_broadcast([128, 2, 128]), op=ALU.add),
                R=BR(1) + [cstf_r], W=TT[1])
            DVE(lambda e: e.tensor_tensor(out=nbp[:, :, 4, :], in0=nbp[:, :, 1, :], in1=identb[:].unsqueeze(1).to_broadcast([128, 2, 128]),
                                          op=ALU.add), R=BR(1) + [identb_r], W=BR(4))
            yield
            cur = 0
            for lev in range(1, nlev + 1):
                nx = 2 - cur
                if lev < nlev:
                    psw, psw_r = psD512()
                    for hi in range(2):
                        PE(lambda e: e.matmul(psw[:, hi * 256:hi * 256 + 128], lhsT=nbp[:, hi, cur + 1, :], rhs=nbp[:, hi, cur, :],
                                              start=True, stop=True), R=[BR(cur)[hi], BR(cur + 1)[hi]], W=[psw_r])
                        PE(lambda e: e.matmul(psw[:, hi * 256 + 128:hi * 256 + 256], lhsT=nbp[:, hi, cur, :], rhs=nbp[:, hi, cur + 1, :],
                                              start=True, stop=True), R=[BR(cur)[hi], BR(cur + 1)[hi]], W=[psw_r])
                    ACT(lambda e: e.activation(out=nbp[:, :, nx:nx + 2, :], in_=psw.rearrange("p (h a b) -> p h a b", h=2, a=2), func=AF.Copy),
                        R=[psw_r], W=BR(nx) + BR(nx + 1))
                else:
                    psx, psx_r = psDw()
                    for hi in range(2):
                        PE(lambda e: e.matmul(psx[:, hi * 128:(hi + 1) * 128], lhsT=nbp[:, hi, cur + 1, :], rhs=nbp[:, hi, cur, :],
                                              start=True, stop=True), R=[BR(cur)[hi], BR(cur + 1)[hi]], W=[psx_r])
                    ACT(lambda e: e.activation(out=nbp[:, :, nx, :], in_=psx.rearrange("p (h b) -> p h b", h=2), func=AF.Copy),
                        R=[psx_r], W=BR(nx))
                yield
                psa, psa_r = psDw()
                for hi in range(2):
                    PE(lambda e: e.matmul(psa[:, hi * 128:(hi + 1) * 128], lhsT=nbp[:, hi, nx, :], rhs=nbp[:, hi, 4, :], start=True, stop=True),
                       R=[BR(nx)[hi], BR(4)[hi]], W=[psa_r])
                DVE(lambda e: e.tensor_tensor(out=TTo[0], in0=psa.rearrange("p (h b) -> p h b", h=2), in1=TT[0], op=ALU.add),
                    R=[psa_r] + TT[1], W=TTo[1])
                TT, TTo = TTo, TT
                if lev < nlev:
                    DVE(lambda e: e.tensor_copy(out=nbp[:, :, 4, :], in_=TT[0]), R=TT[1], W=BR(4))
                cur = nx
                yield
            assert TT is fin
            for hi in range(2):
                NG[(ti, hi)] = (TTfin[:, k0 + hi, :], TTfin_r[k0 + hi])

        def chain_seq(hi, tiles):
            pb = heads[hi]
            cf, cf_r = chF[hi]
            cb2, cb2_r = chB[hi]
            sk, sk_r = slotK[hi]
            RHS = (cf[:, 0, :], cf_r[0])
            QT = (cf[:, 1, :], cf_r[1])
            U = (cb2[:, 0, :dv], cb2_r)
            S = S_of(hi)
            for ti in tiles:
                if ti not in TM:
                    TM[ti] = emit_tm(ti)
                Vt, bSt, kSt = TM[ti][hi]
                k_ = (ti % 4) * 2 + hi
                AKT, RBT, RKT = [(matsB[:, k_ * 3 + i, :], matsB_r[k_ * 3 + i]) for i in range(3)]
                TT = NG[(ti, hi)]
                aS, rS, aSr = ops(ti, hi)
                eTot, eTot_r = eT_of(ti, hi)
                yield
                ps, pres = psD()
                PE(lambda e: e.matmul(ps[:, :dv], lhsT=AKT[0], rhs=Vt[0], start=True, stop=False), R=[AKT[1], Vt[1]], W=[pres])
                if not is_s:
                    PE(lambda e: e.matmul(ps[:, :dv], lhsT=aS, rhs=S["b"], start=False, stop=True), R=[aSr, S["r"]], W=[pres])
                else:
                    psq, psq_r = psD()
                    for j in range(16):
                        PE(lambda e: e.matmul(psq[:dv, 8 * j:8 * j + 8], lhsT=S["b"][:, j * dv:(j + 1) * dv], rhs=aS[:, 8 * j:8 * j + 8],
                                              start=(j == 0), stop=(j == 15)), R=[aSr, S["r"]], W=[psq_r])
                    ACT(lambda e: e.activation(out=QT[0][:dv, :], in_=psq[:dv, :], func=AF.Copy), R=[psq_r], W=[QT[1]])
                    PE(lambda e: e.matmul(ps[:, :dv], lhsT=QT[0][:dv, :], rhs=ident[:dv, :dv], start=False, stop=True),
                       R=[QT[1], cstf_r], W=[pres])
                DVE(lambda e: e.tensor_copy(out=RHS[0][:, :dv], in_=ps[:, :dv]), R=[pres], W=[RHS[1]])
                yield
                ps, pres = psD()
                PE(lambda e: e.matmul(ps[:, :dv], lhsT=TT[0], rhs=RHS[0][:, :dv], start=True, stop=True), R=[TT[1], RHS[1]], W=[pres])
                ACT(lambda e: e.activation(out=U[0], in_=ps[:, :dv], func=AF.Copy), R=[pres], W=[U[1]])
                yield
                if not is_s:
                    pss, pss_r = psD()
                    so = pss[pb:pb + dk, :dv]
                    PE(lambda e: e.matmul(so, lhsT=bSt[0], rhs=U[0], start=True, stop=False), R=[bSt[1], U[1]], W=[pss_r])
                    PE(lambda e: e.matmul(so, lhsT=kSt[0], rhs=Vt[0], start=False, stop=True), R=[kSt[1], Vt[1]], W=[pss_r])
                ps, pres = psD()
                yo = ps[pb:pb + dv, :]
                PE(lambda e: e.matmul(yo, lhsT=U[0], rhs=RBT[0], start=True, stop=False), R=[U[1], RBT[1]], W=[pres])
                PE(lambda e: e.matmul(yo, lhsT=Vt[0], rhs=RKT[0], start=False, stop=False), R=[Vt[1], RKT[1]], W=[pres])
                for j in range(nseg):
                    c0, Lg = (8 * j, 8) if is_s else (0, 128)
                    PE(lambda e: e.matmul(ps[pb:pb + dv, c0:c0 + Lg], lhsT=S["b"][:, j * dv:(j + 1) * dv], rhs=rS[:, c0:c0 + Lg],
                                          start=False, stop=(j == nseg - 1)), R=[aSr, S["r"]], W=[pres])
                if not is_s:
                    DVE(lambda e: e.scalar_tensor_tensor(out=S["f"], in0=S["f"], scalar=eTot[:, 0:1], in1=so, op0=ALU.mult, op1=ALU.add),
                        R=[S["r"], eTot_r, pss_r], W=[S["r"]])
                    ACT(lambda e: e.activation(out=S["b"], in_=S["f"], func=AF.Copy), R=[S["r"]], W=[S["r"]])
                ycb_of(ti, hi)(yo, pres)
                if is_s:
                    yield
                    spg = 512 // dv
                    ind = cstb[:, CB["ind16"]:CB["ind16"] + 16]
                    for g in range(16 // spg):
                        for wi_, src in ((0, U), (1, Vt)):
                            DVE(lambda e: e.tensor_tensor(out=sk[:, wi_, :].rearrange("p (j v) -> p j v", j=spg),
                                                          in0=src[0].unsqueeze(1).to_broadcast([128, spg, dv]),
                                                          in1=ind[:, g * spg:(g + 1) * spg].unsqueeze(2).to_broadcast([128, spg, dv]),
                                                          op=ALU.mult), R=[src[1], cstb_r], W=[sk_r[wi_]])
                        PE(lambda e: e.matmul(pB[pb:pb + dk, :], lhsT=bSt[0], rhs=sk[:, 0, :], start=True, stop=False),
                           R=[bSt[1], sk_r[0]], W=[pB_r])
                        PE(lambda e: e.matmul(pB[pb:pb + dk, :], lhsT=kSt[0], rhs=sk[:, 1, :], start=False, stop=True),
                           R=[kSt[1], sk_r[1]], W=[pB_r])
                        for jj in range(spg):
                            j = g * spg + jj
                            DVE(lambda e: e.scalar_tensor_tensor(out=S["f"][:, j * dv:(j + 1) * dv], in0=S["f"][:, j * dv:(j + 1) * dv],
                                                                 scalar=eTot[:, j:j + 1], in1=pB[pb:pb + dk, jj * dv:(jj + 1) * dv],
                                                                 op0=ALU.mult, op1=ALU.add), R=[S["r"], eTot_r, pB_r], W=[S["r"]])
                yield

        nh = len(heads)
        pairs = [list(range(t0, min(t0 + 2, nt))) for t0 in range(0, nt, 2)]

        def neu_gens(tiles):
            if NEU_BF16 and nh == 2 and nlev % 2 == 0:
                return [neumann_pair(ti, ti % 2) for ti in tiles]
            return [neumann(ti, hi, (ti % 2) * 2 + hi) for ti in tiles for hi in range(nh)]

        def chain_gens(tiles):
            return [chain_seq(hi, tiles) for hi in range(nh)]
        if make_only:
            return dict(neu=neu_gens, chain=chain_gens, pairs=pairs)
        run_gens(neu_gens(pairs[0]) + list(extra))
        for pi, tiles in enumerate(pairs):
            gens = chain_gens(tiles)
            if pi + 1 < len(pairs):
                gens += neu_gens(pairs[pi + 1])
            run_gens(gens + list(extra))

    gp, gp_r = kb.sb("gp", [128, 51], F32)
    kb.dma(kb.qs, gp[:], I["gp"], W=[gp_r])
    negA8, negA8_r = kb.sb("negA8", [8, 1], F32)
    ACT(lambda e: e.activation(out=negA8[:], in_=gp[:8, 49:50], func=AF.Exp), R=[gp_r], W=[negA8_r])
    DVE(lambda e: e.tensor_scalar(out=negA8[:], in0=negA8[:], scalar1=-1.0, scalar2=None, op0=ALU.mult), R=[negA8_r], W=[negA8_r])
    hist, hist_r = kb.sb("hist", [128, 12, 3], F32)
    DVE(lambda e: e.memset(hist[:], 0.0), W=[hist_r])
    oconv, oconv_r = kb.sb("oconv", [128, 12, 17, 3], F32)
    Sg_f, _ = kb.sb("Sg_f", [128, 4, 128], F32)
    Sg_b, _ = kb.sb("Sg_b", [128, 4, 128], BF16)
    Sg_r = [Res() for _ in range(4)]
    DVE(lambda e: e.memset(Sg_f[:], 0.0), W=Sg_r)
    DVE(lambda e: e.memset(Sg_b[:], 0.0), W=Sg_r)

    def s5_tb(tb, ntok, is_s, uaT, uaT_r, catT, catT_r, XR, XR_r):
        nseg, L = (16, 8) if is_s else (4, 128)
        wsets = [arena.alloc([128, 6, 512], F32, nres=6) for _ in range(2)]
        tmpcs = [arena.alloc([128, 4, 16], F32) for _ in range(2)]
        yt, yt_r = arena.alloc([128, 2, 512], F32, nres=2)
        ygT, ygT_r = arena.alloc([128, 4, 512], BF16)
        if is_s:
            h0, h0_r = arena.alloc([128, 2, 16, 16], F32)
            kb.dma(kb.qs, h0.rearrange("p a b c -> p (a b c)"), I["s5h0"], W=[h0_r])
        sv = lambda ap: ap.rearrange("p (s l) -> p s l", s=nseg)

        def sb_gen(sb_, k):
            q, sbq = sb_ // 4, sb_ % 4
            w, w_r = wsets[k]
            tmpc, tmpc_r = tmpcs[k]
            W_ = lambda i: w[:, i, :ntok]
            psr, psr_r = psD512()
            PE(lambda e: e.matmul(psr[:, :ntok], lhsT=lhsTB[:, 0, sb_, :], rhs=uaT[:, q, :ntok], start=True, stop=True),
               R=[lhsTB_r, uaT_r], W=[psr_r])
            psi, psi_r = psD512()
            PE(lambda e: e.matmul(psi[:, :ntok], lhsT=lhsTB[:, 1, sb_, :], rhs=uaT[:, q, :ntok], start=True, stop=True),
               R=[lhsTB_r, uaT_r], W=[psi_r])
            Ec = Ecs[:, 0, sb_, 0:L].unsqueeze(1).to_broadcast([128, nseg, L])
            Es = Ecs[:, 1, sb_, 0:L].unsqueeze(1).to_broadcast([128, nseg, L])

            def tt(eng, o, oi, a, ar, b, op):
                eng(lambda e: e.tensor_tensor(out=sv(o), in0=sv(a), in1=b, op=op), R=[ar, Ecs_r], W=[w_r[oi]])

            def t2(eng, oi, ai, bi, op):
                eng(lambda e: e.tensor_tensor(out=W_(oi), in0=W_(ai), in1=W_(bi), op=op), R=[w_r[ai], w_r[bi]], W=[w_r[oi]])
            ACT(lambda e: e.activation(out=W_(4), in_=psr[:, :ntok], func=AF.Copy), R=[psr_r], W=[w_r[4]])
            ACT(lambda e: e.activation(out=W_(5), in_=psi[:, :ntok], func=AF.Copy), R=[psi_r], W=[w_r[5]])
            yield
            tt(DVE, W_(0), 0, W_(4), w_r[4], Ec, ALU.mult)
            tt(POOL, W_(1), 1, W_(5), w_r[5], Es, ALU.mult)
            tt(POOL, W_(2), 2, W_(5), w_r[5], Ec, ALU.mult)
            tt(DVE, W_(3), 3, W_(4), w_r[4], Es, ALU.mult)
            yield
            t2(POOL, 0, 0, 1, ALU.add)
            t2(POOL, 2, 2, 3, ALU.subtract)
            yield
            absa = s5c[:, 0, sb_:sb_ + 1]
            if not is_s:
                for s_ in range(nseg):
                    c0 = s_ * L
                    ir, ii = s5k[:, 0, sb_, 0:1], s5k[:, 0, sb_, 1:2]
                    DVE(lambda e: e.tensor_tensor_scan(out=w[:, 4, c0:c0 + L], data0=absa.to_broadcast([128, L]), data1=w[:, 0, c0:c0 + L],
                                                       initial=ir, op0=ALU.mult, op1=ALU.add), R=[s5c_r, w_r[0], s5k_r], W=[w_r[4]])
                    DVE(lambda e: e.tensor_tensor_scan(out=w[:, 5, c0:c0 + L], data0=absa.to_broadcast([128, L]), data1=w[:, 2, c0:c0 + L],
                                                       initial=ii, op0=ALU.mult, op1=ALU.add), R=[s5c_r, w_r[2], s5k_r], W=[w_r[5]])
                    wre, wie = w[:, 4, c0 + L - 1:c0 + L], w[:, 5, c0 + L - 1:c0 + L]
                    DVE(lambda e: e.tensor_scalar(out=tmpc[:, 0, 0:2], in0=s5k[:, 2, sb_, :], scalar1=wie, scalar2=None, op0=ALU.mult),
                        R=[w_r[5], s5k_r], W=[tmpc_r])
                    DVE(lambda e: e.scalar_tensor_tensor(out=s5k[:, 0, sb_, :], in0=s5k[:, 1, sb_, :], scalar=wre, in1=tmpc[:, 0, 0:2],
                                                         op0=ALU.mult, op1=ALU.add), R=[w_r[4], s5k_r, tmpc_r], W=[s5k_r])
                    yield
            else:
                d0, d0_i = w[:, 1, :ntok], 1
                DVE(lambda e: e.tensor_scalar(out=d0, in0=cstb[:, CB["rst_s"]:CB["rst_s"] + 128], scalar1=absa, scalar2=None, op0=ALU.mult),
                    R=[cstb_r, s5c_r], W=[w_r[1]])
                for ci_, wi_ in ((0, 0), (1, 2)):
                    v0 = w[:, wi_, :ntok].rearrange("p (s l) -> p s l", s=16)[:, :, 0]
                    DVE(lambda e: e.scalar_tensor_tensor(out=v0, in0=h0[:, ci_, sb_, :], scalar=absa, in1=v0, op0=ALU.mult, op1=ALU.add),
                        R=[h0_r, s5c_r, w_r[wi_]], W=[w_r[wi_]])
                yield
                DVE(lambda e: e.tensor_tensor_scan(out=w[:, 4, :ntok], data0=d0, data1=w[:, 0, :ntok], initial=0.0, op0=ALU.mult, op1=ALU.add),
                    R=[w_r[1], w_r[0]], W=[w_r[4]])
                DVE(lambda e: e.tensor_tensor_scan(out=w[:, 5, :ntok], data0=d0, data1=w[:, 2, :ntok], initial=0.0, op0=ALU.mult, op1=ALU.add),
                    R=[w_r[1], w_r[2]], W=[w_r[5]])
                yield
                cc, ss = Ecs[:, 0, sb_, L - 1:L], Ecs[:, 1, sb_, L - 1:L]
                wre = w[:, 4, :ntok].rearrange("p (s l) -> p s l", s=16)[:, :, L - 1]
                wie = w[:, 5, :ntok].rearrange("p (s l) -> p s l", s=16)[:, :, L - 1]
                DVE(lambda e: e.tensor_scalar(out=tmpc[:, 0, :], in0=wie, scalar1=ss, scalar2=None, op0=ALU.mult), R=[w_r[5], Ecs_r], W=[tmpc_r])
                DVE(lambda e: e.tensor_scalar(out=tmpc[:, 1, :], in0=wie, scalar1=cc, scalar2=None, op0=ALU.mult), R=[w_r[5], Ecs_r], W=[tmpc_r])
                DVE(lambda e: e.scalar_tensor_tensor(out=s5o[:, 0, sb_, 1:17], in0=wre, scalar=cc, in1=tmpc[:, 0, :],
                                                     op0=ALU.mult, op1=ALU.subtract), R=[w_r[4], Ecs_r, tmpc_r], W=[s5o_r])
                DVE(lambda e: e.scalar_tensor_tensor(out=s5o[:, 1, sb_, 1:17], in0=wre, scalar=ss, in1=tmpc[:, 1, :],
                                                     op0=ALU.mult, op1=ALU.add), R=[w_r[4], Ecs_r, tmpc_r], W=[s5o_r])
                yield
            tt(POOL, W_(0), 0, W_(4), w_r[4], Ec, ALU.mult)
            tt(DVE, W_(1), 1, W_(5), w_r[5], Es, ALU.mult)
            tt(POOL, W_(2), 2, W_(4), w_r[4], Es, ALU.mult)
            tt(DVE, W_(3), 3, W_(5), w_r[5], Ec, ALU.mult)
            yield
            POOL(lambda e: e.tensor_tensor(out=XR[:, 0, sbq, :ntok], in0=W_(0), in1=W_(1), op=ALU.subtract), R=[w_r[0], w_r[1]], W=[XR_r])
            POOL(lambda e: e.tensor_tensor(out=XR[:, 1, sbq, :ntok], in0=W_(2), in1=W_(3), op=ALU.add), R=[w_r[2], w_r[3]], W=[XR_r])
            yield

        def y_mm(q):
            ps, pres = psD512()
            for sbq in range(4):
                for ri in range(2):
                    PE(lambda e: e.matmul(ps[:, :ntok], lhsT=lhsTC[:, ri, 4 * q + sbq, :], rhs=XR[:, ri, sbq, :ntok],
                                          start=(sbq == 0 and ri == 0), stop=(sbq == 3 and ri == 1)), R=[lhsTC_r, XR_r], W=[pres])
            return ps, pres

        def y_tail(q, ps, pres):
            Y0, Y1 = yt[:, 0, :ntok], yt[:, 1, :ntok]
            DVE(lambda e: e.scalar_tensor_tensor(out=Y0, in0=uaT[:, q, :ntok], scalar=s5c[:, 1, q:q + 1], in1=ps[:, :ntok],
                                                 op0=ALU.mult, op1=ALU.add), R=[uaT_r, s5c_r, pres], W=[yt_r[0]])
            yield
            ACT(lambda e: e.activation(out=Y1, in_=Y0, func=AF.Square), R=[yt_r[0]], W=[yt_r[1]])
            yield
            DVE(lambda e: e.tensor_scalar(out=Y1, in0=Y1, scalar1=0.044715, scalar2=1.0, op0=ALU.mult, op1=ALU.add), R=[yt_r[1]], W=[yt_r[1]])
            DVE(lambda e: e.tensor_tensor(out=Y1, in0=Y1, in1=Y0, op=ALU.mult), R=[yt_r[1], yt_r[0]], W=[yt_r[1]])
            yield
            ACT(lambda e: e.activation(out=Y1, in_=Y1, func=AF.Sigmoid, scale=1.5957691216), R=[yt_r[1]], W=[yt_r[1]])
            yield
            DVE(lambda e: e.tensor_tensor(out=ygT[:, q, :ntok], in0=Y0, in1=Y1, op=ALU.mult), R=[yt_r[0], yt_r[1]], W=[ygT_r])
            yield

        tail = []
        for q in range(4):
            run_gens([sb_gen(4 * q, 0), sb_gen(4 * q + 1, 1)] + tail)
            tail = []
            run_gens([sb_gen(4 * q + 2, 0), sb_gen(4 * q + 3, 1)])
            ps, pres = y_mm(q)
            tail = [y_tail(q, ps, pres)]
        run_gens(tail)
        if tb == 3:
            DVE(lambda e: e.tensor_copy(out=s5o[:, :, :, 0], in_=s5k[:, 0].rearrange("p s c -> p c s")), R=[s5k_r], W=[s5o_r])

        def cb_glu(ps, pres, col, m):
            c = col // 128
            ACT(lambda e: e.activation(out=yt[:, 0, :ntok], in_=ps[:, :ntok], func=AF.Sigmoid), R=[pres], W=[yt_r[0]])
            DVE(lambda e: e.tensor_tensor(out=catT[:, c, :ntok], in0=ygT[:, c, :ntok], in1=yt[:, 0, :ntok], op=ALU.mult),
                R=[ygT_r, yt_r[0]], W=[catT_r])
        linear_fm(I["W_glu"], 512, 0, 512, [(ygT[:, c, :], ygT_r) for c in range(4)], ntok, cb_glu)

    def layer0(tb, tok0, ntok, is_s):
        arena.reset()
        nseg, L = (16, 8) if is_s else (4, 128)
        nt = ntok // 128
        sq_dummy = None
        qkvs, qkvs_r = arena.alloc([128, 12, 512], BF16)
        zs, zs_r = arena.alloc([128, 4, 512], BF16)
        catT, catT_r = arena.alloc([128, 8, 512], BF16)
        bd, bd_r = arena.alloc([8, 3, 512], F32)
        mark = arena.off
        XR, XR_r = arena.alloc([128, 2, 4, 512], BF16)
        rmsnorm(0, ntok, scratch=(XR.rearrange("p a b c -> p (a b) c"), XR_r))
        uaT, uaT_r = arena.alloc([128, 4, 512], BF16)
        hact = [(hT[:, c, :], hT_r) for c in range(8)]

        def cb_ua(ps, pres, col, m):
            ACT(lambda e: e.activation(out=uaT[:, col // 128, :ntok], in_=ps[:, :ntok], func=AF.Copy), R=[pres], W=[uaT_r])
        linear_fm(I["W_in"], D, 0, 512, hact, ntok, cb_ua)
        s5_tb(tb, ntok, is_s, uaT, uaT_r, catT, catT_r, XR, XR_r)
        kb.barrier()
        arena.off = mark
        nsc, Lc_ = (16, 8) if is_s else (1, 512)
        pre, pre_r = arena.alloc([128, 2, 528], F32, nres=2)
        cacc, cacc_r = arena.alloc([128, 512], F32)
        if is_s:
            c0t, c0t_r = arena.alloc([128, 12, 16, 3], F32)
            kb.dma(kb.qs, c0t.rearrange("p a b c -> p (a b c)"), I["gconv0"], W=[c0t_r])
        mark_front = arena.off
        extra_sl = [arena.alloc([128, 4096], BF16) for _ in range(4)]
        set_pool(list(wsl) + extra_sl)
        cnt = [0]

        def cb_qkv(ps, pres, col, m):
            c = (col - 512) // 128
            i = cnt[0] % 2
            cnt[0] += 1
            pv = pre[:, i, :nsc * (3 + Lc_)].rearrange("p (s l) -> p s l", s=nsc)
            if is_s:
                POOL(lambda e: e.tensor_copy(out=pv[:, :, 0:3], in_=c0t[:, c, :, :]), R=[c0t_r], W=[pre_r[i]])
            else:
                POOL(lambda e: e.tensor_copy(out=pv[:, 0, 0:3], in_=hist[:, c, :]), R=[hist_r], W=[pre_r[i]])
            ACT(lambda e: e.activation(out=pv[:, :, 3:3 + Lc_], in_=ps[:, :ntok].rearrange("p (s l) -> p s l", s=nsc), func=AF.Copy),
                R=[pres], W=[pre_r[i]])
            if is_s:
                POOL(lambda e: e.tensor_copy(out=oconv[:, c, 1:17, :], in_=pv[:, :, Lc_:Lc_ + 3]), R=[pre_r[i]], W=[oconv_r])
            else:
                POOL(lambda e: e.tensor_copy(out=hist[:, c, :], in_=pv[:, 0, Lc_:Lc_ + 3]), R=[pre_r[i]], W=[hist_r])
                if tb == 3:
                    POOL(lambda e: e.tensor_copy(out=oconv[:, c, 0, :], in_=pv[:, 0, Lc_:Lc_ + 3]), R=[pre_r[i]], W=[oconv_r])
            av = cacc[:, :ntok].rearrange("p (s l) -> p s l", s=nsc)
            DVE(lambda e: e.tensor_scalar(out=av, in0=pv[:, :, 0:Lc_], scalar1=gp[:, c * 4:c * 4 + 1], scalar2=None, op0=ALU.mult),
                R=[pre_r[i], gp_r], W=[cacc_r])
            for r in range(1, 4):
                DVE(lambda e: e.scalar_tensor_tensor(out=av, in0=pv[:, :, r:r + Lc_], scalar=gp[:, c * 4 + r:c * 4 + r + 1], in1=av,
                                                     op0=ALU.mult, op1=ALU.add), R=[pre_r[i], gp_r, cacc_r], W=[cacc_r])
            ACT(lambda e: e.activation(out=qkvs[:, c, :ntok], in_=cacc[:, :ntok], func=AF.Silu), R=[cacc_r], W=[qkvs_r])
        linear_fm(I["W_in"], D, 512, 1536, hact, ntok, cb_qkv)

        def cb_bd(ps, pres, col, m):
            ACT(lambda e: e.activation(out=bd[:, 0, :ntok], in_=ps[:8, :ntok], func=AF.Sigmoid), R=[pres], W=[bd_r])
            ACT(lambda e: e.activation(out=bd[:, 2, :ntok], in_=ps[:8, :ntok], func=AF.Exp, bias=gp[:8, 50:51], scale=1.0), R=[pres, gp_r], W=[bd_r])
            ACT(lambda e: e.activation(out=bd[:, 2, :ntok], in_=bd[:, 2, :ntok], func=AF.Ln, bias=kc[:8, 2:3], scale=1.0), R=[bd_r, kc_r], W=[bd_r])
            DVE(lambda e: e.tensor_scalar(out=bd[:, 1, :ntok], in0=bd[:, 2, :ntok], scalar1=negA8[:, 0:1], scalar2=None, op0=ALU.mult),
                R=[bd_r, negA8_r], W=[bd_r])
        linear_fm(I["W_in"], D, 2048, 8, hact, ntok, cb_bd)

        def cb_z(ps, pres, col, m):
            ACT(lambda e: e.activation(out=zs[:, (col - 2056) // 128, :ntok], in_=ps[:, :ntok], func=AF.Silu), R=[pres], W=[zs_r])
        linear_fm(I["W_in"], D, 2056, 512, hact, ntok, cb_z)
        kb.barrier()
        set_pool(wsl)
        arena.off = mark_front

        kTn, kTn_r = arena.alloc([128, 4, 512], BF16)
        qTn, qTn_r = arena.alloc([128, 4, 512], BF16)
        tb16, tb16_r = arena.alloc([128, 512], BF16)
        NH = 1 if is_s else 2
        hp_ = [dict() for _ in range(NH)]
        for s_ in hp_:
            s_["f"], s_["f_r"] = arena.alloc([128, 3, 512], F32, nres=3)
            s_["b"], s_["b_r"] = arena.alloc([128, 5, 512], BF16, nres=5)
            s_["e"], s_["e_r"] = arena.alloc([128, 2, 16], F32)
            s_["o"], s_["o_r"] = arena.alloc([128, 512], F32)
            s_["d"], s_["d_r"] = arena.alloc([128, 3, 128], F32, nres=3)
            s_["c"], s_["c_r"] = arena.alloc([128, 4], F32)
        if is_s:
            S0f, S0f_r = arena.alloc([128, 2048], F32)
            S0b, _ = arena.alloc([128, 2048], BF16)
        for h in range(4):
            ACT(lambda e: e.activation(out=tb16[:, :ntok], in_=qkvs[:, 4 + h, :ntok], func=AF.Square), R=[qkvs_r], W=[tb16_r])
            ps, pres = psA()
            PE(lambda e: e.matmul(ps[:, :ntok], lhsT=onesb, rhs=tb16[:, :ntok], start=True, stop=True), R=[cstb_r, tb16_r], W=[pres])
            rsqrt_into(rstd[:, :ntok], rstd_r, ps[:, :ntok], pres, 1.0, 1, ntok)
            DVE(lambda e: e.tensor_tensor(out=kTn[:, h, :ntok], in0=qkvs[:, 4 + h, :ntok], in1=rstd[:, :ntok], op=ALU.mult),
                R=[qkvs_r, rstd_r], W=[kTn_r])
            ACT(lambda e: e.activation(out=tb16[:, :ntok], in_=qkvs[:, h, :ntok], func=AF.Square), R=[qkvs_r], W=[tb16_r])
            ps, pres = psA()
            PE(lambda e: e.matmul(ps[:, :ntok], lhsT=onesb, rhs=tb16[:, :ntok], start=True, stop=True), R=[cstb_r, tb16_r], W=[pres])
            rsqrt_into(rstd[:, :ntok], rstd_r, ps[:, :ntok], pres, 1.0, 1, ntok)
            DVE(lambda e: e.scalar_tensor_tensor(out=qTn[:, h, :ntok], in0=qkvs[:, h, :ntok], scalar=128.0 ** -0.5, in1=rstd[:, :ntok],
                                                 op0=ALU.mult, op1=ALU.mult), R=[qkvs_r, rstd_r], W=[qTn_r])
        rst = cstb[:, CB["rst_s"]:CB["rst_s"] + 128] if is_s else cstb[:, CB["rst_p"]:CB["rst_p"] + 512]
        MsT = cb_("MsT_s") if is_s else cb_("MsT_p")
        NEG = cb_("NEG_s") if is_s else cb_("NEG_p")

        def prep_head(h, s_):
            f, f_r, b, b_r = s_["f"], s_["f_r"], s_["b"], s_["b_r"]
            psb, psb_r = psA()
            PE(lambda e: e.matmul(psb[:, :ntok], lhsT=sel8[:8, h * 128:(h + 1) * 128], rhs=bd[:, 0, :ntok], start=True, stop=True),
               R=[sel8_r, bd_r], W=[psb_r])
            ACT(lambda e: e.activation(out=f[:, 0, :ntok], in_=psb[:, :ntok], func=AF.Copy), R=[psb_r], W=[f_r[0]])
            psg, psg_r = psA()
            PE(lambda e: e.matmul(psg[:, :ntok], lhsT=sel8[:8, (4 + h) * 128:(5 + h) * 128], rhs=bd[:, 1, :ntok], start=True, stop=True),
               R=[sel8_r, bd_r], W=[psg_r])
            DVE(lambda e: e.tensor_tensor_scan(out=f[:, 1, :ntok], data0=rst[:, :ntok], data1=psg[:, :ntok], initial=0.0,
                                               op0=ALU.mult, op1=ALU.add), R=[cstb_r, psg_r], W=[f_r[1]])
            ACT(lambda e: e.activation(out=f[:, 2, :ntok], in_=f[:, 1, :ntok], func=AF.Exp), R=[f_r[1]], W=[f_r[2]])
            DVE(lambda e: e.tensor_tensor(out=b[:, 0, :ntok], in0=kTn[:, h, :ntok], in1=f[:, 2, :ntok], op=ALU.mult), R=[kTn_r, f_r[2]], W=[b_r[0]])
            DVE(lambda e: e.tensor_tensor(out=b[:, 1, :ntok], in0=qTn[:, h, :ntok], in1=f[:, 2, :ntok], op=ALU.mult), R=[qTn_r, f_r[2]], W=[b_r[0]])
            Lcv = f[:, 1, :ntok].rearrange("p (s l) -> p s l", s=nseg)
            DVE(lambda e: e.tensor_copy(out=s_["e"][:, 0, :nseg], in_=Lcv[:, :, L - 1]), R=[f_r[1]], W=[s_["e_r"]])
            ACT(lambda e: e.activation(out=s_["e"][:, 1, :nseg], in_=s_["e"][:, 0, :nseg], func=AF.Exp), R=[s_["e_r"]], W=[s_["e_r"]])
            DVE(lambda e: e.tensor_tensor(out=f[:, 2, :ntok].rearrange("p (s l) -> p s l", s=nseg),
                                          in0=s_["e"][:, 0, :nseg].unsqueeze(2).to_broadcast([128, nseg, L]), in1=Lcv, op=ALU.subtract),
                R=[s_["e_r"], f_r[1]], W=[f_r[2]])
            ACT(lambda e: e.activation(out=f[:, 2, :ntok], in_=f[:, 2, :ntok], func=AF.Exp), R=[f_r[2]], W=[f_r[2]])
            DVE(lambda e: e.tensor_tensor(out=b[:, 2, :ntok], in0=kTn[:, h, :ntok], in1=f[:, 2, :ntok], op=ALU.mult), R=[kTn_r, f_r[2]], W=[b_r[2]])
            DVE(lambda e: e.scalar_tensor_tensor(out=b[:, 3, :ntok], in0=b[:, 2, :ntok], scalar=-1.0, in1=f[:, 0, :ntok],
                                                 op0=ALU.mult, op1=ALU.mult), R=[b_r[2], f_r[0]], W=[b_r[3]])
            DVE(lambda e: e.tensor_tensor(out=b[:, 4, :ntok], in0=qkvs[:, 8 + h, :ntok], in1=f[:, 0, :ntok], op=ALU.mult), R=[qkvs_r, f_r[0]], W=[b_r[4]])

        def gdn_run(hs, Ss):
            def emit_XT(ti, hi, xt, xt_r, mats):
                h, s_ = hs[hi], hp_[hi]
                f, f_r, d, d_r = s_["f"], s_["f_r"], s_["d"], s_["d_r"]
                cs = slice(ti * 128, (ti + 1) * 128)
                DVE(lambda e: e.scalar_tensor_tensor(out=d[:, 0, :], in0=f[:, 1, cs], scalar=-1.0, in1=ident, op0=ALU.mult, op1=ALU.mult,
                                                     accum_out=s_["c"][:, 0:1]), R=[f_r[1], cstf_r], W=[d_r[0], s_["c_r"]])
                DVE(lambda e: e.scalar_tensor_tensor(out=d[:, 0, :], in0=f[:, 0, cs], scalar=-1.0, in1=ident, op0=ALU.mult, op1=ALU.mult,
                                                     accum_out=s_["c"][:, 1:2]), R=[f_r[0], cstf_r, d_r[0]], W=[d_r[0], s_["c_r"]])
                DVE(lambda e: e.tensor_tensor(out=d[:, 0, :], in0=f[:, 1, cs], in1=NEG, op=ALU.add), R=[f_r[1], cstb_r], W=[d_r[0]])
                ACT(lambda e: e.activation(out=d[:, 1, :], in_=d[:, 0, :], func=AF.Exp, bias=s_["c"][:, 0:1], scale=1.0),
                    R=[d_r[0], s_["c_r"]], W=[d_r[1]])
                DVE(lambda e: e.tensor_tensor(out=d[:, 2, :], in0=d[:, 1, :], in1=MsT, op=ALU.mult), R=[d_r[1], cstb_r], W=[d_r[2]])
                p1, p1_r = psD()
                PE(lambda e: e.matmul(p1, lhsT=kTn[:, h, cs], rhs=kTn[:, h, cs], start=True, stop=True), R=[kTn_r], W=[p1_r])
                DVE(lambda e: e.scalar_tensor_tensor(out=xt, in0=p1, scalar=s_["c"][:, 1:2], in1=d[:, 2, :], op0=ALU.mult, op1=ALU.mult),
                    R=[p1_r, s_["c_r"], d_r[2]], W=[xt_r])
                DVE(lambda e: e.tensor_tensor(out=mats[0][0], in0=p1, in1=d[:, 2, :], op=ALU.mult), R=[p1_r, d_r[2]], W=[mats[0][1]])
                p2, p2_r = psD()
                PE(lambda e: e.matmul(p2, lhsT=kTn[:, h, cs], rhs=qTn[:, h, cs], start=True, stop=True), R=[kTn_r, qTn_r], W=[p2_r])
                DVE(lambda e: e.scalar_tensor_tensor(out=mats[1][0], in0=p2, scalar=s_["c"][:, 1:2], in1=d[:, 1, :], op0=ALU.mult, op1=ALU.mult),
                    R=[p2_r, s_["c_r"], d_r[1]], W=[mats[1][1]])
                DVE(lambda e: e.tensor_tensor(out=mats[2][0], in0=p2, in1=d[:, 1, :], op=ALU.mult), R=[p2_r, d_r[1]], W=[mats[2][1]])

            def emit_tm(ti):
                cs = slice(ti * 128, (ti + 1) * 128)
                out = []
                for hi in range(len(hs)):
                    b, b_r = hp_[hi]["b"], hp_[hi]["b_r"]
                    i0 = ((ti % 2) * 2 + hi) * 3
                    pt, pt_r = psT_full()
                    for k, src in enumerate((4, 3, 2)):
                        PE(lambda e: e.transpose(pt[:, k * 128:(k + 1) * 128], b[:, src, cs], identb[:]), R=[b_r[src], identb_r], W=[pt_r])
                    ACT(lambda e: e.activation(out=tmB[:, i0:i0 + 3, :], in_=pt[:, 0:384].rearrange("p (k b) -> p k b", k=3), func=AF.Copy),
                        R=[pt_r], W=[tmB_r[i0], tmB_r[i0 + 1], tmB_r[i0 + 2]])
                    out.append([(tmB[:, i0 + k, :], tmB_r[i0 + k]) for k in range(3)])
                return out

            def ops(ti, hi):
                b, b_r = hp_[hi]["b"], hp_[hi]["b_r"]
                cs = slice(ti * 128, (ti + 1) * 128)
                return b[:, 0, cs], b[:, 1, cs], b_r[0]

            def eT_of(ti, hi):
                s_ = hp_[hi]
                return (s_["e"][:, 1, ti:ti + 1] if not is_s else s_["e"][:, 1, :16]), s_["e_r"]

            def ycb_of(ti, hi):
                s_ = hp_[hi]
                cs = slice(ti * 128, (ti + 1) * 128)

                def ycb(yo, pres):
                    ACT(lambda e: e.activation(out=s_["o"][:, cs], in_=yo, func=AF.Copy), R=[pres], W=[s_["o_r"]])
                return ycb
            dplr_run(nt, [0] * len(hs), 128, 128, is_s, emit_XT, emit_tm, ops, lambda hi: Ss[hi], eT_of, ycb_of)

        def head_finish(h, s_):
            f, f_r = s_["f"], s_["f_r"]
            ACT(lambda e: e.activation(out=tb16[:, :ntok], in_=s_["o"][:, :ntok], func=AF.Square), R=[s_["o_r"]], W=[tb16_r])
            ps, pres = psA()
            PE(lambda e: e.matmul(ps[:, :ntok], lhsT=onesb, rhs=tb16[:, :ntok], start=True, stop=True), R=[cstb_r, tb16_r], W=[pres])
            rsqrt_into(rstd[:, :ntok], rstd_r, ps[:, :ntok], pres, 1.0 / 128, 1, ntok)
            DVE(lambda e: e.scalar_tensor_tensor(out=f[:, 2, :ntok], in0=s_["o"][:, :ntok], scalar=gp[:, 48:49], in1=rstd[:, :ntok],
                                                 op0=ALU.mult, op1=ALU.mult), R=[s_["o_r"], gp_r, rstd_r], W=[f_r[2]])
            DVE(lambda e: e.tensor_tensor(out=catT[:, 4 + h, :ntok], in0=f[:, 2, :ntok], in1=zs[:, h, :ntok], op=ALU.mult),
                R=[f_r[2], zs_r], W=[catT_r])

        for h0_ in range(0, 4, NH):
            hs = list(range(h0_, h0_ + NH))
            Ss = []
            for i, h in enumerate(hs):
                prep_head(h, hp_[i])
                if is_s:
                    kb.dma(kb.qs, S0f, I["gs0"][h], W=[S0f_r])
                    kb.dma(kb.qg, S0b, I["gs0"][h], W=[S0f_r])
                    Ss.append(dict(f=S0f, b=S0b, r=S0f_r))
                else:
                    Ss.append(dict(f=Sg_f[:, h, :], b=Sg_b[:, h, :], r=Sg_r[h]))
            gdn_run(hs, Ss)
            for i, h in enumerate(hs):
                head_finish(h, hp_[i])
                if is_s:
                    kb.dma(kb.qs, O["o_gdn_s"][h], S0f, R=[S0f_r])
                elif tb == 3:
                    kb.dma(kb.qs, O["o_gdn_p"][h], Sg_f[:, h, :], R=[Sg_r[h]])
        linear_fm(I["W_out"], D, 0, D, [(catT[:, c, :], catT_r) for c in range(8)], ntok, resid_add_cb(ntok))

    rwp, rwp_r = kb.sb("rwp", [128, 104], F32)
    kb.dma(kb.qs, rwp[:], I["rwp"], W=[rwp_r])
    hprev, hprev_r = kb.sb("hprev", [128, 8, 1], F32)
    DVE(lambda e: e.memset(hprev[:], 0.0), W=[hprev_r])
    oshift, oshift_r = kb.sb("oshift", [128, 8, 17], F32)
    Sr_f, _ = kb.sb("Sr_f", [128, 8, 64], F32)
    Sr_b, _ = kb.sb("Sr_b", [128, 8, 64], BF16)
    Sr_r = [Res() for _ in range(8)]
    DVE(lambda e: e.memset(Sr_f[:], 0.0), W=Sr_r)
    DVE(lambda e: e.memset(Sr_b[:], 0.0), W=Sr_r)

    def layer1(tb, tok0, ntok, is_s):
        arena.reset()
        nseg, L = (16, 8) if is_s else (4, 128)
        nsc, Lc_ = (16, 8) if is_s else (1, 512)
        nt = ntok // 128
        mark1 = arena.off
        rmsnorm(2, ntok)
        kb.barrier()
        arena.off = mark1
        mix, mix_r = arena.alloc([128, 3, 8, 512], BF16, nres=3)
        lora, lora_r = arena.alloc([128, 4, 512], BF16)
        ygT, ygT_r = arena.alloc([128, 8, 512], BF16)
        def mixes(js):
            for c in range(8):
                i = c % 2
                hv = hF[:, i, :nsc * (1 + Lc_)].rearrange("p (s l) -> p s l", s=nsc)
                DVE(lambda e: e.scalar_tensor_tensor(out=hv[:, :, 1:1 + Lc_], in0=xT[:, c, :ntok].rearrange("p (s l) -> p s l", s=nsc),
                                                     scalar=nw[:, 16 + c:17 + c], in1=rstd[:, :ntok].rearrange("p (s l) -> p s l", s=nsc),
                                                     op0=ALU.mult, op1=ALU.mult), R=[xT_r, nw_r, rstd_r], W=[hF_r[i]])
                if is_s:
                    POOL(lambda e: e.tensor_copy(out=hv[:, :, 0], in_=sh0[:, c, :]), R=[sh0_r], W=[hF_r[i]])
                else:
                    POOL(lambda e: e.tensor_copy(out=hv[:, 0, 0:1], in_=hprev[:, c, :]), R=[hprev_r], W=[hF_r[i]])
                if js[0] == 0:
                    if is_s:
                        POOL(lambda e: e.tensor_copy(out=oshift[:, c, 1:17], in_=hv[:, :, Lc_]), R=[hF_r[i]], W=[oshift_r])
                    else:
                        POOL(lambda e: e.tensor_copy(out=hprev[:, c, :], in_=hv[:, 0, Lc_:Lc_ + 1]), R=[hF_r[i]], W=[hprev_r])
                        if tb == 3:
                            POOL(lambda e: e.tensor_copy(out=oshift[:, c, 0:1], in_=hv[:, 0, Lc_:Lc_ + 1]), R=[hF_r[i]], W=[oshift_r])
                xv = xx[:, i, :ntok].rearrange("p (s l) -> p s l", s=nsc)
                DVE(lambda e: e.tensor_tensor(out=xv, in0=hv[:, :, 0:Lc_], in1=hv[:, :, 1:1 + Lc_], op=ALU.subtract), R=[hF_r[i]], W=[xx_r[i]])
                for k_, j in enumerate(js):
                    DVE(lambda e: e.scalar_tensor_tensor(out=mix[:, k_, c, :ntok].rearrange("p (s l) -> p s l", s=nsc), in0=xv,
                                                         scalar=rwp[:, 56 + j * 8 + c:57 + j * 8 + c], in1=hv[:, :, 1:1 + Lc_],
                                                         op0=ALU.mult, op1=ALU.add), R=[xx_r[i], rwp_r, hF_r[i]], W=[mix_r[k_]])
        if is_s:
            sh0, sh0_r = arena.alloc([128, 8, 16], F32)
            kb.dma(kb.qs, sh0.rearrange("p a b -> p (a b)"), I["rshift0"], W=[sh0_r])
        mark2 = arena.off
        hF, hF_r = arena.alloc([128, 2, 520], F32, nres=2)
        xx, xx_r = arena.alloc([128, 2, 512], F32, nres=2)
        mixes([1, 4, 5])
        act_of = lambda k_: [(mix[:, k_, c, :], mix_r[k_]) for c in range(8)]

        def cb_hw(ps, pres, col, m):
            ACT(lambda e: e.activation(out=lora[:64, 0, :ntok], in_=ps[:64, :ntok], func=AF.Tanh), R=[pres], W=[lora_r])
        linear_fm(I["w1"], D, 0, 64, act_of(0), ntok, cb_hw)

        def cb_ha(ps, pres, col, m):
            ACT(lambda e: e.activation(out=lora[:64, 1, :ntok], in_=ps[:64, :ntok], func=AF.Copy), R=[pres], W=[lora_r])
        linear_fm(I["a1"], D, 0, 64, act_of(1), ntok, cb_ha)

        def cb_hg(ps, pres, col, m):
            ACT(lambda e: e.activation(out=lora[:m, 2 + col // 128, :ntok], in_=ps[:m, :ntok], func=AF.Sigmoid), R=[pres], W=[lora_r])
        linear_fm(I["g1"], D, 0, 160, act_of(2), ntok, cb_hg)
        mixes([0, 2, 3])
        kb.barrier()
        arena.off = mark2

        P_, P_r = arena.alloc([128, 10, 512], F32, nres=10)
        NL = 1 if is_s else 2
        Ls = []
        for _i in range(NL):
            d_ = {}
            d_["Bq"], d_["Bq_r"] = arena.alloc([128, 7, 512], BF16, nres=7)
            d_["yF"], d_["yF_r"] = arena.alloc([128, 512], F32)
            d_["rg"], d_["rg_r"] = arena.alloc([128, 2, 512], BF16, nres=2)
            d_["eL"], d_["eL_r"] = arena.alloc([128, 2, 16], F32)
            Ls.append(d_)
        if is_s:
            S0f, S0f_r = arena.alloc([128, 1024], F32)
            S0b, _ = arena.alloc([128, 1024], BF16)
        rst = cstb[:, CB["rst_s"]:CB["rst_s"] + 128] if is_s else cstb[:, CB["rst_p"]:CB["rst_p"] + 512]
        MsT = cb_("MsT_s") if is_s else cb_("MsT_p")
        MiT = cb_("MiT_s") if is_s else cb_("MiT_p")
        PV = lambda i: P_[:, i, :ntok]

        def prep_gen(hp, Lh):
            Bq, Bq_r, eL, eL_r, rg, rg_r = Lh["Bq"], Lh["Bq_r"], Lh["eL"], Lh["eL_r"], Lh["rg"], Lh["rg_r"]
            col = lambda i: rwp[:, hp * 7 + i:hp * 7 + i + 1]

            def cp(i):
                def cb(ps, pres, c_, m):
                    ACT(lambda e: e.activation(out=PV(i), in_=ps[:, :ntok], func=AF.Copy), R=[pres], W=[P_r[i]])
                return cb
            linear_fm(I["W_r"], D, hp * 128, 128, act_of(0), ntok, cp(0))
            yield
            linear_fm(I["W_k"], D, hp * 128, 128, act_of(1), ntok, cp(1))
            yield
            linear_fm(I["W_v"], D, hp * 128, 128, act_of(2), ntok, cp(2))
            yield

            def cb_w(ps, pres, c_, m):
                ACT(lambda e: e.activation(out=PV(3), in_=ps[:, :ntok], func=AF.Sigmoid, bias=col(0), scale=1.0), R=[pres, rwp_r], W=[P_r[3]])
                DVE(lambda e: e.tensor_scalar(out=PV(3), in0=PV(3), scalar1=-EXPM05, scalar2=None, op0=ALU.mult), R=[P_r[3]], W=[P_r[3]])
            linear_fm(I["w2"], 64, hp * 128, 128, [(lora[:, 0, :], lora_r)], ntok, cb_w)

            def cb_a(ps, pres, c_, m):
                ACT(lambda e: e.activation(out=PV(4), in_=ps[:, :ntok], func=AF.Sigmoid, bias=col(1), scale=1.0), R=[pres, rwp_r], W=[P_r[4]])
            linear_fm(I["a2"], 64, hp * 128, 128, [(lora[:, 1, :], lora_r)], ntok, cb_a)
            yield

            def cb_g(ps, pres, c_, m):
                ACT(lambda e: e.activation(out=rg[:, 1, :ntok], in_=ps[:, :ntok], func=AF.Copy), R=[pres], W=[rg_r[1]])
            linear_fm(I["g2"], 160, hp * 128, 128, [(lora[:, 2, :], lora_r), (lora[:, 3, :], lora_r)], ntok, cb_g)
            yield
            DVE(lambda e: e.tensor_scalar(out=PV(6), in0=PV(1), scalar1=col(2), scalar2=None, op0=ALU.mult), R=[P_r[1], rwp_r], W=[P_r[6]])
            ACT(lambda e: e.activation(out=PV(9), in_=PV(6), func=AF.Square), R=[P_r[6]], W=[P_r[9]])
            yield
            ps, pres = psA()
            PE(lambda e: e.matmul(ps[:, :ntok], lhsT=blk64, rhs=PV(9), start=True, stop=True), R=[cstf_r, P_r[9]], W=[pres])
            rsqrt_into(PV(9), P_r[9], ps[:, :ntok], pres, 1.0, 1, ntok)
            DVE(lambda e: e.tensor_scalar(out=PV(7), in0=PV(4), scalar1=-1.0, scalar2=col(3), op0=ALU.add, op1=ALU.mult), R=[P_r[4], rwp_r], W=[P_r[7]])
            DVE(lambda e: e.scalar_tensor_tensor(out=PV(7), in0=PV(7), scalar=1.0, in1=PV(1), op0=ALU.add, op1=ALU.mult), R=[P_r[7], P_r[1]], W=[P_r[7]])
            DVE(lambda e: e.scalar_tensor_tensor(out=rg[:, 0, :ntok], in0=PV(0), scalar=col(6), in1=PV(7), op0=ALU.mult, op1=ALU.mult),
                R=[P_r[0], rwp_r, P_r[7]], W=[rg_r[0]])
            yield
            DVE(lambda e: e.tensor_tensor_scan(out=PV(8), data0=rst[:, :ntok], data1=PV(3), initial=0.0, op0=ALU.mult, op1=ALU.add),
                R=[cstb_r, P_r[3]], W=[P_r[8]])
            yield
            DVE(lambda e: e.tensor_tensor(out=PV(6), in0=PV(6), in1=PV(9), op=ALU.mult), R=[P_r[6], P_r[9]], W=[P_r[6]])
            DVE(lambda e: e.tensor_tensor(out=PV(9), in0=PV(8), in1=PV(3), op=ALU.subtract), R=[P_r[8], P_r[3]], W=[P_r[9]])
            yield
            ACT(lambda e: e.activation(out=PV(9), in_=PV(9), func=AF.Exp), R=[P_r[9]], W=[P_r[9]])
            DVE(lambda e: e.tensor_tensor(out=PV(5), in0=PV(6), in1=PV(4), op=ALU.mult), R=[P_r[6], P_r[4]], W=[P_r[5]])
            yield
            DVE(lambda e: e.scalar_tensor_tensor(out=Bq[:, 0, :ntok], in0=PV(6), scalar=-1.0, in1=PV(9), op0=ALU.mult, op1=ALU.mult),
                R=[P_r[6], P_r[9]], W=[Bq_r[0]])
            yield
            ACT(lambda e: e.activation(out=PV(9), in_=PV(8), func=AF.Exp), R=[P_r[8]], W=[P_r[9]])
            yield
            DVE(lambda e: e.tensor_tensor(out=Bq[:, 3, :ntok], in0=PV(0), in1=PV(9), op=ALU.mult), R=[P_r[0], P_r[9]], W=[Bq_r[3]])
            yield
            ACT(lambda e: e.activation(out=PV(9), in_=PV(8), func=AF.Exp, scale=-1.0), R=[P_r[8]], W=[P_r[9]])
            yield
            DVE(lambda e: e.tensor_tensor(out=Bq[:, 1, :ntok], in0=PV(5), in1=PV(9), op=ALU.mult), R=[P_r[5], P_r[9]], W=[Bq_r[1]])
            DVE(lambda e: e.tensor_tensor(out=Bq[:, 2, :ntok], in0=PV(7), in1=PV(9), op=ALU.mult), R=[P_r[7], P_r[9]], W=[Bq_r[2]])
            Lcv = PV(8).rearrange("p (s l) -> p s l", s=nseg)
            DVE(lambda e: e.tensor_copy(out=eL[:, 0, :nseg], in_=Lcv[:, :, L - 1]), R=[P_r[8]], W=[eL_r])
            yield
            ACT(lambda e: e.activation(out=eL[:, 1, :nseg], in_=eL[:, 0, :nseg], func=AF.Exp), R=[eL_r], W=[eL_r])
            DVE(lambda e: e.tensor_tensor(out=PV(9).rearrange("p (s l) -> p s l", s=nseg),
                                          in0=eL[:, 0, :nseg].unsqueeze(2).to_broadcast([128, nseg, L]), in1=Lcv, op=ALU.subtract),
                R=[eL_r, P_r[8]], W=[P_r[9]])
            yield
            ACT(lambda e: e.activation(out=PV(9), in_=PV(9), func=AF.Exp), R=[P_r[9]], W=[P_r[9]])
            ACT(lambda e: e.activation(out=Bq[:, 6, :ntok], in_=PV(2), func=AF.Copy), R=[P_r[2]], W=[Bq_r[6]])
            yield
            DVE(lambda e: e.tensor_tensor(out=Bq[:, 4, :ntok], in0=PV(5), in1=PV(9), op=ALU.mult), R=[P_r[5], P_r[9]], W=[Bq_r[4]])
            DVE(lambda e: e.tensor_tensor(out=Bq[:, 5, :ntok], in0=PV(7), in1=PV(9), op=ALU.mult), R=[P_r[7], P_r[9]], W=[Bq_r[5]])
            if is_s:
                kb.dma(kb.qs, S0f, I["rs0"][hp], W=[S0f_r])
                kb.dma(kb.qg, S0b, I["rs0"][hp], W=[S0f_r])
            yield

        def dplr_hp(hp, Lh, extra):
            Bq, Bq_r, eL, eL_r = Lh["Bq"], Lh["Bq_r"], Lh["eL"], Lh["eL_r"]

            def emit_XT(ti, hi, xt, xt_r, mats):
                cs = slice(ti * 128, (ti + 1) * 128)
                hsl = slice(64 * hi, 64 * hi + 64)

                p, p_r = psD512()
                for blk, (lh, rh) in enumerate(((1, 0), (2, 0), (1, 3), (2, 3))):
                    PE(lambda e: e.matmul(p[:, blk * 128:(blk + 1) * 128], lhsT=Bq[hsl, lh, cs], rhs=Bq[hsl, rh, cs], start=True, stop=True),
                       R=[Bq_r[lh], Bq_r[rh]], W=[p_r])
                k_ = (ti % 4) * 2 + hi
                mb = CB["MsT_s"] if is_s else CB["MsT_p"]
                DVE(lambda e: e.tensor_tensor(out=xt, in0=p[:, 0:128], in1=MsT, op=ALU.mult), R=[p_r, cstb_r], W=[xt_r])
                DVE(lambda e: e.tensor_tensor(out=matsB[:, k_ * 3:k_ * 3 + 2, :], in0=p[:, 128:384].rearrange("p (a b) -> p a b", a=2),
                                              in1=cstb[:, mb:mb + 256].rearrange("p (a b) -> p a b", a=2), op=ALU.mult),
                    R=[p_r, cstb_r], W=[mats[0][1], mats[1][1]])
                DVE(lambda e: e.tensor_tensor(out=mats[2][0], in0=p[:, 384:512], in1=MiT, op=ALU.mult), R=[p_r, cstb_r], W=[mats[2][1]])

            def emit_tm(ti):
                cs = slice(ti * 128, (ti + 1) * 128)
                i0 = ((ti % 2) * 2) * 3
                pt, pt_r = psT_full()
                for k, src in enumerate((6, 4, 5)):
                    PE(lambda e: e.transpose(pt[:, k * 128:(k + 1) * 128], Bq[:, src, cs], identb[:]), R=[Bq_r[src], identb_r], W=[pt_r])
                ACT(lambda e: e.activation(out=tmB[:, i0:i0 + 3, :], in_=pt[:, 0:384].rearrange("p (k b) -> p k b", k=3), func=AF.Copy),
                    R=[pt_r], W=[tmB_r[i0], tmB_r[i0 + 1], tmB_r[i0 + 2]])
                tm = [(tmB[:, i0 + k, :], tmB_r[i0 + k]) for k in range(3)]
                return [[(t_[:, 64 * hi:64 * hi + 64], r_) for t_, r_ in tm] for hi in range(2)]

            def ops(ti, hi):
                cs = slice(ti * 128, (ti + 1) * 128)
                hsl = slice(64 * hi, 64 * hi + 64)
                return Bq[hsl, 0, cs], Bq[hsl, 3, cs], Bq_r[0]

            def S_of(hi):
                hsl = slice(64 * hi, 64 * hi + 64)
                if is_s:
                    return dict(f=S0f[hsl, :], b=S0b[hsl, :], r=S0f_r)
                return dict(f=Sr_f[hsl, hp, :], b=Sr_b[hsl, hp, :], r=Sr_r[hp])

            def eT_of(ti, hi):
                hsl = slice(64 * hi, 64 * hi + 64)
                return (eL[hsl, 1, :16] if is_s else eL[hsl, 1, ti:ti + 1]), eL_r

            def ycb_of(ti, hi):
                cs = slice(ti * 128, (ti + 1) * 128)
                pb = 64 * hi

                def ycb(yo, pres):
                    ACT(lambda e: e.activation(out=Lh["yF"][pb:pb + 64, cs], in_=yo, func=AF.Copy), R=[pres], W=[Lh["yF_r"]])
                return ycb
            if extra is None:
                return dplr_run(nt, [0, 64], 64, 64, is_s, emit_XT, emit_tm, ops, S_of, eT_of, ycb_of, NEU_BF16=True, make_only=True)
            dplr_run(nt, [0, 64], 64, 64, is_s, emit_XT, emit_tm, ops, S_of, eT_of, ycb_of, NEU_BF16=True, extra=extra)
            if is_s:
                kb.dma(kb.qs, O["o_rw_s"][hp], S0f, R=[S0f_r])
            elif tb == 3:
                kb.dma(kb.qs, O["o_rw_p"][hp], Sr_f[:, hp, :], R=[Sr_r[hp]])

        def tail_gen(hp, Lh):
            col = lambda i: rwp[:, hp * 7 + i:hp * 7 + i + 1]
            yF, yF_r, rg, rg_r, Bq, Bq_r = Lh["yF"], Lh["yF_r"], Lh["rg"], Lh["rg_r"], Lh["Bq"], Lh["Bq_r"]
            Y = yF[:, :ntok]
            ps, pres = psA()
            PE(lambda e: e.matmul(ps[:, :ntok], lhsT=blk64, rhs=Y, start=True, stop=True), R=[cstf_r, yF_r], W=[pres])
            DVE(lambda e: e.scalar_tensor_tensor(out=PV(9), in0=ps[:, :ntok], scalar=-1.0 / 64, in1=Y, op0=ALU.mult, op1=ALU.add),
                R=[pres, yF_r], W=[P_r[9]])
            yield
            ACT(lambda e: e.activation(out=PV(5), in_=PV(9), func=AF.Square), R=[P_r[9]], W=[P_r[5]])
            yield
            ps, pres = psA()
            PE(lambda e: e.matmul(ps[:, :ntok], lhsT=blk64, rhs=PV(5), start=True, stop=True), R=[cstf_r, P_r[5]], W=[pres])
            rsqrt_into(PV(5), P_r[5], ps[:, :ntok], pres, 1.0 / 64, 3, ntok)
            yield
            DVE(lambda e: e.tensor_tensor(out=PV(9), in0=PV(9), in1=PV(5), op=ALU.mult), R=[P_r[9], P_r[5]], W=[P_r[9]])
            DVE(lambda e: e.tensor_scalar(out=PV(9), in0=PV(9), scalar1=col(4), scalar2=col(5), op0=ALU.mult, op1=ALU.add),
                R=[P_r[9], rwp_r], W=[P_r[9]])
            ps, pres = psA()
            PE(lambda e: e.matmul(ps[:, :ntok], lhsT=blk64b[:], rhs=rg[:, 0, :ntok], start=True, stop=True), R=[blk64b_r, rg_r[0]], W=[pres])
            DVE(lambda e: e.tensor_tensor(out=PV(5), in0=ps[:, :ntok], in1=Bq[:, 6, :ntok], op=ALU.mult), R=[pres, Bq_r[6]], W=[P_r[5]])
            DVE(lambda e: e.tensor_tensor(out=PV(9), in0=PV(9), in1=PV(5), op=ALU.add), R=[P_r[9], P_r[5]], W=[P_r[9]])
            DVE(lambda e: e.tensor_tensor(out=ygT[:, hp, :ntok], in0=PV(9), in1=rg[:, 1, :ntok], op=ALU.mult), R=[P_r[9], rg_r[1]], W=[ygT_r])
            yield

        def seq_gens(*gs):
            for g_ in gs:
                yield from g_

        if is_s:
            for hp in range(8):
                run_gens([prep_gen(hp, Ls[0])])
                dplr_hp(hp, Ls[0], [])
                run_gens([tail_gen(hp, Ls[0])])
        else:
            run_gens([prep_gen(0, Ls[0])])
            ctx = dplr_hp(0, Ls[0], None)
            P0, P1 = ctx["pairs"]
            run_gens(ctx["neu"](P0))
            for hp in range(8):
                parts = []
                if hp > 0:
                    parts.append(tail_gen(hp - 1, Ls[(hp - 1) % 2]))
                if hp < 7:
                    parts.append(prep_gen(hp + 1, Ls[(hp + 1) % 2]))
                ex = [seq_gens(*parts)] if parts else []
                run_gens(ctx["chain"](P0) + ctx["neu"](P1) + ex)
                run_gens(ex)
                gens = ctx["chain"](P1)
                nctx = None
                if hp < 7:
                    nctx = dplr_hp(hp + 1, Ls[(hp + 1) % 2], None)
                    gens += nctx["neu"](P0)
                run_gens(gens)
                if tb == 3:
                    kb.dma(kb.qs, O["o_rw_p"][hp], Sr_f[:, hp, :], R=[Sr_r[hp]])
                ctx = nctx
            run_gens([tail_gen(7, Ls[1])])
        linear_fm(I["W_o"], D, 0, D, [(ygT[:, c, :], ygT_r) for c in range(8)], ntok, resid_add_cb(ntok))

    TBS = [(0, 512, False), (512, 512, False), (1024, 512, False), (1536, 512, False), (2048, 128, True)]
    if stop_after is not None and stop_after.startswith("s:"):
        TBS = [TBS[4]]
    try:
      checkpoint("setup")
      for tb, (tok0, ntok, is_s) in enumerate(TBS):
        if len(TBS) == 1:
            tb = 4
        kb.dma(kb.qs, xT[:, :, :ntok], I["xT"][:, tok0:tok0 + ntok].rearrange("(c p) t -> p c t", p=128), W=[xT_r])
        layer0(tb, tok0, ntok, is_s)
        checkpoint("l0") ; checkpoint("s:l0")
        ffn(0, ntok)
        checkpoint("f0") ; checkpoint("s:f0")
        layer1(tb, tok0, ntok, is_s)
        checkpoint("l1") ; checkpoint("s:l1")
        ffn(1, ntok)
        arena.reset()
        yo_, yo_r = arena.alloc([128, 8, 512], F32)
        rmsnorm(4, ntok)
        for c in range(8):
            DVE(lambda e: e.scalar_tensor_tensor(out=yo_[:, c, :ntok], in0=xT[:, c, :ntok], scalar=nw[:, 32 + c:33 + c], in1=rstd[:, :ntok],
                                                 op0=ALU.mult, op1=ALU.mult), R=[xT_r, nw_r, rstd_r], W=[yo_r])
        kb.dma(kb.qs, O["yT"][:, tok0:tok0 + ntok].rearrange("(c p) t -> p c t", p=128), yo_[:, :, :ntok], R=[yo_r])
        checkpoint("tb0") ; checkpoint("s:tb")
    except _Stop:
        dbgout("xT", xT[:].rearrange("p a b -> p (a b)"), xT_r, [128, 4096])
        dbgout("Ecs", Ecs[:].rearrange("p a b c -> p (a b c)"), Ecs_r, [128, 4096])
        dbgout("s5c", s5c[:].rearrange("p a b -> p (a b)"), s5c_r, [128, 128])
        dbgout("s5o", s5o[:].rearrange("p a b c -> p (a b c)"), s5o_r, [128, 544])
    kb.dma(kb.qs, O["o_s5"], s5o[:].rearrange("p a b c -> p (a b c)"), R=[s5o_r])
    kb.dma(kb.qs, O["o_conv"], oconv[:].rearrange("p a b c -> p (a b c)"), R=[oconv_r])
    kb.dma(kb.qs, O["o_shift"], oshift[:].rearrange("p a b -> p (a b)"), R=[oshift_r])
    kb.finish()
    DEBUG["arena_peak"] = arena.peak * 4
    DEBUG["nins"] = kb.nins
    return nc


def _consts():
    i = np.arange(128)
    s, t = i[:, None], i[None, :]
    same = (s // 8) == (t // 8)
    ident = np.eye(128, dtype=np.float32)
    jidx = np.tile(np.arange(1, 129, dtype=np.float32), (128, 1))
    blk64 = ((s // 64) == (t // 64)).astype(np.float32)
    ones = np.ones((128, 128), np.float32)
    cstf = np.concatenate([ident, jidx, blk64, ones], 1)
    MsT_p, MiT_p = (s < t), (s <= t)
    MsT_s, MiT_s = MsT_p & same, MiT_p & same
    NEG_p = np.where(MiT_p, 0.0, -30000.0)
    NEG_s = np.where(MiT_s, 0.0, -30000.0)
    g2 = (i // 64)[:, None, None]
    j4 = np.arange(4)[None, :, None]
    g8 = (np.arange(128) // 16)[None, None, :]
    mask4 = (g8 == 2 * j4 + g2).astype(np.float32).reshape(128, 512)
    rst_p = np.ones((128, 512), np.float32)
    rst_p[:, ::128] = 0
    rst_s = np.ones((128, 128), np.float32)
    rst_s[:, ::8] = 0
    ind16 = ((i[:, None] // 8) == np.arange(16)[None, :]).astype(np.float32)
    cstb = np.concatenate([MsT_p, MiT_p, MsT_s, MiT_s, NEG_p, NEG_s, ones, mask4, rst_p, rst_s, ind16], 1).astype(np.float32)
    sel8 = np.zeros((8, 8, 128), np.float32)
    for j in range(8):
        sel8[j, j, :] = 1.0
    return cstf, cstb, sel8.reshape(8, 1024)


_NC_CACHE = {}


def _prep(inp):
    f = lambda a: np.ascontiguousarray(np.asarray(a, dtype=np.float32))
    inp = {k: np.asarray(v) for k, v in inp.items()}
    cstf, cstb, sel8 = _consts()
    pc = lambda w: f(w.reshape(-1, 128).T)
    nw = np.concatenate([pc(inp["norm_mix"][0]), pc(inp["norm_ffn"][0]), pc(inp["norm_mix"][1]),
                         pc(inp["norm_ffn"][1]), pc(inp["norm_final"])], 1)
    sp = lambda a: a.reshape(16, 2, 64).transpose(1, 2, 0).reshape(128, 16)
    ls = np.repeat(inp["s5_log_step"][0].reshape(16, 2, 1), 64, axis=2).transpose(1, 2, 0).reshape(128, 16)
    Bl = lambda a: a.reshape(16, 2, 64, 16).transpose(1, 2, 0, 3).reshape(128, 256)
    Cl = lambda a: a.reshape(16, 2, 16, 64).transpose(1, 3, 0, 2).reshape(128, 256)
    s5p = f(np.concatenate([sp(inp["s5_lambda_re"][0]), sp(inp["s5_lambda_im"][0]), ls, Bl(inp["s5_B_re"][0]),
                            Bl(inp["s5_B_im"][0]), Cl(inp["s5_C_re"][0]), Cl(inp["s5_C_im"][0]),
                            inp["s5_D"][0].reshape(4, 128).T], 1))
    gp = np.zeros((128, 51), np.float32)
    gp[:, 0:48] = inp["gdn_conv_w"][0].reshape(4, 12, 128).transpose(2, 1, 0).reshape(128, 48)
    gp[:, 48] = inp["gdn_norm_w"][0]
    gp[4:8, 49] = inp["gdn_A_log"][0]
    gp[4:8, 50] = inp["gdn_dt_bias"][0]
    rwp = np.zeros((128, 104), np.float32)
    for i, k in enumerate(["rw_w0", "rw_a0", "rw_k_k", "rw_k_a", "rw_ln_w", "rw_ln_b"]):
        rwp[:, i:56:7] = inp[k][0].reshape(8, 128).T
    rwp[:, 6:56:7] = inp["rw_r_k"][0].reshape(8, 128).T
    rwp[:, 56:104] = inp["rw_maa"][0].reshape(6, 8, 128).transpose(2, 0, 1).reshape(128, 48)
    shared = dict(cstf=cstf, cstb=cstb, sel8=sel8, nw=f(nw), s5p=s5p, gp=gp, rwp=rwp,
                  W_in=f(inp["ab_w_in"][0]), W_out=f(inp["ab_w_out"][0]), W_glu=f(inp["s5_w_glu"][0]),
                  Wg0=f(inp["ffn_w_gate"][0]), Wu0=f(inp["ffn_w_up"][0]), Wd0=f(inp["ffn_w_down"][0]),
                  Wg1=f(inp["ffn_w_gate"][1]), Wu1=f(inp["ffn_w_up"][1]), Wd1=f(inp["ffn_w_down"][1]),
                  W_r=f(inp["rw_w_r"][0]), W_k=f(inp["rw_w_k"][0]), W_v=f(inp["rw_w_v"][0]), W_o=f(inp["rw_w_o"][0]),
                  w1=f(inp["rw_w1"][0]), w2=f(inp["rw_w2"][0]), a1=f(inp["rw_a1"][0]), a2=f(inp["rw_a2"][0]),
                  g1=f(inp["rw_g1"][0]), g2=f(inp["rw_g2"][0]))
    in_maps = []
    for c in range(8):
        sl = slice(16 * c, 16 * c + 16)
        x = np.concatenate([inp["x_prompt"][c], inp["x_sample"][sl].reshape(128, D)], 0)
        m = dict(shared)
        m["xT"] = f(x.T)
        st = lambda a: a[0, sl].reshape(16, 16, 2, 64).transpose(2, 3, 1, 0).reshape(128, 256)
        m["s5h0"] = f(np.concatenate([st(inp["state_s5_re"]), st(inp["state_s5_im"])], 1))
        m["gconv0"] = f(inp["state_gdn_conv"][0, sl].reshape(16, 3, 12, 128).transpose(3, 2, 0, 1).reshape(128, 576))
        m["gs0"] = f(inp["state_gdn"][0, sl].transpose(1, 2, 0, 3).reshape(4, 128, 2048))
        m["rshift0"] = f(inp["state_rwkv_shift"][0, sl].reshape(16, 8, 128).transpose(2, 1, 0).reshape(128, 128))
        m["rs0"] = f(inp["state_rwkv"][0, sl].reshape(16, 8, 2, 64, 64).transpose(1, 2, 4, 0, 3).reshape(8, 128, 1024))
        in_maps.append(m)
    return in_maps


def kernel(**inp):
    in_maps = _prep(inp)
    if "nc" not in _NC_CACHE:
        _NC_CACHE["nc"] = build()
    res = run_bass_kernel_spmd(_NC_CACHE["nc"], in_maps, core_ids=list(range(8)))
    return _post(res.results)


def _post(R):
    y_p = np.zeros((8, 2048, D), np.float32)
    y_s = np.zeros((128, 8, D), np.float32)
    p_s5 = np.zeros((2, 1, 8, 32, 64), np.float32)
    s_s5 = np.zeros((2, 1, 128, 32, 64), np.float32)
    p_gdn = np.zeros((1, 8, 4, 128, 128), np.float32)
    s_gdn = np.zeros((1, 128, 4, 128, 128), np.float32)
    p_conv = np.zeros((1, 8, 3, 1536), np.float32)
    s_conv = np.zeros((1, 128, 3, 1536), np.float32)
    p_rw = np.zeros((1, 8, 16, 64, 64), np.float32)
    s_rw = np.zeros((1, 128, 16, 64, 64), np.float32)
    p_sh = np.zeros((1, 8, D), np.float32)
    s_sh = np.zeros((1, 128, D), np.float32)
    for c in range(8):
        r = R[c]
        sl = slice(16 * c, 16 * c + 16)
        y = r["yT"].T
        y_p[c] = y[:2048]
        y_s[sl] = y[2048:].reshape(16, 8, D)
        o5 = r["o_s5"].reshape(2, 64, 2, 16, 17)
        o5 = o5.transpose(2, 4, 3, 0, 1).reshape(2, 17, 32, 64)
        p_s5[:, 0, c] = o5[:, 0]
        s_s5[:, 0, sl] = o5[:, 1:]
        p_gdn[0, c] = r["o_gdn_p"]
        s_gdn[0, sl] = r["o_gdn_s"].reshape(4, 128, 16, 128).transpose(2, 0, 1, 3)
        oc = r["o_conv"].reshape(128, 12, 17, 3).transpose(2, 3, 1, 0).reshape(17, 3, 1536)
        p_conv[0, c] = oc[0]
        s_conv[0, sl] = oc[1:]
        p_rw[0, c] = r["o_rw_p"].reshape(8, 2, 64, 64).transpose(0, 1, 3, 2).reshape(16, 64, 64)
        s_rw[0, sl] = r["o_rw_s"].reshape(8, 2, 64, 16, 64).transpose(3, 0, 1, 4, 2).reshape(16, 16, 64, 64)
        osx = r["o_shift"].reshape(128, 8, 17).transpose(2, 1, 0).reshape(17, D)
        p_sh[0, c] = osx[0]
        s_sh[0, sl] = osx[1:]
    return (y_p, y_s, p_s5[0], p_s5[1], p_gdn, p_conv, p_rw, p_sh,
            s_s5[0], s_s5[1], s_gdn, s_conv, s_rw, s_sh)
```
